# Optimizing a Trainium2 kernel written in Bass

```python
import math
import jax, jax.numpy as jnp
from jax import lax
import numpy as np

D_MODEL = 1024
BATCH = 32
SEQ = 256
DEPTH = 2
DEC_BATCH = 4
DEC_SEQ = 4096
PAST_LEN = 512

GRID_W = 64
CHUNK = 64
SUB = 16
EPS = 1e-6
D_FF = 2816
N_DIR = 2
N_BRANCH = 3
M_HEADS = 4
M_DK = 64
M_DV = 128
G_HEADS = 4
G_DK = 64
G_DV = 128
G_RANK = 16
G_TEMP = 16.0
H_HEADS = 4
H_DE = 64
H_DV = 128
MIX_W = 512

IN_SIZES = (
    M_HEADS * M_DK, M_HEADS * M_DK, M_HEADS * M_DV, M_HEADS * M_DV, N_DIR * 2 * M_HEADS,
    G_HEADS * G_DK, G_HEADS * G_DK, G_HEADS * G_DV, G_HEADS * G_DV, N_DIR * G_RANK,
    H_HEADS * H_DE, N_DIR * H_HEADS * H_DE, H_HEADS * H_DV, H_HEADS * H_DV,
    N_BRANCH * D_MODEL,
)
IN_COLS = sum(IN_SIZES)

kernel_name = "bidir_mlstm_gla_hgrn2_prefix_diffusion_step"


def rmsnorm(x, w):
    x32 = x.astype(jnp.float32)
    y = x32 * lax.rsqrt(jnp.mean(x32 * x32, axis=-1, keepdims=True) + EPS)
    return y.astype(x.dtype) * w


def head_rmsnorm(x, w, n_heads):
    b, t, wd = x.shape
    xh = x.reshape(b, t, n_heads, wd // n_heads).astype(jnp.float32)
    xh = xh * lax.rsqrt(jnp.mean(xh * xh, axis=-1, keepdims=True) + EPS)
    return xh.reshape(b, t, wd) * w.astype(jnp.float32)


def to_heads(x, n_heads):
    b, t, _ = x.shape
    return x.reshape(b, t, n_heads, -1).transpose(0, 2, 1, 3)


def from_heads(x):
    b, h, t, d = x.shape
    return x.transpose(0, 2, 1, 3).reshape(b, t, h * d)


def dir_stack(fwd, bwd):
    return jnp.concatenate([fwd, jnp.flip(bwd, axis=2)], axis=1)


def dir_merge(o, n_heads):
    return o[:, :n_heads] + jnp.flip(o[:, n_heads:], axis=2)


def to_chunks(x):
    b, h, t = x.shape[:3]
    x = x.reshape(b, h, t // CHUNK, CHUNK, *x.shape[3:])
    return jnp.moveaxis(x, 2, 0)


def from_chunks(y):
    y = jnp.moveaxis(y, 0, 2)
    b, h, n, l = y.shape[:4]
    return y.reshape(b, h, n * l, *y.shape[4:])


def gated_linear_scan(q, k, v, log_a, s0):
    nb = CHUNK // SUB
    tri = jnp.tril(jnp.ones((SUB, SUB), bool))
    blk_lower = jnp.tril(jnp.ones((nb, nb), bool), -1)

    def step(s, inp):
        qc, kc, vc, la = inp
        b, h, l, dk = qc.shape
        dv = vc.shape[-1]
        g = jnp.cumsum(la, axis=2)
        o_inter = jnp.einsum('bhld,bhde->bhle', qc * jnp.exp(g), s)
        qs = qc.reshape(b, h, nb, SUB, dk)
        ks = kc.reshape(b, h, nb, SUB, dk)
        gs = g.reshape(b, h, nb, SUB, dk)
        vs = vc.reshape(b, h, nb, SUB, dv)
        g_end = gs[:, :, :, -1]
        q_off = qs[:, :, :, None] * jnp.exp(jnp.minimum(gs[:, :, :, None] - g_end[:, :, None, :, None], 0.0))
        k_off = ks * jnp.exp(g_end[:, :, :, None] - gs)
        a_off = jnp.einsum('bhijtd,bhjsd->bhijts', q_off, k_off) * blk_lower[:, :, None, None]
        o_off = jnp.einsum('bhijts,bhjse->bhite', a_off, vs)
        decay = jnp.exp(jnp.minimum(gs[:, :, :, :, None] - gs[:, :, :, None, :], 0.0))
        a_diag = jnp.einsum('bhntd,bhntsd,bhnsd->bhnts', qs, decay, ks) * tri
        o_diag = jnp.einsum('bhnts,bhnse->bhnte', a_diag, vs)
        o = o_inter + (o_off + o_diag).reshape(b, h, l, dv)
        g_last = g[:, :, -1]
        s_new = jnp.exp(g_last)[..., None] * s + jnp.einsum(
            'bhld,bhle->bhde', kc * jnp.exp(g_last[:, :, None] - g), vc)
        return s_new, o

    s_fin, o = lax.scan(step, s0, (to_chunks(q), to_chunks(k), to_chunks(v), to_chunks(log_a)))
    return from_chunks(o), s_fin


def mlstm_scan(q, k, v, log_i, log_f, c0, n0, m0):
    tri = jnp.tril(jnp.ones((CHUNK, CHUNK), bool))

    def step(carry, inp):
        c, n, m = carry
        qc, kc, vc, lic, lfc = inp
        b = jnp.cumsum(lfc, axis=-1)
        d = jnp.where(tri, b[..., :, None] - b[..., None, :] + lic[..., None, :], -jnp.inf)
        inter = b + m[..., None]
        m_t = jnp.maximum(inter, jnp.max(d, axis=-1))
        w = jnp.exp(d - m_t[..., None])
        sc = jnp.einsum('bhtd,bhsd->bhts', qc, kc) * w
        e_inter = jnp.exp(inter - m_t)
        num = jnp.einsum('bhts,bhse->bhte', sc, vc) + e_inter[..., None] * jnp.einsum('bhtd,bhde->bhte', qc, c)
        den = jnp.sum(sc, axis=-1) + e_inter * jnp.einsum('bhtd,bhd->bht', qc, n)
        h = num / jnp.maximum(jnp.abs(den), jnp.exp(-m_t))[..., None]
        b_last = b[..., -1]
        lw = b_last[..., None] - b + lic
        m_new = jnp.maximum(b_last + m, jnp.max(lw, axis=-1))
        ws = jnp.exp(lw - m_new[..., None])
        dec = jnp.exp(b_last + m - m_new)
        c_new = dec[..., None, None] * c + jnp.einsum('bhs,bhsd,bhse->bhde', ws, kc, vc)
        n_new = dec[..., None] * n + jnp.einsum('bhs,bhsd->bhd', ws, kc)
        return (c_new, n_new, m_new), h

    (c1, n1, m1), h = lax.scan(step, (c0, n0, m0), (to_chunks(q), to_chunks(k), to_chunks(v),
                                                  to_chunks(log_i), to_chunks(log_f)))
    return from_chunks(h), (c1, n1, m1)


def swiglu(h, w_in, w_out):
    gate, up = jnp.split(jnp.einsum('btd,df->btf', h, w_in), 2, axis=-1)
    return jnp.einsum('btf,fd->btd', jax.nn.silu(gate) * up, w_out)


def grid_position(n_tokens):
    rows = n_tokens // GRID_W
    quarter = D_MODEL // 4
    freqs = jnp.exp(-math.log(10000.0) * jnp.arange(quarter, dtype=jnp.float32) / quarter)
    r = jnp.arange(rows, dtype=jnp.float32)[:, None] * freqs
    cl = jnp.arange(GRID_W, dtype=jnp.float32)[:, None] * freqs
    r_emb = jnp.concatenate([jnp.sin(r), jnp.cos(r)], axis=-1)
    c_emb = jnp.concatenate([jnp.sin(cl), jnp.cos(cl)], axis=-1)
    emb = jnp.concatenate([jnp.broadcast_to(r_emb[:, None], (rows, GRID_W, D_MODEL // 2)),
                           jnp.broadcast_to(c_emb[None], (rows, GRID_W, D_MODEL // 2))], axis=-1)
    return emb.reshape(rows * GRID_W, D_MODEL)


def mixer(h, st, w_in, gate_bias_m, gla_w_up, gla_b, lb, head_norm, w_branch, w_out):
    f32 = jnp.float32
    b, t, _ = h.shape
    proj = jnp.einsum('btd,dc->btc', h, w_in).astype(f32)
    split_at = [int(s) for s in np.cumsum(IN_SIZES)[:-1]]
    (mq, mk, mv, mo, mif, gq, gk, gv, gr, glr, hq, hf, hv, hg, mg) = jnp.split(proj, split_at, axis=-1)
    c0, n0, m0, sg0, sh0 = [s.astype(f32) for s in st]

    q = to_heads(mq, M_HEADS) * (M_DK ** -0.5)
    k = to_heads(mk, M_HEADS)
    v = to_heads(mv, M_HEADS)
    gates = mif.reshape(b, t, N_DIR, 2, M_HEADS) + gate_bias_m.astype(f32)
    log_i = gates[:, :, :, 0].transpose(0, 2, 3, 1)
    log_f = jax.nn.log_sigmoid(gates[:, :, :, 1]).transpose(0, 2, 3, 1)
    h_m, (c1, n1, m1) = mlstm_scan(dir_stack(q, q), dir_stack(k, k), dir_stack(v, v),
                                   dir_stack(log_i[:, 0], log_i[:, 1]), dir_stack(log_f[:, 0], log_f[:, 1]),
                                   c0, n0, m0)
    y_m = jax.nn.sigmoid(mo) * head_rmsnorm(from_heads(dir_merge(h_m, M_HEADS)), head_norm[0], M_HEADS)

    q = to_heads(gq, G_HEADS) * (G_DK ** -0.5)
    k = to_heads(gk, G_HEADS)
    v = to_heads(gv, G_HEADS)
    la = jax.nn.log_sigmoid(jnp.einsum('btzr,zrc->btzc', glr.reshape(b, t, N_DIR, G_RANK), gla_w_up.astype(f32))
                            + gla_b.astype(f32)) / G_TEMP
    o_g, sg1 = gated_linear_scan(dir_stack(q, q), dir_stack(k, k), dir_stack(v, v),
                                 dir_stack(to_heads(la[:, :, 0], G_HEADS), to_heads(la[:, :, 1], G_HEADS)), sg0)
    y_g = jax.nn.silu(gr) * head_rmsnorm(from_heads(dir_merge(o_g, G_HEADS)), head_norm[1], G_HEADS)

    z = hf.reshape(b, t, N_DIR, H_HEADS * H_DE)
    lb = lb.astype(f32)
    log_fh = jnp.log(lb + (1.0 - lb) * jax.nn.sigmoid(z))
    key_h = (1.0 - lb) * jax.nn.sigmoid(-z)
    q = to_heads(hq, H_HEADS)
    i_v = to_heads(jax.nn.silu(hv), H_HEADS)
    o_h, sh1 = gated_linear_scan(dir_stack(q, q),
                                 dir_stack(to_heads(key_h[:, :, 0], H_HEADS), to_heads(key_h[:, :, 1], H_HEADS)),
                                 dir_stack(i_v, i_v),
                                 dir_stack(to_heads(log_fh[:, :, 0], H_HEADS), to_heads(log_fh[:, :, 1], H_HEADS)),
                                 sh0)
    y_h = jax.nn.silu(hg) * head_rmsnorm(from_heads(dir_merge(o_h, H_HEADS)), head_norm[2], H_HEADS)

    ys = jnp.stack([y_m, y_g, y_h], axis=2).astype(h.dtype)
    branch = jnp.einsum('btnc,ncd->btnd', ys, w_branch)
    merge_gate = jax.nn.sigmoid(mg).reshape(b, t, N_BRANCH, D_MODEL).astype(h.dtype)
    out = jnp.einsum('btd,de->bte', jnp.sum(merge_gate * branch, axis=2), w_out)
    return out.astype(h.dtype), (c1, n1, m1, sg1, sh1)


def trunk_layer(x, mod, pos, st, norm_pre, norm_post, w_ffn_in, w_ffn_out, w_in, gate_bias_m,
                gla_w_up, gla_b, lb, head_norm, w_branch, w_out):
    md = [mod[:, :, i] for i in range(9)]

    def modulate(x_, i, j):
        return rmsnorm(x_, norm_pre[j]) * (1.0 + md[i + 1]) + md[i]

    h = modulate(x, 0, 0)
    x = x + 0.5 * md[2] * rmsnorm(swiglu(h, w_ffn_in[0], w_ffn_out[0]), norm_post[0])
    h = modulate(x, 3, 1)
    if pos is not None:
        h = h + pos
    y, st = mixer(h, st, w_in, gate_bias_m, gla_w_up, gla_b, lb, head_norm, w_branch, w_out)
    x = x + md[5] * rmsnorm(y, norm_post[1])
    h = modulate(x, 6, 2)
    x = x + 0.5 * md[8] * rmsnorm(swiglu(h, w_ffn_in[1], w_ffn_out[1]), norm_post[2])
    return x, st


def setup_inputs(seed: int = 0) -> dict:
    key = jax.random.key(seed)
    ks = jax.random.split(key, 23)

    def nrm(k, shape, s):
        return jax.random.normal(k, shape, jnp.float32) * s

    gb = nrm(ks[16], (DEPTH, N_DIR, 2, M_HEADS), 1.0)
    mlstm_gate_bias = gb * jnp.array([0.1, 0.5], jnp.float32)[:, None] + jnp.array([0.0, 3.0], jnp.float32)[:, None]
    return {
        "x_prompt": nrm(ks[0], (BATCH, SEQ, D_MODEL), 1.0),
        "x_sample": nrm(ks[1], (DEC_BATCH, DEC_SEQ, D_MODEL), 1.0),
        "c": nrm(ks[2], (DEC_BATCH, D_MODEL), 1.0),
        "state_mlstm_C": nrm(ks[3], (DEC_BATCH, DEPTH, N_DIR, M_HEADS, M_DK, M_DV), 0.1),
        "state_mlstm_n": nrm(ks[4], (DEC_BATCH, DEPTH, N_DIR, M_HEADS, M_DK), 0.1),
        "state_mlstm_m": nrm(ks[5], (DEC_BATCH, DEPTH, N_DIR, M_HEADS), 0.5),
        "state_gla_S": nrm(ks[6], (DEC_BATCH, DEPTH, N_DIR, G_HEADS, G_DK, G_DV), 0.3),
        "state_hgrn_S": nrm(ks[7], (DEC_BATCH, DEPTH, N_DIR, H_HEADS, H_DE, H_DV), 0.3),
        "c_ctx": nrm(ks[8], (D_MODEL,), 1.0),
        "w_ada": nrm(ks[9], (DEPTH, D_MODEL, 9 * D_MODEL), 0.5 * D_MODEL ** -0.5),
        "b_ada": nrm(ks[10], (DEPTH, 9 * D_MODEL), 0.02),
        "norm_pre": 1.0 + nrm(ks[11], (DEPTH, 3, D_MODEL), 0.1),
        "norm_post": 1.0 + nrm(ks[12], (DEPTH, 3, D_MODEL), 0.1),
        "w_ffn_in": nrm(ks[13], (DEPTH, 2, D_MODEL, 2 * D_FF), D_MODEL ** -0.5),
        "w_ffn_out": nrm(ks[14], (DEPTH, 2, D_FF, D_MODEL), D_FF ** -0.5),
        "w_in": nrm(ks[15], (DEPTH, D_MODEL, IN_COLS), D_MODEL ** -0.5),
        "mlstm_gate_bias": mlstm_gate_bias,
        "gla_w_up": nrm(ks[17], (DEPTH, N_DIR, G_RANK, G_HEADS * G_DK), G_RANK ** -0.5),
        "gla_b": nrm(ks[18], (DEPTH, N_DIR, G_HEADS * G_DK), 0.1),
        "hgrn_gamma": nrm(ks[19], (DEPTH, H_HEADS * H_DE), 1.0),
        "head_norm": 1.0 + nrm(ks[20], (DEPTH, N_BRANCH, MIX_W), 0.1),
        "w_branch": nrm(ks[21], (DEPTH, N_BRANCH, MIX_W, D_MODEL), MIX_W ** -0.5),
        "w_out": nrm(ks[22], (DEPTH, D_MODEL, D_MODEL), D_MODEL ** -0.5),
    }


def reference(x_prompt, x_sample, c, state_mlstm_C, state_mlstm_n, state_mlstm_m, state_gla_S, state_hgrn_S,
              c_ctx, w_ada, b_ada, norm_pre, norm_post, w_ffn_in, w_ffn_out, w_in, mlstm_gate_bias,
              gla_w_up, gla_b, hgrn_gamma, head_norm, w_branch, w_out):
    f32 = jnp.float32
    bp = x_prompt.shape[0]
    bs, ts = x_sample.shape[:2]
    p_gamma = jax.nn.softmax(hgrn_gamma.astype(f32), axis=0)
    lb_all = jnp.cumsum(p_gamma, axis=0) - p_gamma[0]
    pos = grid_position(ts).astype(x_sample.dtype)
    silu_ctx = jax.nn.silu(c_ctx)[None]
    silu_c = jax.nn.silu(c)

    xp, xs = x_prompt, x_sample
    new_c, new_n, new_m, new_sg, new_sh = [], [], [], [], []
    for l in range(DEPTH):
        lp = (norm_pre[l], norm_post[l], w_ffn_in[l], w_ffn_out[l], w_in[l], mlstm_gate_bias[l],
              gla_w_up[l], gla_b[l], lb_all[l], head_norm[l], w_branch[l], w_out[l])
        mod_ctx = (silu_ctx @ w_ada[l] + b_ada[l]).reshape(1, 1, 9, D_MODEL)
        st0 = (jnp.zeros((bp, N_DIR * M_HEADS, M_DK, M_DV), f32),
               jnp.zeros((bp, N_DIR * M_HEADS, M_DK), f32),
               jnp.zeros((bp, N_DIR * M_HEADS), f32),
               jnp.zeros((bp, N_DIR * G_HEADS, G_DK, G_DV), f32),
               jnp.zeros((bp, N_DIR * H_HEADS, H_DE, H_DV), f32))
        xp, (c1, n1, m1, sg1, sh1) = trunk_layer(xp, mod_ctx, None, st0, *lp)
        new_c.append(c1.reshape(bp, N_DIR, M_HEADS, M_DK, M_DV))
        new_n.append(n1.reshape(bp, N_DIR, M_HEADS, M_DK))
        new_m.append(m1.reshape(bp, N_DIR, M_HEADS))
        new_sg.append(sg1.reshape(bp, N_DIR, G_HEADS, G_DK, G_DV))
        new_sh.append(sh1.reshape(bp, N_DIR, H_HEADS, H_DE, H_DV))
        mod_lat = (silu_c @ w_ada[l] + b_ada[l]).reshape(bs, 1, 9, D_MODEL)
        st_lat = (state_mlstm_C[:, l].reshape(bs, N_DIR * M_HEADS, M_DK, M_DV),
                  state_mlstm_n[:, l].reshape(bs, N_DIR * M_HEADS, M_DK),
                  state_mlstm_m[:, l].reshape(bs, N_DIR * M_HEADS),
                  state_gla_S[:, l].reshape(bs, N_DIR * G_HEADS, G_DK, G_DV),
                  state_hgrn_S[:, l].reshape(bs, N_DIR * H_HEADS, H_DE, H_DV))
        xs, _ = trunk_layer(xs, mod_lat, pos, st_lat, *lp)

    new_mlstm_C = jnp.stack(new_c, axis=1)
    new_mlstm_n = jnp.stack(new_n, axis=1)
    new_mlstm_m = jnp.stack(new_m, axis=1)
    new_gla_S = jnp.stack(new_sg, axis=1)
    new_hgrn_S = jnp.stack(new_sh, axis=1)
    return (xp, xs, new_mlstm_C, new_mlstm_n, new_mlstm_m, new_gla_S, new_hgrn_S)
```

```python
import math
from contextlib import ExitStack
import numpy as np
import ml_dtypes
import concourse.bass as bass
import concourse.mybir as mybir
from concourse.bass_utils import run_bass_kernel_spmd

F32 = mybir.dt.float32
BF16 = mybir.dt.bfloat16
AF = mybir.ActivationFunctionType
ALU = mybir.AluOpType
AX = mybir.AxisListType

D = 1024
KC = 8
FF = 2816
FC = 22
T = 4096
TT = 512
NT = T // TT
NSEG = 16
SEGLEN = 256
EPS = 1e-6
INC = 7984
BIG = 3.0e38
C_MQ, C_MK, C_MV, C_MO, C_MIF = 0, 256, 512, 1024, 1536
C_GQ, C_GK, C_GV, C_GR, C_GLR = 1552, 1808, 2064, 2576, 3088
C_HQ, C_HF, C_HV, C_HG, C_MG = 3120, 3376, 3888, 4400, 4912
MIXL = {"m": 64, "g": 64, "h": 32}
MIXI = {"m": 0, "g": 1, "h": 2}
DBG_SPARTS = {"mpass", "m", "g", "h"}
DBG_SKIP_A = False
DBG_NSUB = None
DBG_LEVEL = 9


class Prog:
    ENGS = ("pe", "act", "dve", "pool", "sp")

    def __init__(self, nc, stack):
        self.nc = nc
        self.stack = stack
        self.sem = {e: stack.enter_context(nc.semaphore("prog_" + e)) for e in self.ENGS}
        self.cnt = {e: 0 for e in self.ENGS}
        self.prog = {e: [] for e in self.ENGS}
        self.seen = {e: {} for e in self.ENGS}
        self.last_w = {}
        self.readers = {}
        self.dma_sems = {}
        self.dma_cnt = {}
        self.n_inst = 0
        self.cap = None

    def capture(self, fn):
        assert self.cap is None
        self.cap = []
        try:
            fn()
        finally:
            lst, self.cap = self.cap, None
        return lst

    def commit(self, *lists):
        lists = [l_ for l_ in lists if l_]
        if not lists:
            return
        main = max(lists, key=len)
        others = [l_ for l_ in lists if l_ is not main]
        pos = [0] * len(others)
        for i, item in enumerate(main):
            self._replay(item)
            for oi, ol in enumerate(others):
                want = (len(ol) * (i + 1)) // len(main)
                while pos[oi] < want:
                    self._replay(ol[pos[oi]])
                    pos[oi] += 1

    def _replay(self, item):
        kind = item[0]
        if kind == "op":
            self.op(*item[1:])
        elif kind == "pe":
            self.pe_group(*item[1:])
        else:
            self.dma(*item[1:])

    def _need(self, eng, reads, writes):
        toks = []
        for k in reads:
            t = self.last_w.get(k)
            if t is not None:
                toks.append(t)
        for k in writes:
            t = self.last_w.get(k)
            if t is not None:
                toks.append(t)
            toks.extend(self.readers.get(k, ()))
        best = {}
        for (s, v) in toks:
            if v > best.get(s.name, (None, 0))[1]:
                best[s.name] = (s, v)
        out = []
        seen = self.seen[eng]
        own = self.sem[eng].name if eng == "pe" else None
        for name, (s, v) in best.items():
            if name == own:
                continue
            if seen.get(name, 0) >= v:
                continue
            seen[name] = v
            out.append((s, v))
        return out

    def _commit(self, tok, reads, writes):
        for k in writes:
            self.last_w[k] = tok
            self.readers[k] = []
        for k in reads:
            if k in writes:
                continue
            self.readers.setdefault(k, []).append(tok)

    def op(self, eng, fn, reads=(), writes=()):
        if self.cap is not None:
            self.cap.append(("op", eng, fn, tuple(reads), tuple(writes)))
            return None
        waits = self._need(eng, reads, writes)
        self.cnt[eng] += 1
        sem = self.sem[eng]
        tok = (sem, self.cnt[eng])
        self._commit(tok, reads, writes)

        def emit(e, fn=fn, waits=waits, sem=sem):
            for (s, v) in waits:
                e.wait_ge(s, v)
            fn(e).then_inc(sem, 1)
        self.prog[eng].append(emit)
        self.n_inst += 1
        return tok

    def pe_group(self, fns, reads=(), writes=()):
        if self.cap is not None:
            self.cap.append(("pe", fns, tuple(reads), tuple(writes)))
            return None
        waits = self._need("pe", reads, writes)
        self.cnt["pe"] += 1
        sem = self.sem["pe"]
        tok = (sem, self.cnt["pe"])
        self._commit(tok, reads, writes)

        def emit(e, fns=fns, waits=waits, sem=sem):
            for (s, v) in waits:
                e.wait_ge(s, v)
            for f in fns[:-1]:
                f(e)
            fns[-1](e).then_inc(sem, 1)
        self.prog["pe"].append(emit)
        self.n_inst += len(fns)
        return tok

    def dma(self, eng, slot, fn, reads=(), writes=()):
        if self.cap is not None:
            self.cap.append(("dma", eng, slot, fn, tuple(reads), tuple(writes)))
            return None
        slot = f"{slot}_{eng}"
        if slot not in self.dma_sems:
            self.dma_sems[slot] = self.stack.enter_context(self.nc.semaphore("dma_" + str(slot)))
            self.dma_cnt[slot] = 0
        waits = self._need(eng, reads, writes)
        self.dma_cnt[slot] += 16
        sem = self.dma_sems[slot]
        tok = (sem, self.dma_cnt[slot])
        self._commit(tok, reads, writes)

        def emit(e, fn=fn, waits=waits, sem=sem):
            for (s, v) in waits:
                e.wait_ge(s, v)
            fn(e).then_inc(sem, 16)
        self.prog[eng].append(emit)
        self.n_inst += 1
        return tok

    def barrier(self):
        toks = [(self.sem[e], self.cnt[e]) for e in self.ENGS if self.cnt[e] > 0]
        toks += [(self.dma_sems[s], self.dma_cnt[s]) for s in self.dma_sems]
        for eng in self.ENGS:
            seen = self.seen[eng]
            waits = []
            for (s, v) in toks:
                if seen.get(s.name, 0) >= v:
                    continue
                seen[s.name] = v
                waits.append((s, v))

            def emit(e, waits=waits):
                for (s, v) in waits:
                    e.wait_ge(s, v)
            self.prog[eng].append(emit)
        self.last_w = {}
        self.readers = {}

    def finish(self, block):
        self.barrier()
        P = self.prog

        @block.tensor
        def _(e):
            for f in P["pe"]:
                f(e)

        @block.scalar
        def _(e):
            for f in P["act"]:
                f(e)

        @block.vector
        def _(e):
            for f in P["dve"]:
                f(e)

        @block.gpsimd
        def _(e):
            for f in P["pool"]:
                f(e)

        @block.sync
        def _(e):
            for f in P["sp"]:
                f(e)


def MM(out, lhsT, rhs, start=True, stop=True):
    return lambda e: e.matmul(out, lhsT, rhs, start=start, stop=stop)


def ACT(out, in_, func, **kw):
    return lambda e: e.activation(out, in_, func, **kw)


def TT_(out, a, b, op):
    return lambda e: e.tensor_tensor(out, a, b, op)


def TS(out, a, s1, s2, op0, op1=None):
    if op1 is None:
        return lambda e: e.tensor_scalar(out, a, s1, None, op0)
    return lambda e: e.tensor_scalar(out, a, s1, s2, op0, op1)


def STT(out, in0, scalar, in1, op0, op1):
    return lambda e: e.scalar_tensor_tensor(out, in0, scalar, in1, op0, op1)


def CP(out, in_):
    return lambda e: e.tensor_copy(out, in_)


def DMA(out, in_):
    return lambda e: e.dma_start(out=out, in_=in_)


def scan_consts(L, z, scale):
    s = np.arange(L)[:, None]
    t = np.arange(L)[None, :]
    if z == 0:
        mid = L // 2 - 1
        Wm = ((s > mid) & (s <= t)).astype(np.float32) - ((s > t) & (s <= mid)).astype(np.float32)
        wa = (np.arange(L) <= mid).astype(np.float32)
        wb = (np.arange(L) > mid).astype(np.float32)
        mask = (s <= t).astype(np.float32)
    else:
        mid = L // 2
        Wm = ((s >= t) & (s < mid)).astype(np.float32) - ((s >= mid) & (s < t)).astype(np.float32)
        wa = (np.arange(L) >= mid).astype(np.float32)
        wb = (np.arange(L) < mid).astype(np.float32)
        mask = (s >= t).astype(np.float32)
    Wcat = np.concatenate([Wm, wa[:, None], wb[:, None]], axis=1) * scale
    pos_in_chunk = np.arange(128) % L
    if z == 0:
        kz = (pos_in_chunk < L // 2).astype(np.float32)
    else:
        kz = (pos_in_chunk >= L // 2).astype(np.float32)
    kz = np.ascontiguousarray(np.broadcast_to(kz[None, :], (64, 128))).astype(np.float32)
    return (Wcat.astype(np.float32), (Wm * scale).astype(np.float32),
            (-BIG * mask).astype(np.float32), mask.astype(np.float32), kz)


def grid_position_T(n_tokens, grid_w=64):
    rows = n_tokens // grid_w
    quarter = D // 4
    freqs = np.exp(-math.log(10000.0) * np.arange(quarter, dtype=np.float32) / quarter).astype(np.float32)
    r = np.arange(rows, dtype=np.float32)[:, None] * freqs
    cl = np.arange(grid_w, dtype=np.float32)[:, None] * freqs
    r_emb = np.concatenate([np.sin(r), np.cos(r)], axis=-1)
    c_emb = np.concatenate([np.sin(cl), np.cos(cl)], axis=-1)
    emb = np.concatenate([np.broadcast_to(r_emb[:, None], (rows, grid_w, D // 2)),
                          np.broadcast_to(c_emb[None], (rows, grid_w, D // 2))], axis=-1)
    return np.ascontiguousarray(emb.reshape(rows * grid_w, D).T.astype(np.float32))


def build(dbg=False, stop_after=None, layers=(0, 1)):
    nc = bass.Bass("TRN2", target_bir_lowering=False)

    def din(name, shape, dt=F32):
        return nc.dram_tensor(name, list(shape), dt, kind="ExternalInput").ap()

    def dout(name, shape, dt=F32):
        return nc.dram_tensor(name, list(shape), dt, kind="ExternalOutput").ap()

    def dscr(name, shape, dt=F32):
        return nc.dram_tensor(name, list(shape), dt, kind=("ExternalOutput" if dbg else "Internal")).ap()

    xT_in = din("xT", [D, T])
    posT = din("posT", [D, T])
    cT_in = din("cT", [128, KC])
    w_ada = din("w_ada", [2, D, 9 * D])
    bada_in = din("bada", [128, 2, 72])
    npre_in = din("npre", [128, 2, 3, KC])
    npost_in = din("npost", [128, 2, 3, KC])
    hnorm_in = din("hnorm", [128, 2, 3, 4])
    w_ffn_in = din("w_ffn_in", [2, 2, D, 2 * FF])
    w_ffn_out = din("w_ffn_out", [2, 2, FF, D])
    w_in = din("w_in", [2, D, INC])
    w_branch = din("w_branch", [2, 3, 512, D])
    w_out = din("w_out", [2, D, D])
    mgb_in = din("mgb", [1, 32])
    gwup_in = din("gla_w_up", [2, 2, 16, 256])
    gb_in = din("gla_b", [1, 2 * 2 * 256])
    hgam_in = din("hgam", [1, 512])
    hgamT_in = din("hgamT", [64, 2, 4])
    mC0 = din("mC0", [2, 2, 4, 64, 128])
    mN0 = din("mN0", [2, 2, 4, 64, 128])
    mM0 = din("mM0", [1, 16])
    mM0T_in = din("mM0T", [4, 4])
    mgbT_in = din("mgbT", [4, 8])
    gS0 = din("gS0", [2, 2, 4, 64, 128])
    hS0 = din("hS0", [2, 2, 4, 64, 128])
    flags_in = din("flags", [1, 2 * NSEG * 2])
    ones_in = din("ones", [128, 128])
    consts_in = {}
    for mix, L in MIXL.items():
        for z in (0, 1):
            consts_in[(mix, z)] = (din(f"wcat_{mix}{z}", [L, L + 2]), din(f"wm_{mix}{z}", [L, L]),
                                   din(f"mlo_{mix}{z}", [L, L]), din(f"mhi_{mix}{z}", [L, L]),
                                   din(f"kz_{mix}{z}", [64, 128]))
    yT_out = dout("yT", [D, T])
    o_mC = dout("o_mC", [2, NSEG, 2, 4, 64, 128])
    o_mN = dout("o_mN", [2, NSEG, 2, 4, 64])
    o_mM = dout("o_mM", [2, NSEG, 2, 4])
    o_gS = dout("o_gS", [2, NSEG, 2, 4, 64, 128])
    o_hS = dout("o_hS", [2, NSEG, 2, 4, 64, 128])
    X1 = dscr("X1", [D, T])
    X3 = dscr("X3", [D, T])
    QKT = dscr("QKT", [1280, T])
    HFT = dscr("HFT", [512, T])
    GLRT = dscr("GLRT", [32, T])
    MGT = dscr("MGT", [16, T])
    KTOK = dscr("KTOK", [T, 512])
    HFTOK = dscr("HFTOK", [T, 512])
    MGTOK = dscr("MGTOK", [T, 16])
    VT = dscr("V", [T, 1536], BF16)
    GATES = dscr("GATES", [4608, T], BF16)
    OTs = dscr("OT", [2, 1536, T])
    WB = {"fin": dscr("WB_fin", [2, 2, D, 2 * FF], BF16), "fout": dscr("WB_fout", [2, 2, FF, D], BF16),
          "in": dscr("WB_in", [2, D, INC], BF16), "br": dscr("WB_br", [2, 3, 512, D], BF16),
          "out": dscr("WB_out", [2, D, D], BF16)}
    WF = {"fin": w_ffn_in, "fout": w_ffn_out, "in": w_in, "br": w_branch, "out": w_out}
    NOCONV = {("fin", 0, 0), ("fout", 0, 0), ("in", 0, None)}

    def wsrc(name, l, idx=None):
        conv = (name, l, idx) not in NOCONV
        t = WB[name] if conv else WF[name]
        return (t[l] if idx is None else t[l, idx]), conv

    dests = {"qkt": QKT, "hft": HFT, "glrt": GLRT, "mgt": MGT, "ktok": KTOK, "hftok": HFTOK,
             "mgtok": MGTOK, "v": VT, "gates": GATES}

    with ExitStack() as st:
        P = Prog(nc, st)

        def sb(name, shape, dt=F32):
            return st.enter_context(nc.sbuf_tensor("s_" + name, list(shape), dt))

        ones_f = sb("ones_f", [128, 128])
        ones_b = sb("ones_b", [128, 128], BF16)
        cTt = sb("cTt", [128, KC])
        scT = sb("scT", [128, KC])
        bada = sb("bada", [128, 2, 72])
        npre = sb("npre", [128, 2, 3, KC])
        npost = sb("npost", [128, 2, 3, KC])
        hnorm = sb("hnorm", [128, 2, 3, 4])
        modT = sb("modT", [128, 2, 72])
        A1 = sb("A1", [128, 2, 3, KC])
        G1 = sb("G1", [128, 2, 3, KC])
        fl = sb("fl", [128, 2, NSEG, 2])
        arena = sb("arena", [128, 17408])
        arena2 = sb("arena2", [128, 17408])

        def carve(off, parts, free, dt, ar=None):
            ar = arena if ar is None else ar
            n = 1
            for d_ in free:
                n *= d_
            if dt == BF16:
                v = ar[:].bitcast(BF16)[0:parts, off // 2:off // 2 + n]
            else:
                v = ar[0:parts, off // 4:off // 4 + n]
            if len(free) == 2:
                v = v.rearrange("p (a b) -> p a b", a=free[0])
            elif len(free) == 3:
                v = v.rearrange("p (a b c) -> p a b c", a=free[0], b=free[1])
            return v

        actT = carve(0, 128, [FC, TT], BF16)
        yF = carve(22528, 128, [KC, TT], F32)
        ring = [carve(12288 * i, 128, [3072], F32, arena2) for i in range(3)]
        xT = carve(36864, 128, [KC, TT], F32, arena2)
        hT = carve(53248, 128, [KC, TT], BF16, arena2)
        arena3 = sb("arena3", [128, 7680])
        posb = [carve(2048 * i, 128, [TT], F32, arena3) for i in range(2)]
        sq = [carve(4096 + 1024 * i, 128, [TT], BF16, arena3) for i in range(2)]
        tmp = [carve(6144 + 2048 * i, 128, [TT], F32, arena3) for i in range(2)]
        rstd = carve(10240, 128, [TT], F32, arena3)
        sgt = [carve(12288 + 2048 * i, 128, [TT], F32, arena3) for i in range(2)]
        stf = [carve(61440 + 2048 * i, 128, [TT], F32, arena2) for i in range(4)]
        stb = [carve(16384 + 1024 * i, 128, [TT], BF16, arena3) for i in range(4)]
        yTbs = [carve(38912, 128, [12, TT], BF16), carve(51200, 128, [12, TT], BF16)]
        sq2 = [carve(63488 + 1024 * i, 128, [TT], BF16) for i in range(2)]
        tmp2 = [carve(65536 + 2048 * i, 128, [TT], F32) for i in range(2)]
        gtb2 = [sb(f"gtb2_{i}", [128, TT], BF16) for i in range(2)]
        ofb = [carve(20480 + 2048 * i, 128, [TT], F32, arena3) for i in range(4)]
        gtb = [carve(28672 + 1024 * i, 128, [TT], BF16, arena3) for i in range(2)]
        pb = [st.enter_context(nc.psum_tensor(f"pb{i}", [128, 512], F32)) for i in range(8)]

        block = st.enter_context(nc.Block())

        ring_i = [0]

        def ring_next():
            i = ring_i[0] % 3
            ring_i[0] += 1
            return i

        def load_w(slot, pieces, kc, width, cast=True, bf=False):
            if cast:
                view = ring[slot].bitcast(BF16)[:, 0:kc * width].rearrange("p (k c) -> p k c", k=kc)
            else:
                view = ring[slot][:, 0:kc * width].rearrange("p (k c) -> p k c", k=kc)
            q = "sp" if (bf or not cast) else "pool"
            for (ap, c0, w, off) in pieces:
                src = ap[:, c0:c0 + w].rearrange("(k p) c -> p k c", p=128)
                P.dma(q, f"ring{slot}", DMA(view[:, :, off:off + w], src), writes=[f"ring{slot}"])
            return view

        stq = ["sp"]

        P.dma("sp", "cst", DMA(ones_f[:], ones_in), writes=["ones_f"])
        P.dma("sp", "cst", DMA(cTt[:], cT_in), writes=["cTt"])
        P.dma("sp", "cst", DMA(bada[:], bada_in), writes=["bada"])
        P.dma("sp", "cst", DMA(npre[:], npre_in), writes=["npre"])
        P.dma("sp", "cst", DMA(npost[:], npost_in), writes=["npost"])
        P.dma("sp", "cst", DMA(hnorm[:], hnorm_in), writes=["hnorm"])
        P.dma("sp", "cst", DMA(fl[:].rearrange("p a b c -> p (a b c)"), flags_in.partition_broadcast(128)),
              writes=["fl"])
        def stage0():
            P.op("dve", CP(ones_b[:], ones_f[:]), reads=["ones_f"], writes=["ones_b"])
            P.op("act", ACT(scT[:], cTt[:], AF.Silu), reads=["cTt"], writes=["scT"])
            pm = pb[7]
            for l in layers:
                for blk in range(36):
                    slot = ring_next()
                    wv = load_w(slot, [(w_ada[l], blk * 256, 256, 0)], KC, 256, cast=False)
                    fns = []
                    for c in range(2):
                        j = blk * 2 + c
                        for k in range(KC):
                            fns.append(MM(pm[:, j:j + 1], wv[:, k, c * 128:(c + 1) * 128], scT[:, k:k + 1],
                                          start=(k == 0), stop=(k == KC - 1)))
                    P.pe_group(fns, reads=[f"ring{slot}", "scT"], writes=["pm"])
                P.op("dve", TT_(modT[:, l, :], pm[:, 0:72], bada[:, l, :], ALU.add), reads=["pm", "bada"],
                     writes=["modT"])
                for j in range(3):
                    P.op("dve", STT(A1[:, l, j, :], modT[:, l, (3 * j + 1) * 8:(3 * j + 2) * 8], 1.0, npre[:, l, j, :],
                                    ALU.add, ALU.mult), reads=["modT", "npre"], writes=["A1"])
                    P.op("dve", STT(G1[:, l, j, :], modT[:, l, (3 * j + 2) * 8:(3 * j + 3) * 8],
                                    (1.0 if j == 1 else 0.5), npost[:, l, j, :], ALU.mult, ALU.mult),
                         reads=["modT", "npost"], writes=["G1"])

        def B1(l, j, k):
            return modT[:, l, 3 * j * 8 + k:3 * j * 8 + k + 1]

        def norm_stats(src_tile, srckey, n_chunks, inv_n):
            pss = pb[4]
            for k in range(n_chunks):
                b = k % 2
                if b == 0:
                    P.op("act", ACT(sq[b][:], src_tile[:, k, :], AF.Square), reads=[f"{srckey}{k}"], writes=[f"sq{b}"])
                else:
                    P.op("dve", TT_(sq[b][:], src_tile[:, k, :], src_tile[:, k, :], ALU.mult), reads=[f"{srckey}{k}"],
                         writes=[f"sq{b}"])
                P.pe_group([MM(pss[:, :], ones_b[:], sq[b][:], start=(k == 0), stop=(k == n_chunks - 1))],
                           reads=[f"sq{b}", "ones_b"], writes=["pss"])
            P.op("dve", TS(rstd[:], pss[:, :], inv_n, EPS, ALU.mult, ALU.add), reads=["pss"], writes=["rstd"])
            P.op("act", ACT(rstd[:], rstd[:], AF.Sqrt), reads=["rstd"], writes=["rstd"])
            P.op("dve", lambda e: e.reciprocal(rstd[:], rstd[:]), reads=["rstd"], writes=["rstd"])

        def norm_to_h(l, j, tok0, pos):
            norm_stats(xT, "xT", KC, 1.0 / D)
            for k in range(KC):
                b = k % 2
                meng = "dve" if b == 0 else "pool"
                if pos:
                    P.dma("sp", f"posb{b}", DMA(posb[b][:], posT[k * 128:(k + 1) * 128, tok0:tok0 + TT]),
                          writes=[f"posb{b}"])
                P.op(meng, TT_(tmp[b][:], xT[:, k, :], rstd[:], ALU.mult), reads=[f"xT{k}", "rstd"],
                     writes=[f"tmp{b}"])
                if pos:
                    P.op("dve", STT(tmp[b][:], tmp[b][:], A1[:, l, j, k:k + 1], posb[b][:], ALU.mult, ALU.add),
                         reads=[f"tmp{b}", "A1", f"posb{b}"], writes=[f"tmp{b}"])
                    P.op("act", ACT(hT[:, k, :], tmp[b][:], AF.Identity, bias=B1(l, j, k)),
                         reads=[f"tmp{b}", "modT"], writes=[f"hT{k}"])
                elif b == 0:
                    P.op("act", ACT(hT[:, k, :], tmp[b][:], AF.Identity, bias=B1(l, j, k),
                                    scale=A1[:, l, j, k:k + 1]),
                         reads=[f"tmp{b}", "modT", "A1"], writes=[f"hT{k}"])
                else:
                    P.op("dve", TS(hT[:, k, :], tmp[b][:], A1[:, l, j, k:k + 1], B1(l, j, k), ALU.mult, ALU.add),
                         reads=[f"tmp{b}", "modT", "A1"], writes=[f"hT{k}"])

        hkeys = [f"hT{k}" for k in range(KC)]

        def resid_update(l, j):
            norm_stats(yF, "yF", KC, 1.0 / D)
            for k in range(KC):
                b = k % 2
                meng = "dve" if b == 0 else "pool"
                P.op(meng, TT_(tmp[b][:], yF[:, k, :], rstd[:], ALU.mult), reads=[f"yF{k}", "rstd"],
                     writes=[f"tmp{b}"])
                P.op("dve", STT(xT[:, k, :], tmp[b][:], G1[:, l, j, k:k + 1], xT[:, k, :], ALU.mult, ALU.add),
                     reads=[f"tmp{b}", "G1", f"xT{k}"], writes=[f"xT{k}"])

        ffn_cnt = [0]

        def ffn(l, which, j, tok0):
            norm_to_h(l, j, tok0, pos=False)
            Wi, bfi = wsrc("fin", l, which)
            Wo, bfo = wsrc("fout", l, which)
            for blk in range(11):
                slot = ring_next()
                wv = load_w(slot, [(Wi, blk * 256, 256, 0), (Wi, FF + blk * 256, 256, 256)], KC, 512, bf=bfi)
                for c in range(2):
                    n = ffn_cnt[0] % 2
                    ffn_cnt[0] += 1
                    pg, pu = pb[0 + n], pb[2 + n]
                    fns = []
                    for k in range(KC):
                        fns.append(MM(pg[:, :], wv[:, k, c * 128:(c + 1) * 128], hT[:, k, :], start=(k == 0),
                                      stop=(k == KC - 1)))
                        fns.append(MM(pu[:, :], wv[:, k, 256 + c * 128:256 + (c + 1) * 128], hT[:, k, :],
                                      start=(k == 0), stop=(k == KC - 1)))
                    P.pe_group(fns, reads=[f"ring{slot}"] + hkeys, writes=[f"pb{n}", f"pb{2 + n}"])
                    f = blk * 2 + c
                    P.op("act", ACT(sgt[n][:], pg[:, :], AF.Silu), reads=[f"pb{n}"], writes=[f"sgt{n}"])
                    P.op("dve", TT_(actT[:, f, :], pu[:, :], sgt[n][:], ALU.mult), reads=[f"pb{2 + n}", f"sgt{n}"],
                         writes=[f"actT{f}"])
            akeys = [f"actT{f}" for f in range(FC)]
            for ob in range(4):
                slot = ring_next()
                wv = load_w(slot, [(Wo, ob * 256, 256, 0)], FC, 256, bf=bfo)
                for c in range(2):
                    dc = ob * 2 + c
                    n = dc % 2
                    py = pb[5 + n]
                    fns = [MM(py[:, :], wv[:, f, c * 128:(c + 1) * 128], actT[:, f, :], start=(f == 0),
                              stop=(f == FC - 1)) for f in range(FC)]
                    P.pe_group(fns, reads=[f"ring{slot}"] + akeys, writes=[f"pb{5 + n}"])
                    P.op("act", ACT(yF[:, dc, :], py[:, :], AF.Copy), reads=[f"pb{5 + n}"], writes=[f"yF{dc}"])
            resid_update(l, j)

        st_cnt = {"f": 0, "b": 0, "pj": 0}

        def proj_stage(l, tok0):
            Wl, bfw = wsrc("in", l)
            blocks = [
                (0, 512, [("fm", 0, 256, "qkt", 0, None, 0.125), ("fm", 256, 256, "qkt", 256, None, 1.0),
                          ("tm", 256, 256, "ktok", 0, None)]),
                (C_MV, 512, [("tm", 0, 512, "v", 0, None)]),
                (C_MO, 528, [("fm", 0, 512, "gates", 0, AF.Sigmoid, 1.0), ("fm", 512, 16, "mgt", 0, None, 1.0),
                             ("tm", 512, 16, "mgtok", 0, None)]),
                (C_GQ, 512, [("fm", 0, 256, "qkt", 512, None, 0.125), ("fm", 256, 256, "qkt", 768, None, 1.0),
                             ("tm", 256, 256, "ktok", 256, None)]),
                (C_GV, 512, [("tm", 0, 512, "v", 512, None)]),
                (C_GR, 544, [("fm", 0, 512, "gates", 512, AF.Silu, 1.0), ("fm", 512, 32, "glrt", 0, None, 1.0)]),
                (C_HQ, 256, [("fm", 0, 256, "qkt", 1024, None, 1.0)]),
                (C_HF, 512, [("fm", 0, 512, "hft", 0, None, 1.0), ("tm", 0, 512, "hftok", 0, None)]),
                (C_HV, 512, [("tm", 0, 512, "v", 1024, AF.Silu)]),
                (C_HG, 512, [("fm", 0, 512, "gates", 1024, AF.Silu, 1.0)]),
            ] + [(C_MG + i * 512, 512, [("fm", 0, 512, "gates", 1536 + i * 512, AF.Sigmoid, 1.0)]) for i in range(6)]
            for (c0, width, jobs) in blocks:
                slot = ring_next()
                wv = load_w(slot, [(Wl, c0, width, 0)], KC, width, bf=bfw)
                for job in jobs:
                    if job[0] == "fm":
                        _, off, ncols, dname, row0, func, scale = job
                        dst = dests[dname]
                        isb = (dst.dtype == BF16)
                        for cc in range(0, ncols, 128):
                            m = min(128, ncols - cc)
                            n = st_cnt["pj"] % 2
                            st_cnt["pj"] += 1
                            pp = pb[5 + n]
                            fns = [MM(pp[0:m, :], wv[:, k, off + cc:off + cc + m], hT[:, k, :], start=(k == 0),
                                      stop=(k == KC - 1)) for k in range(KC)]
                            P.pe_group(fns, reads=[f"ring{slot}"] + hkeys, writes=[f"pb{5 + n}"])
                            if isb:
                                si = st_cnt["b"] % 4
                                st_cnt["b"] += 1
                                stg, skey = stb[si], f"stb{si}"
                            else:
                                si = st_cnt["f"] % 4
                                st_cnt["f"] += 1
                                stg, skey = stf[si], f"stf{si}"
                            if func is not None:
                                P.op("act", ACT(stg[0:m, :], pp[0:m, :], func), reads=[f"pb{5 + n}"], writes=[skey])
                            elif scale != 1.0:
                                P.op("act", ACT(stg[0:m, :], pp[0:m, :], AF.Copy, scale=scale), reads=[f"pb{5 + n}"],
                                     writes=[skey])
                            else:
                                P.op("dve", CP(stg[0:m, :], pp[0:m, :]), reads=[f"pb{5 + n}"], writes=[skey])
                            P.dma(stq[0], skey, DMA(dst[row0 + cc:row0 + cc + m, tok0:tok0 + TT], stg[0:m, :]),
                                  reads=[skey], writes=[f"d_{dname}"])
                    else:
                        _, off, ncols, dname, col0, func = job
                        dst = dests[dname]
                        isb = (dst.dtype == BF16)
                        for jj in range(TT // 128):
                            n = st_cnt["pj"] % 2
                            st_cnt["pj"] += 1
                            pp = pb[5 + n]
                            fns = [MM(pp[:, 0:ncols], hT[:, k, jj * 128:(jj + 1) * 128], wv[:, k, off:off + ncols],
                                      start=(k == 0), stop=(k == KC - 1)) for k in range(KC)]
                            P.pe_group(fns, reads=[f"ring{slot}"] + hkeys, writes=[f"pb{5 + n}"])
                            if isb:
                                si = st_cnt["b"] % 4
                                st_cnt["b"] += 1
                                stg, skey = stb[si], f"stb{si}"
                            else:
                                si = st_cnt["f"] % 4
                                st_cnt["f"] += 1
                                stg, skey = stf[si], f"stf{si}"
                            if func is not None:
                                P.op("act", ACT(stg[:, 0:ncols], pp[:, 0:ncols], func), reads=[f"pb{5 + n}"],
                                     writes=[skey])
                            else:
                                P.op("dve", CP(stg[:, 0:ncols], pp[:, 0:ncols]), reads=[f"pb{5 + n}"], writes=[skey])
                            t0 = tok0 + jj * 128
                            P.dma(stq[0], skey, DMA(dst[t0:t0 + 128, col0:col0 + ncols], stg[:, 0:ncols]),
                                  reads=[skey], writes=[f"d_{dname}"])

        def load_x(src, tok0):
            P.dma("sp", "xT", DMA(xT[:], src[:, tok0:tok0 + TT].rearrange("(k p) t -> p k t", p=128)),
                  reads=[], writes=[f"xT{k}" for k in range(KC)])

        def store_x(dst, tok0, key):
            P.dma(stq[0], "xTst", DMA(dst[:, tok0:tok0 + TT].rearrange("(k p) t -> p k t", p=128), xT[:]),
                  reads=[f"xT{k}" for k in range(KC)], writes=[key])

        def stage_A(l):
            src = xT_in if l == 0 else X3
            for tt in range(NT):
                tok0 = tt * TT
                load_x(src, tok0)
                ffn(l, 0, 0, tok0)
                store_x(X1, tok0, "d_x1")
                norm_to_h(l, 1, tok0, pos=True)
                proj_stage(l, tok0)

        def make_set(si):
            regions = [[arena if si == 0 else arena2, 0, 17408 * 4], [arena3, si * 15360, (si + 1) * 15360]]

            def cv(nbytes, parts, free, dt):
                for r in regions:
                    if r[1] + nbytes <= r[2]:
                        v = carve(r[1], parts, free, dt, r[0])
                        r[1] += nbytes
                        return v
                raise AssertionError("scan set does not fit")
            d_ = {}
            d_["qTt"] = [cv(2048, 64, [4, 128], F32) for _ in range(2)]
            d_["kin"] = [cv(2048, 64, [4, 128], F32) for _ in range(2)]
            d_["ktok"] = [cv(4096, 64, [4, 256], F32) for _ in range(2)]
            d_["vtk"] = [cv(4096, 64, [4, 512], BF16) for _ in range(2)]
            d_["V2"] = [cv(4096, 64, [8, 256], BF16) for _ in range(2)]
            d_["lat"] = cv(4096, 64, [4, 256], F32)
            d_["Ktk"] = [cv(2048, 64, [4, 256], BF16) for _ in range(2)]
            d_["ATm"] = [cv(2048, 64, [16, 64], BF16) for _ in range(2)]
            d_["Sint"] = cv(4096, 64, [4, 256], F32)
            d_["Stm"] = cv(4096, 64, [4, 256], F32)
            d_["Sini"] = cv(4096, 64, [4, 256], F32)
            d_["Sst"] = cv(4096, 64, [4, 256], F32)
            d_["Sbf"] = cv(2048, 64, [4, 256], BF16)
            d_["oev"] = [cv(2048, 128, [4, 128], F32) for _ in range(2)]
            d_["dn"] = cv(2048, 128, [4, 128], F32)
            d_["kTc"] = cv(2048, 64, [4, 128], F32)
            d_["GcQ"] = cv(2048, 64, [4, 128], F32)
            d_["GcK"] = cv(2048, 64, [4, 128], F32)
            d_["QTs"] = [cv(1024, 64, [4, 128], BF16) for _ in range(2)]
            d_["KTs"] = cv(1024, 64, [4, 128], BF16)
            d_["KTz"] = cv(1024, 64, [4, 128], BF16)
            d_["sgk"] = cv(4096, 64, [4, 256], F32)
            d_["mgk"] = [sb(f"mgk{si}_{i}", [64, 2, 16]) for i in range(2)]
            d_["spm"] = sb(f"spm{si}", [64, 2, 4])
            d_["eli"] = sb(f"eli{si}", [64, 2, 4])
            d_["glrTt"] = [sb(f"glrTt{si}_{i}", [16, 128]) for i in range(2)]
            d_["ABt"] = [sb(f"ABt{si}_{i}", [64, 4, 4, 2]) for i in range(2)]
            d_["emi4"] = sb(f"emi4{si}", [64, 4])
            d_["Eo4"] = sb(f"Eo4{si}", [64, NSEG, 4])
            d_["liT"] = sb(f"liT{si}", [4, SEGLEN])
            d_["xfT"] = sb(f"xfT{si}", [4, SEGLEN])
            d_["bT_"] = sb(f"bT_{si}", [4, SEGLEN])
            d_["mrow"] = sb(f"mrow{si}", [4, 8])
            d_["mcur"] = sb(f"mcur{si}", [4, 1])
            return d_

        SETS = [make_set(0), make_set(1)]
        wup = sb("wup", [16, 2, 256])
        gbb = sb("gbb", [64, 2, 2, 256])
        lbb = sb("lbb", [64, 2, 256])
        hgb = sb("hgb", [64, 2, 256])
        lbT = sb("lbT", [64, 4, 2])
        hgT = sb("hgT", [64, 2, 4])
        mgbb = sb("mgbb", [64, 32])
        onesr = sb("onesr", [4, SEGLEN])
        mgbT = sb("mgbT", [4, 8])
        mm0T = sb("mm0T", [4, 4])
        cst = {}
        for mix, L in MIXL.items():
            for z in (0, 1):
                cst[(mix, z)] = (sb(f"wcat{mix}{z}", [L, L + 2]), sb(f"wm{mix}{z}", [L, L]),
                                 sb(f"mlo{mix}{z}", [L, L]), sb(f"mhi{mix}{z}", [L, L]), sb(f"kz{mix}{z}", [64, 128]))
        ps_open = [False] * 8
        ps_ptr = [0]

        def acquire(n):
            while True:
                free = [(ps_ptr[0] + i) % 8 for i in range(8) if not ps_open[(ps_ptr[0] + i) % 8]]
                if len(free) >= n:
                    sel = free[:n]
                    for i in sel:
                        ps_open[i] = True
                    ps_ptr[0] = (sel[-1] + 1) % 8
                    return [(pb[i], f"pb{i}", i) for i in sel]
                yield

        def release(*idx):
            for i in idx:
                ps_open[i] = False

        def stage_S_consts():
            for key, tiles in cst.items():
                for i in range(5):
                    P.dma("sp", "cst", DMA(tiles[i][:], consts_in[key][i]), writes=[f"cst{key}{i}"])
            P.dma("sp", "cst", DMA(mgbb[:], mgb_in.partition_broadcast(64)), writes=["mgbb"])
            P.dma("sp", "cst", DMA(gbb[:].rearrange("p a b c -> p (a b c)"), gb_in.partition_broadcast(64)),
                  writes=["gbb"])
            P.dma("sp", "cst", DMA(hgb[:].rearrange("p a c -> p (a c)"), hgam_in.partition_broadcast(64)),
                  writes=["hgb"])
            P.dma("sp", "cst", DMA(hgT[:], hgamT_in), writes=["hgT"])
            P.op("pool", lambda e: e.memset(onesr[:], 1.0), writes=["onesr"])
            P.dma("sp", "cst", DMA(mgbT[:], mgbT_in), writes=["mgbT"])
            P.dma("sp", "cst", DMA(mm0T[:], mM0T_in), writes=["mm0T"])
            for z in (0, 1):
                P.dma("sp", "cst", DMA(wup[:, z, :], gwup_in[0, z]), writes=["wup"])

        def hgrn_lb(l):
            if l == 0:
                P.op("dve", lambda e: e.memset(lbb[:, 0, :], 0.0), writes=["lbb"])
                P.op("dve", lambda e: e.memset(lbb[:, 1, :], 1.0), writes=["lbb"])
                P.op("dve", lambda e: e.memset(lbT[:, :, 0:1], -1.0), writes=["lbT"])
                P.op("dve", lambda e: e.memset(lbT[:, :, 1:2], 1.0), writes=["lbT"])
            else:
                P.op("dve", TT_(lbb[:, 0, :], hgb[:, 1, :], hgb[:, 0, :], ALU.subtract), reads=["hgb"], writes=["lbb"])
                P.op("act", ACT(lbb[:, 0, :], lbb[:, 0, :], AF.Sigmoid), reads=["lbb"], writes=["lbb"])
                P.op("dve", TS(lbb[:, 1, :], lbb[:, 0, :], -1.0, 1.0, ALU.mult, ALU.add), reads=["lbb"], writes=["lbb"])
                P.op("dve", TT_(lbT[:, :, 1], hgT[:, 0, :], hgT[:, 1, :], ALU.subtract), reads=["hgT"], writes=["lbT"])
                P.op("act", ACT(lbT[:, :, 1], lbT[:, :, 1], AF.Sigmoid), reads=["lbT"], writes=["lbT"])
                P.op("dve", TS(lbT[:, :, 0], lbT[:, :, 1], -1.0, None, ALU.mult), reads=["lbT"], writes=["lbT"])
                for z in (0, 1):
                    P.dma("sp", "wup", DMA(wup[:, z, :], gwup_in[l, z]), writes=["wup"])

        def mlstm_m_pass(l, z, si):
            B = SETS[si]
            liT, xfT, bT_, mrow, mcur, Eo4 = B["liT"], B["xfT"], B["bT_"], B["mrow"], B["mcur"], B["Eo4"]
            K = lambda n: f"{n}_{si}"
            order = range(NSEG) if z == 0 else range(NSEG - 1, -1, -1)
            bi = mgbT[:, (l * 2 + z) * 2 + 0:(l * 2 + z) * 2 + 1]
            bf = mgbT[:, (l * 2 + z) * 2 + 1:(l * 2 + z) * 2 + 2]
            P.op("dve", lambda e: e.memset(mcur[:], 0.0), writes=[K("mcur")])
            for seg in order:
                t0 = seg * SEGLEN
                P.dma("sp", K("liT"), DMA(liT[:], MGT[z * 8:z * 8 + 4, t0:t0 + SEGLEN]), writes=[K("liT")])
                P.dma("sp", K("xfT"), DMA(xfT[:], MGT[z * 8 + 4:z * 8 + 8, t0:t0 + SEGLEN]), writes=[K("xfT")])
                yield
                P.op("act", ACT(xfT[:], xfT[:], AF.Identity, bias=bf), reads=[K("xfT"), "mgbT"], writes=[K("xfT")])
                yield
                P.op("act", ACT(xfT[:], xfT[:], AF.Exp, scale=-1.0), reads=[K("xfT")], writes=[K("xfT")])
                yield
                P.op("act", ACT(xfT[:], xfT[:], AF.Ln, bias=1.0), reads=[K("xfT")], writes=[K("xfT")])
                yield
                for c in range(4):
                    cs = slice(c * 64, (c + 1) * 64)
                    P.op("dve", lambda e, cs=cs: e.tensor_tensor_scan(bT_[:, cs], onesr[:, cs], xfT[:, cs], 0.0,
                                                                      ALU.mult, ALU.add),
                         reads=[K("xfT"), "onesr"], writes=[K("bT_")])
                yield
                if z == 0:
                    P.op("dve", STT(liT[:], liT[:], bi, bT_[:], ALU.add, ALU.add), reads=[K("liT"), K("bT_"), "mgbT"],
                         writes=[K("liT")])
                else:
                    for c in range(4):
                        cs = slice(c * 64, (c + 1) * 64)
                        P.op("dve", STT(liT[:, cs], liT[:, cs], bT_[:, c * 64 + 63:c * 64 + 64], bT_[:, cs],
                                        ALU.add, ALU.subtract), reads=[K("liT"), K("bT_")], writes=[K("liT")])
                    yield
                    P.op("dve", STT(liT[:], liT[:], bi, xfT[:], ALU.add, ALU.add), reads=[K("liT"), K("xfT"), "mgbT"],
                         writes=[K("liT")])
                yield
                P.op("dve", lambda e: e.tensor_reduce(mrow[:, 0:4], liT[:].rearrange("p (c t) -> p c t", c=4), AX.X,
                                                      ALU.max), reads=[K("liT")], writes=[K("mrow")])
                P.op("dve", TS(mrow[:, 4:8], bT_[:].rearrange("p (c t) -> p c t", c=4)[:, :, 63], -1.0, None, ALU.mult),
                     reads=[K("bT_")], writes=[K("mrow")])
                yield
                P.op("dve", TS(mcur[:], mcur[:], fl[0:4, z, seg, 0:1], None, ALU.mult), reads=[K("mcur"), "fl"],
                     writes=[K("mcur")])
                yield
                P.op("dve", STT(mcur[:], mm0T[:, l * 2 + z:l * 2 + z + 1], fl[0:4, z, seg, 1:2], mcur[:], ALU.mult,
                                ALU.add), reads=[K("mcur"), "fl", "mm0T"], writes=[K("mcur")])
                yield
                corder = range(4) if z == 0 else range(3, -1, -1)
                for c in corder:
                    P.op("dve", STT(mcur[:], mcur[:], mrow[:, c:c + 1], mrow[:, 4 + c:5 + c], ALU.max, ALU.add),
                         reads=[K("mcur"), K("mrow")], writes=[K("mcur")])
                    yield
                P.dma("sp", K("mst"), DMA(o_mM[l, seg, z, :].rearrange("(h o) -> h o", o=1), mcur[:]),
                      reads=[K("mcur")], writes=[f"d_mm{z}_{seg}"])
                yield
                P.dma("sp", K("Eo4"), DMA(Eo4[:, seg, :], o_mM[l, seg:seg + 1, z, :].partition_broadcast(64)),
                      reads=[f"d_mm{z}_{seg}"], writes=[K("Eo4")])
                yield
            P.op("act", ACT(Eo4[:], Eo4[:], AF.Exp, scale=-1.0), reads=[K("Eo4")], writes=[K("Eo4")])
            yield

        def state_dram(ap5, l, z):
            return ap5[l, z].rearrange("h d e -> d h e")

        def scan_chain(l, mix, z, si):
            B = SETS[si]
            K = lambda n: f"{n}_{si}"
            lat, sgk = B["lat"], B["sgk"]
            Gtk = lat
            Sint, Stm, Sini, Sst, Sbf, oev, dn = (B[n] for n in ("Sint", "Stm", "Sini", "Sst", "Sbf", "oev", "dn"))
            GcQ, GcK, KTs, KTz, kTc = (B[n] for n in ("GcQ", "GcK", "KTs", "KTz", "kTc"))
            spm, eli, emi4, Eo4 = (B[n] for n in ("spm", "eli", "emi4", "Eo4"))
            pre_done = [0]
            seq_done = [0]
            L = MIXL[mix]
            nch = 128 // L
            E = 256 if mix == "m" else 128
            wcat, wm, mlo, mhi, kzm = cst[(mix, z)]
            ck = [f"cst{(mix, z)}{i}" for i in range(5)]
            qrow = {"m": 0, "g": 512, "h": 1024}[mix]
            krow = {"m": 256, "g": 768}.get(mix)
            kcol = {"m": 0, "g": 256}.get(mix)
            vcol = {"m": 0, "g": 512, "h": 1024}[mix]
            SE = lambda t_: t_[:, :, 0:E]
            order = list(range(T // 128)) if z == 0 else list(range(T // 128 - 1, -1, -1))
            if DBG_NSUB is not None:
                order = order[:DBG_NSUB]
            def pre():
                for ji, j in enumerate(order):
                    while seq_done[0] < ji - 1:
                        yield
                    tok0 = j * 128
                    seg = j // 2
                    jb = ji % 2
                    qTt, kTt, hfTt, ktok, vtk = B["qTt"][jb], B["kin"][jb], B["kin"][jb], B["ktok"][jb], B["vtk"][jb]
                    mgk, glrTt = B["mgk"][jb], B["glrTt"][jb]
                    V2, Ktk, ATm, QTs, ABt = B["V2"][jb], B["Ktk"][jb], B["ATm"][jb], B["QTs"][jb], B["ABt"][jb]
                    KV2, KKtk, KATm, KQTs, KABt = K(f"V2{jb}"), K(f"Ktk{jb}"), K(f"ATm{jb}"), K(f"QTs{jb}"), K(f"ABt{jb}")
                    KQ, KK, KT, KV = K(f"qTt{jb}"), K(f"kin{jb}"), K(f"ktok{jb}"), K(f"vtk{jb}")
                    P.dma("sp", KQ, DMA(qTt[:], QKT[qrow:qrow + 256, tok0:tok0 + 128].rearrange("(h d) t -> d h t", d=64)),
                          writes=[KQ])
                    if mix != "h":
                        P.dma("sp", KK, DMA(kTt[:], QKT[krow:krow + 256, tok0:tok0 + 128].rearrange(
                            "(h d) t -> d h t", d=64)), writes=[KK])
                        P.dma("sp", KT, DMA(ktok[0:L, 0:nch, :],
                                            KTOK[tok0:tok0 + 128, kcol:kcol + 256].rearrange("(c l) f -> l c f", l=L)),
                              writes=[KT])
                    else:
                        P.dma("sp", KK, DMA(hfTt[:], HFT[z * 256:(z + 1) * 256, tok0:tok0 + 128].rearrange(
                            "(h d) t -> d h t", d=64)), writes=[KK])
                        P.dma("sp", KT, DMA(ktok[0:L, 0:nch, :],
                                            HFTOK[tok0:tok0 + 128, z * 256:(z + 1) * 256].rearrange(
                                                "(c l) f -> l c f", l=L)), writes=[KT])
                    P.dma("sp", KV, DMA(vtk[0:L, 0:nch, :],
                                        VT[tok0:tok0 + 128, vcol:vcol + 512].rearrange("(c l) f -> l c f", l=L)),
                          writes=[KV])
                    yield
                    latv = lat[0:L, 0:nch, :]
                    if mix == "g":
                        P.dma("sp", K(f"glrTt{jb}"), DMA(glrTt[:], GLRT[z * 16:(z + 1) * 16, tok0:tok0 + 128]),
                              writes=[K(f"glrTt{jb}")])
                        ((bank, bk, bi_),) = yield from acquire(1)
                        pla = bank[0:L, :].rearrange("p (c f) -> p c f", c=nch)
                        P.pe_group([MM(pla[:, c, :], glrTt[0:16, c * L:(c + 1) * L], wup[:, z, :]) for c in range(nch)],
                                   reads=[K(f"glrTt{jb}"), "wup"], writes=[bk])
                        yield
                        P.op("dve", TT_(latv, pla, gbb[0:L, l, z, :].unsqueeze(1).to_broadcast([L, nch, 256]), ALU.add),
                             reads=[bk, "gbb"], writes=[K("lat")])
                        release(bi_)
                        yield
                        P.op("act", ACT(latv, latv, AF.Exp, scale=-1.0), reads=[K("lat")], writes=[K("lat")])
                        yield
                        P.op("act", ACT(latv, latv, AF.Ln, bias=1.0), reads=[K("lat")], writes=[K("lat")])
                        yield
                        vsrc = vtk[0:L, 0:nch, :].rearrange("p c (h e) -> p (c h) e", h=4)
                        ksrc_tok = ktok[0:L, 0:nch, :]
                        ksrc_T = kTt
                        vkeys = [KV]
                        ktk_reads = [KT]
                        kT_reads = [KK]
                    elif mix == "h":
                        sgv = sgk[0:L, 0:nch, :]
                        hfin = ktok[0:L, 0:nch, :]
                        lb_b = lbb[0:L, 0, :].unsqueeze(1).to_broadcast([L, nch, 256])
                        oml_b = lbb[0:L, 1, :].unsqueeze(1).to_broadcast([L, nch, 256])
                        P.op("act", ACT(sgv, hfin, AF.Sigmoid), reads=[KT], writes=[K("sgk")])
                        P.op("act", ACT(kTc[:], hfTt[:], AF.Sigmoid), reads=[KK], writes=[K("kTc")])
                        yield
                        P.op("dve", TT_(latv, sgv, oml_b, ALU.mult), reads=[K("sgk"), "lbb"], writes=[K("lat")])
                        P.op("pool", TT_(kTc[:], kTc[:], lbT[:, :, 0:1].to_broadcast([64, 4, 128]), ALU.mult),
                             reads=[K("kTc"), "lbT"], writes=[K("kTc")])
                        yield
                        P.op("dve", TT_(latv, latv, lb_b, ALU.add), reads=[K("lat"), "lbb"], writes=[K("lat")])
                        P.op("pool", TT_(kTc[:], kTc[:], lbT[:, :, 1:2].to_broadcast([64, 4, 128]), ALU.add),
                             reads=[K("kTc"), "lbT"], writes=[K("kTc")])
                        yield
                        P.op("act", ACT(latv, latv, AF.Ln), reads=[K("lat")], writes=[K("lat")])
                        P.op("pool", TS(sgv, sgv, -1.0, 1.0, ALU.mult, ALU.add), reads=[K("sgk"), K("lat")],
                             writes=[K("sgk")])
                        yield
                        P.op("pool", TT_(sgv, sgv, oml_b, ALU.mult), reads=[K("sgk"), "lbb"], writes=[K("sgk")])
                        yield
                        vsrc = vtk[0:L, 0:nch, :].rearrange("p c (h e) -> p (c h) e", h=4)
                        ksrc_tok = sgv
                        ksrc_T = kTc
                        vkeys = [KV]
                        ktk_reads = [K("sgk")]
                        kT_reads = [K("kTc")]
                    else:
                        P.dma("sp", K(f"mgk{jb}"), DMA(mgk[0:L, 0:nch, :],
                                                       MGTOK[tok0:tok0 + 128, :].rearrange("(c l) f -> l c f", l=L)),
                              writes=[K(f"mgk{jb}")])
                        yield
                        P.op("dve", TT_(mgk[:], mgk[:], mgbb[0:L, l * 16:(l + 1) * 16].unsqueeze(1).to_broadcast(
                            [L, nch, 16]), ALU.add), reads=[K(f"mgk{jb}"), "mgbb"], writes=[K(f"mgk{jb}")])
                        yield
                        P.op("act", ACT(spm[:], mgk[:, :, z * 8 + 4:z * 8 + 8], AF.Exp, scale=-1.0), reads=[K(f"mgk{jb}")],
                             writes=[K("spm")])
                        P.op("act", ACT(eli[:], mgk[:, :, z * 8:z * 8 + 4], AF.Exp), reads=[K(f"mgk{jb}")], writes=[K("eli")])
                        yield
                        P.op("act", ACT(spm[:], spm[:], AF.Ln, bias=1.0), reads=[K("spm")], writes=[K("spm")])
                        elb = eli[:].rearrange("p c h -> p (c h)").unsqueeze(2).to_broadcast([L, nch * 4, 128])
                        P.op("pool", TT_(V2[0:L, 0:nch * 4, 0:128],
                                         vtk[0:L, 0:nch, :].rearrange("p c (h e) -> p (c h) e", h=4), elb, ALU.mult),
                             reads=[KV, K("eli")], writes=[KV2])
                        yield
                        P.op("dve", CP(lat[0:L, 0:nch, :].rearrange("p c (h d) -> p (c h) d", h=4),
                                       spm[:].rearrange("p c h -> p (c h)").unsqueeze(2).to_broadcast([L, nch * 4, 64])),
                             reads=[K("spm")], writes=[K("lat")])
                        P.op("pool", CP(V2[0:L, 0:nch * 4, 128:256], elb), reads=[K("eli")], writes=[KV2])
                        yield
                        vsrc = V2[0:L, 0:nch * 4, :]
                        ksrc_tok = ktok[0:L, 0:nch, :]
                        ksrc_T = kTt
                        vkeys = [KV2]
                        ktk_reads = [KT]
                        kT_reads = [KK]
                    nhalf = max(1, nch // 2)
                    got = yield from acquire(2 + nhalf)
                    (bank, bk, gi_), (bank2, bk2, ai_) = got[0], got[1]
                    GTp = bank[0:64, 0:4 * nch * L].rearrange("p (h c w) -> p h c w", h=4, c=nch)
                    fns = []
                    for h in range(4):
                        for c in range(nch):
                            fns.append(MM(GTp[:, h, c, :], lat[0:L, c, h * 64:(h + 1) * 64], wcat[:, 0:L]))
                    P.pe_group(fns, reads=[K("lat"), ck[0]], writes=[bk])
                    ABp = bank2[0:64, 0:4 * nch * 2].rearrange("p (h c w) -> p h c w", h=4, c=nch)
                    fns = []
                    for h in range(4):
                        for c in range(nch):
                            fns.append(MM(ABp[:, h, c, :], lat[0:L, c, h * 64:(h + 1) * 64], wcat[:, L:L + 2]))
                    P.pe_group(fns, reads=[K("lat"), ck[0]], writes=[bk2])
                    gtoks = []
                    for half in range(nhalf):
                        ncc = min(2, nch)
                        bank3, bk3, ti_ = got[2 + half]
                        pgk = bank3[0:L, 0:ncc * 256]
                        P.pe_group([MM(pgk, wm[:, :], lat[0:L, half * 2:half * 2 + ncc, :].rearrange("p c f -> p (c f)"))],
                                   reads=[K("lat"), ck[1]], writes=[bk3])
                        gtoks.append((pgk, half * 2, ncc, bk3, ti_))
                    yield
                    GQ = GcQ[:].rearrange("p h (c t) -> p h c t", c=nch)
                    GK = GcK[:].rearrange("p h (c t) -> p h c t", c=nch)
                    P.op("act", ACT(GQ, GTp, AF.Exp), reads=[bk], writes=[K("GcQ")])
                    yield
                    P.op("act", ACT(GK, GTp, AF.Exp, scale=-1.0), reads=[bk], writes=[K("GcK")])
                    release(gi_)
                    P.op("dve", TT_(QTs[:], qTt[:], GcQ[:], ALU.mult), reads=[KQ, K("GcQ")], writes=[KQTs])
                    yield
                    ABv = ABt[:, :, 0:nch, :]
                    P.op("act", ACT(ABv, ABp, AF.Exp), reads=[bk2], writes=[KABt])
                    release(ai_)
                    P.op("dve", TT_(KTs[:], ksrc_T[:], GcK[:], ALU.mult), reads=kT_reads + [K("GcK")], writes=[K("KTs")])
                    yield
                    for (pgk, c0, ncc, bk3, ti_) in gtoks:
                        P.op("act", ACT(Gtk[0:L, c0:c0 + ncc, :].rearrange("p c f -> p (c f)"), pgk, AF.Exp, scale=-1.0),
                             reads=[bk3], writes=[K("lat")])
                        release(ti_)
                    P.op("pool", TT_(KTz[:], KTs[:], kzm[:, :].unsqueeze(1).to_broadcast([64, 4, 128]), ALU.mult),
                         reads=[K("KTs"), ck[4]], writes=[K("KTz")])
                    yield
                    P.op("pool", TT_(Ktk[0:L, 0:nch, :], ksrc_tok, Gtk[0:L, 0:nch, :], ALU.mult),
                         reads=ktk_reads + [K("lat")], writes=[KKtk])
                    ((bank4, bk4, pi_),) = yield from acquire(1)
                    ATp = bank4[0:L, 0:nch * 4 * L].rearrange("p (a t) -> p a t", t=L)
                    fns = []
                    H2 = L // 2
                    for c in range(nch):
                        for h in range(4):
                            full_t = slice(c * L + H2, (c + 1) * L) if z == 0 else slice(c * L, c * L + H2)
                            zero_t = slice(c * L, c * L + H2) if z == 0 else slice(c * L + H2, (c + 1) * L)
                            fo = (H2, L) if z == 0 else (0, H2)
                            zo = (0, H2) if z == 0 else (H2, L)
                            fns.append(MM(ATp[:, c * 4 + h, fo[0]:fo[1]], KTs[:, h, c * L:(c + 1) * L], QTs[:, h, full_t]))
                            fns.append(MM(ATp[:, c * 4 + h, zo[0]:zo[1]], KTz[:, h, c * L:(c + 1) * L], QTs[:, h, zero_t]))
                    P.pe_group(fns, reads=[K("KTs"), K("KTz"), KQTs], writes=[bk4])
                    yield
                    ATmv = ATm[0:L, 0:nch * 4, 0:L]
                    P.op("dve", TT_(ATmv, ATp, mhi[:, :].unsqueeze(1).to_broadcast([L, nch * 4, L]), ALU.mult),
                         reads=[bk4, ck[3]], writes=[KATm])
                    release(pi_)
                    yield

                    pre_done[0] = ji + 1
                    yield

            def seq():
                P.op("dve", lambda e: e.memset(Sst[:], 0.0), writes=[K("S")])
                if mix == "m":
                    P.dma("sp", K("sini"), DMA(Sini[:, :, 0:128], state_dram(mC0, l, z)), writes=[K("Sini")])
                    P.dma("sp", K("sini"), DMA(Sini[:, :, 128:256], state_dram(mN0, l, z)), writes=[K("Sini")])
                    P.dma("sp", K("emi4"), DMA(emi4[:], mM0[0:1, (l * 2 + z) * 4:(l * 2 + z) * 4 + 4].partition_broadcast(
                        64)), writes=[K("emi4")])
                    P.op("act", ACT(emi4[:], emi4[:], AF.Exp), reads=[K("emi4")], writes=[K("emi4")])
                    P.op("dve", TT_(Sini[:], Sini[:], emi4[:].unsqueeze(2).to_broadcast([64, 4, 256]), ALU.mult),
                         reads=[K("Sini"), K("emi4")], writes=[K("Sini")])
                else:
                    P.dma("sp", K("sini"), DMA(Sini[:, :, 0:128], state_dram(gS0 if mix == "g" else hS0, l, z)),
                          writes=[K("Sini")])
                yield

                for ji, j in enumerate(order):
                    while pre_done[0] <= ji:
                        yield
                    tok0 = j * 128
                    seg = j // 2
                    jb = ji % 2
                    V2, Ktk, ATm, QTs, ABt = B["V2"][jb], B["Ktk"][jb], B["ATm"][jb], B["QTs"][jb], B["ABt"][jb]
                    KV2, KKtk, KATm, KQTs, KABt = K(f"V2{jb}"), K(f"Ktk{jb}"), K(f"ATm{jb}"), K(f"QTs{jb}"), K(f"ABt{jb}")
                    vtk = B["vtk"][jb]
                    if mix == "m":
                        vsrc = V2[0:L, 0:nch * 4, :]
                        vkeys = [KV2]
                    else:
                        vsrc = vtk[0:L, 0:nch, :].rearrange("p c (h e) -> p (c h) e", h=4)
                        vkeys = [K(f"vtk{jb}")]
                    ATmv = ATm[0:L, 0:nch * 4, 0:L]
                    ob = oev[j % 2]
                    okey = K(f"oev{j % 2}")
                    corder = list(range(nch)) if z == 0 else list(range(nch - 1, -1, -1))
                    for ci, c in enumerate(corder):
                        first_of_seg = (ci == 0) and ((j % 2 == 0) if z == 0 else (j % 2 == 1))
                        last_of_seg = (ci == nch - 1) and ((j % 2 == 1) if z == 0 else (j % 2 == 0))
                        if first_of_seg:
                            P.op("dve", TS(SE(Sst), SE(Sst), fl[0:64, z, seg, 0:1], None, ALU.mult),
                                 reads=[K("S"), "fl"], writes=[K("S")])
                            yield
                            P.op("dve", STT(SE(Sst), SE(Sini), fl[0:64, z, seg, 1:2], SE(Sst), ALU.mult,
                                            ALU.add), reads=[K("S"), "fl", K("Sini")], writes=[K("S")])
                            yield
                        P.op("dve", TT_(SE(Sint), SE(Sst), ABt[:, :, c, 0:1].to_broadcast([64, 4, E]), ALU.mult),
                             reads=[K("S"), KABt], writes=[K("Sint")])
                        pbanks = []
                        fns = []
                        if E == 256:
                            (b1, k1, i1_), (b2, k2, i2_), (bo, bok, io_) = yield from acquire(3)
                            Pp = [b1[0:64, :].rearrange("p (a e) -> p a e", a=2), b2[0:64, :].rearrange("p (a e) -> p a e", a=2)]
                            ppk = [k1, k2]
                            pidx = [i1_, i2_]
                        else:
                            (b1, k1, i1_), (bo, bok, io_) = yield from acquire(2)
                            Pp = [b1[0:64, 0:256].rearrange("p (a e) -> p a e", a=2),
                                  b1[0:64, 256:512].rearrange("p (a e) -> p a e", a=2)]
                            ppk = [k1, k1]
                            pidx = [i1_]
                        for h in range(4):
                            fns.append(MM(Pp[h // 2][:, h % 2, :], Ktk[0:L, c, h * 64:(h + 1) * 64], vsrc[:, c * 4 + h, 0:E]))
                        P.pe_group(fns, reads=vkeys + [KKtk], writes=list(set(ppk)))
                        yield
                        P.op("act", ACT(SE(Sbf), SE(Sint), AF.Copy), reads=[K("Sint")], writes=[K("Sbf")])
                        if E == 128:
                            P.op("dve", TT_(Stm[:, :, 0:E], b1[0:64, :].rearrange("p (a e) -> p a e", a=4),
                                            Sint[:, :, 0:E], ALU.add), reads=[k1, K("Sint")], writes=[K("Stm")])
                        else:
                            for a in range(2):
                                P.op("dve", TT_(Stm[:, 2 * a:2 * a + 2, 0:E], Pp[a], Sint[:, 2 * a:2 * a + 2, 0:E],
                                                ALU.add), reads=[ppk[a], K("Sint")], writes=[K("Stm")])
                        release(*pidx)
                        yield
                        P.op("dve", TT_(SE(Sst), SE(Stm), ABt[:, :, c, 1:2].to_broadcast([64, 4, E]), ALU.mult),
                             reads=[K("Stm"), KABt], writes=[K("S")])
                        if mix == "m":
                            OC = bo[:, :].rearrange("p (h w t) -> p h w t", h=4, w=2)
                        else:
                            OC = bo[:, 0:4 * L].rearrange("p (h w t) -> p h w t", h=4, w=1)
                        fns = []
                        cols = slice(c * L, (c + 1) * L)
                        for h in range(4):
                            fns.append(MM(OC[:, h, 0, :], vsrc[:, c * 4 + h, 0:128], ATmv[:, c * 4 + h, :], start=True,
                                          stop=False))
                            fns.append(MM(OC[:, h, 0, :], Sbf[:, h, 0:128], QTs[:, h, cols], start=False, stop=True))
                            if mix == "m":
                                fns.append(MM(OC[:, h, 1, :], vsrc[:, c * 4 + h, 128:256], ATmv[:, c * 4 + h, :],
                                              start=True, stop=False))
                                fns.append(MM(OC[:, h, 1, :], Sbf[:, h, 128:256], QTs[:, h, cols], start=False,
                                              stop=True))
                        P.pe_group(fns, reads=vkeys + [KATm, K("Sbf"), KQTs], writes=[bok])
                        yield
                        if mix == "m":
                            dnv = dn[:, :, 0:L]
                            P.op("dve", TS(dnv, OC[:, :, 1, :], -1.0, 1.0, ALU.mult, ALU.max), reads=[bok], writes=[K("dn")])
                            yield
                            P.op("dve", TT_(dnv, OC[:, :, 1, :], dnv, ALU.max), reads=[bok, K("dn")], writes=[K("dn")])
                            yield
                            P.op("dve", lambda e, dnv=dnv: e.reciprocal(dnv, dnv), reads=[K("dn")], writes=[K("dn")])
                            yield
                            P.op("dve", TT_(ob[:, :, cols], OC[:, :, 0, :], dnv, ALU.mult), reads=[bok, K("dn")],
                                 writes=[okey])
                        else:
                            P.op("act", ACT(ob[:, :, cols], OC[:, :, 0, :], AF.Copy), reads=[bok], writes=[okey])
                        release(io_)
                        if last_of_seg:
                            if mix == "m":
                                P.op("dve", TT_(Stm[:], Sst[:], Eo4[:, seg, :].unsqueeze(2).to_broadcast([64, 4, 256]),
                                                ALU.mult), reads=[K("S"), K("Eo4")], writes=[K("Stm")])
                                P.dma("pool", K("stoC"), DMA(state_dram(o_mC[:, seg], l, z), Stm[:, :, 0:128]),
                                      reads=[K("Stm")], writes=["d_so"])
                                for h in range(4):
                                    dstn = o_mN[l, seg, z, h, :].rearrange("(d o) -> d o", o=1)
                                    P.dma("pool", K(f"stoN{h}"), DMA(dstn, Stm[:, h, 128:129]), reads=[K("Stm")],
                                          writes=["d_so"])
                            else:
                                od = o_gS if mix == "g" else o_hS
                                P.dma("pool", K("stoS"), DMA(state_dram(od[:, seg], l, z), Sst[:, :, 0:128]), reads=[K("S")],
                                      writes=["d_so"])
                        yield
                    r0 = MIXI[mix] * 512
                    P.dma("pool", okey, DMA(OTs[z, r0:r0 + 512, tok0:tok0 + 128].rearrange("(h e) t -> e h t", e=128), ob[:]),
                          reads=[okey], writes=["d_ot"])
                    seq_done[0] = ji + 1
                    yield


            return pre(), seq()

        def conv_gen(l_):
            items = []
            if l_ == 0:
                items += [("fin", 0, 1), ("fout", 0, 1), ("br", 0, 0), ("br", 0, 1), ("br", 0, 2), ("out", 0, None)]
                items += [("fin", 1, 0), ("fout", 1, 0), ("in", 1, None), ("fin", 1, 1), ("fout", 1, 1),
                          ("br", 1, 0), ("br", 1, 1), ("br", 1, 2), ("out", 1, None)]
            n = 0
            for (name, l2, idx) in items:
                src = WF[name][l2] if idx is None else WF[name][l2, idx]
                dst = WB[name][l2] if idx is None else WB[name][l2, idx]
                rows = src.shape[0]
                for r0 in range(0, rows, 128):
                    r1 = min(rows, r0 + 128)
                    P.dma("pool", f"cv{n % 4}", DMA(dst[r0:r1, :], src[r0:r1, :]), writes=[f"wb_{name}{l2}{idx}"])
                    n += 1
                    for _ in range(14 if l_ == 0 else 1):
                        yield

        def lockstep(gens, background=None):
            gens = list(gens)
            bg = background if background is not None else []
            while gens:
                for g in list(gens):
                    try:
                        next(g)
                    except StopIteration:
                        gens.remove(g)
                for g in list(bg):
                    try:
                        next(g)
                    except StopIteration:
                        bg.remove(g)

        def stage_S(l):
            hgrn_lb(l)
            bg = []
            if not DBG_SKIP_A and l == 0:
                bg.append(conv_gen(l))
            mp = []
            if "mpass" in DBG_SPARTS:
                mp = [mlstm_m_pass(l, 0, 0), mlstm_m_pass(l, 1, 1)]
            bg_all = bg + mp
            for mix in ("g", "h", "m"):
                if mix == "m" and mp:
                    rest = [g for g in mp if g in bg_all]
                    lockstep(rest)
                    for g in rest:
                        if g in bg_all:
                            bg_all.remove(g)
                if mix in DBG_SPARTS:
                    p0, s0 = scan_chain(l, mix, 0, 0)
                    p1, s1 = scan_chain(l, mix, 1, 1)
                    lockstep([p0, s0, p1, s1], background=bg_all)
            lockstep(list(bg_all))

        bcnt = [0]

        def B_phase1(l, tt, yb):
            yTb = yTbs[yb]
            tok0 = tt * TT
            for i in range(3):
                for h in range(4):
                    n = bcnt[0] % 2
                    bcnt[0] += 1
                    r0 = i * 512 + h * 128
                    o0, o1 = ofb[2 * n], ofb[2 * n + 1]
                    P.dma("sp", f"ofb{2 * n}", DMA(o0[:], OTs[0, r0:r0 + 128, tok0:tok0 + TT]),
                          writes=[f"ofb{2 * n}"])
                    P.dma("sp", f"ofb{2 * n + 1}", DMA(o1[:], OTs[1, r0:r0 + 128, tok0:tok0 + TT]),
                          writes=[f"ofb{2 * n + 1}"])
                    P.dma("sp", f"gtb2{n}", DMA(gtb2[n][:], GATES[r0:r0 + 128, tok0:tok0 + TT]),
                          writes=[f"gtb2{n}"])
                    P.op("pool", TT_(o0[:], o0[:], o1[:], ALU.add), reads=[f"ofb{2 * n}", f"ofb{2 * n + 1}"],
                         writes=[f"ofb{2 * n}"])
                    P.op("act", ACT(sq2[n][:], o0[:], AF.Square), reads=[f"ofb{2 * n}"], writes=[f"sq2{n}"])
                    ph = pb[7]
                    P.pe_group([MM(ph[:, :], ones_b[:], sq2[n][:])], reads=[f"sq2{n}", "ones_b"], writes=["pb7"])
                    P.op("dve", TS(tmp2[n][:], ph[:, :], 1.0 / 128, EPS, ALU.mult, ALU.add), reads=["pb7"],
                         writes=[f"tmp2{n}"])
                    P.op("act", ACT(tmp2[n][:], tmp2[n][:], AF.Sqrt), reads=[f"tmp2{n}"], writes=[f"tmp2{n}"])
                    P.op("dve", lambda e, n=n: e.reciprocal(tmp2[n][:], tmp2[n][:]), reads=[f"tmp2{n}"],
                         writes=[f"tmp2{n}"])
                    P.op("pool", TT_(o0[:], o0[:], tmp2[n][:], ALU.mult), reads=[f"ofb{2 * n}", f"tmp2{n}"],
                         writes=[f"ofb{2 * n}"])
                    P.op("dve", STT(yTb[:, i * 4 + h, :], o0[:], hnorm[:, l, i, h:h + 1], gtb2[n][:], ALU.mult,
                                    ALU.mult), reads=[f"ofb{2 * n}", "hnorm", f"gtb2{n}"],
                         writes=[f"yTb{yb}_{i * 4 + h}"])

        def stage_B(l):
            cap = P.capture(lambda: B_phase1(l, 0, 0))
            P.commit(cap)
            for tt in range(NT):
                capR = P.capture(lambda: B_rest(l, tt, tt % 2))
                capP = P.capture(lambda: B_phase1(l, tt + 1, (tt + 1) % 2)) if tt + 1 < NT else []
                P.commit(capR, capP)

        def B_rest(l, tt, yb):
            dst = X3 if l == 0 else yT_out
            yTb = yTbs[yb]
            if True:
                tok0 = tt * TT
                load_x(X1, tok0)
                for i in range(3):
                    slot = ring_next()
                    wv = load_w(slot, [(wsrc("br", l, i)[0], 0, D, 0)], 4, D, bf=True)
                    for dc in range(KC):
                        n = bcnt[0] % 2
                        bcnt[0] += 1
                        pp = pb[5 + n]
                        fns = [MM(pp[:, :], wv[:, h, dc * 128:(dc + 1) * 128], yTb[:, i * 4 + h, :], start=(h == 0),
                                  stop=(h == 3)) for h in range(4)]
                        P.pe_group(fns, reads=[f"ring{slot}"] + [f"yTb{yb}_{i * 4 + h}" for h in range(4)],
                                   writes=[f"pb{5 + n}"])
                        r0 = 1536 + i * D + dc * 128
                        P.dma("sp", f"gtb{n}", DMA(gtb[n][:], GATES[r0:r0 + 128, tok0:tok0 + TT]), reads=["d_gates"],
                              writes=[f"gtb{n}"])
                        if i == 0:
                            P.op("dve", TT_(yF[:, dc, :], pp[:, :], gtb[n][:], ALU.mult),
                                 reads=[f"pb{5 + n}", f"gtb{n}"], writes=[f"yF{dc}"])
                        else:
                            P.op("dve", TT_(tmp[n][:], pp[:, :], gtb[n][:], ALU.mult), reads=[f"pb{5 + n}", f"gtb{n}"],
                                 writes=[f"tmp{n}"])
                            P.op("pool", TT_(yF[:, dc, :], yF[:, dc, :], tmp[n][:], ALU.add),
                                 reads=[f"yF{dc}", f"tmp{n}"], writes=[f"yF{dc}"])
                for k in range(KC):
                    P.op("act", ACT(hT[:, k, :], yF[:, k, :], AF.Copy), reads=[f"yF{k}"], writes=[f"hT{k}"])
                for ob_ in range(2):
                    slot = ring_next()
                    wv = load_w(slot, [(wsrc("out", l)[0], ob_ * 512, 512, 0)], KC, 512, bf=True)
                    for c in range(4):
                        dc = ob_ * 4 + c
                        n = bcnt[0] % 2
                        bcnt[0] += 1
                        pp = pb[5 + n]
                        fns = [MM(pp[:, :], wv[:, k, c * 128:(c + 1) * 128], hT[:, k, :], start=(k == 0),
                                  stop=(k == KC - 1)) for k in range(KC)]
                        P.pe_group(fns, reads=[f"ring{slot}"] + hkeys, writes=[f"pb{5 + n}"])
                        P.op("act", ACT(yF[:, dc, :], pp[:, :], AF.Copy), reads=[f"pb{5 + n}"], writes=[f"yF{dc}"])
                resid_update(l, 1)
                ffn(l, 1, 2, tok0)
                store_x(dst, tok0, "d_x")

        stage_S_consts()
        P.barrier()
        stage0()
        done = False
        for l in layers:
            stq[0] = "sp" if l == 0 else "pool"
            if not DBG_SKIP_A:
                stage_A(l)
            P.barrier()
            if stop_after == ("A", l):
                break
            stage_S(l)
            P.barrier()
            if stop_after == ("S", l):
                break
            stq[0] = "pool"
            stage_B(l)
            P.barrier()
        P.finish(block)
        print("instructions:", P.n_inst, "dma sems:", len(P.dma_sems))
    return nc


_CACHE = {}


def make_in_maps(inputs):
    f = lambda a: np.ascontiguousarray(np.asarray(a, dtype=np.float32))
    x_prompt, x_sample = f(inputs["x_prompt"]), f(inputs["x_sample"])
    c, c_ctx = f(inputs["c"]), f(inputs["c_ctx"])
    common = {
        "w_ada": f(inputs["w_ada"]),
        "bada": f(np.transpose(f(inputs["b_ada"]).reshape(2, 72, 128), (2, 0, 1))),
        "npre": f(np.transpose(f(inputs["norm_pre"]).reshape(2, 3, KC, 128), (3, 0, 1, 2))),
        "npost": f(np.transpose(f(inputs["norm_post"]).reshape(2, 3, KC, 128), (3, 0, 1, 2))),
        "hnorm": f(np.transpose(f(inputs["head_norm"]).reshape(2, 3, 4, 128), (3, 0, 1, 2))),
        "w_ffn_in": f(inputs["w_ffn_in"]), "w_ffn_out": f(inputs["w_ffn_out"]), "w_in": f(inputs["w_in"]),
        "w_branch": f(inputs["w_branch"]), "w_out": f(inputs["w_out"]),
        "mgb": f(inputs["mlstm_gate_bias"]).reshape(1, 32),
        "mgbT": f(f(inputs["mlstm_gate_bias"]).reshape(8, 4).T),
        "gla_w_up": f(inputs["gla_w_up"]),
        "gla_b": f(inputs["gla_b"]).reshape(1, 1024),
        "hgam": f(inputs["hgrn_gamma"]).reshape(1, 512),
        "hgamT": f(np.transpose(f(inputs["hgrn_gamma"]).reshape(2, 4, 64), (2, 0, 1))),
        "ones": np.ones((128, 128), np.float32),
    }
    for mix, L in MIXL.items():
        scale = {"m": -1.0, "g": -1.0 / 16.0, "h": 1.0}[mix]
        for z in (0, 1):
            wcat, wm, mlo, mhi, kz = scan_consts(L, z, scale)
            common[f"kz_{mix}{z}"] = kz
            common[f"wcat_{mix}{z}"] = wcat
            common[f"wm_{mix}{z}"] = wm
            common[f"mlo_{mix}{z}"] = mlo
            common[f"mhi_{mix}{z}"] = mhi
    pos_s = grid_position_T(T)
    zeros_pos = np.zeros((D, T), np.float32)
    maps = []
    for core in range(8):
        m = dict(common)
        fl = np.zeros((2, NSEG, 2), np.float32)
        if core < 4:
            b = core
            m["xT"] = f(x_sample[b].T)
            m["posT"] = pos_s
            m["cT"] = f(c[b].reshape(KC, 128).T)
            m["mC0"] = f(np.transpose(f(inputs["state_mlstm_C"])[b], (0, 1, 2, 3, 4)))
            n0 = f(inputs["state_mlstm_n"])[b]
            m["mN0"] = f(np.repeat(n0[..., None], 128, axis=-1))
            m["mM0"] = f(inputs["state_mlstm_m"])[b].reshape(1, 16)
            m["mM0T"] = f(f(inputs["state_mlstm_m"])[b].reshape(4, 4).T)
            m["gS0"] = f(inputs["state_gla_S"])[b]
            m["hS0"] = f(inputs["state_hgrn_S"])[b]
            fl[0, :, 0] = 1.0
            fl[0, 0, 0] = 0.0
            fl[0, 0, 1] = 1.0
            fl[1, :, 0] = 1.0
            fl[1, NSEG - 1, 0] = 0.0
            fl[1, NSEG - 1, 1] = 1.0
        else:
            j = core - 4
            xp = np.zeros((T, D), np.float32)
            xp[:2048] = x_prompt[8 * j:8 * j + 8].reshape(2048, D)
            m["xT"] = f(xp.T)
            m["posT"] = zeros_pos
            m["cT"] = f(c_ctx.reshape(KC, 128).T)
            m["mC0"] = np.zeros((2, 2, 4, 64, 128), np.float32)
            m["mN0"] = np.zeros((2, 2, 4, 64, 128), np.float32)
            m["mM0"] = np.zeros((1, 16), np.float32)
            m["mM0T"] = np.zeros((4, 4), np.float32)
            m["gS0"] = np.zeros((2, 2, 4, 64, 128), np.float32)
            m["hS0"] = np.zeros((2, 2, 4, 64, 128), np.float32)
        m["flags"] = fl.reshape(1, -1)
        maps.append(m)
    return maps


def kernel(**inputs):
    if "nc" not in _CACHE:
        _CACHE["nc"] = build()
    nc = _CACHE["nc"]
    maps = make_in_maps(inputs)
    res = run_bass_kernel_spmd(nc, maps, core_ids=list(range(8)))
    R = res.results
    y_sample = np.stack([np.ascontiguousarray(R[b]["yT"].T) for b in range(4)], axis=0).astype(np.float32)
    yp = []
    mC, mN, mM, gS, hS = [], [], [], [], []
    for j in range(4):
        r = R[4 + j]
        yp.append(np.ascontiguousarray(r["yT"].T)[:2048].reshape(8, 256, D))
        mC.append(np.transpose(r["o_mC"][:, :8], (1, 0, 2, 3, 4, 5)))
        mN.append(np.transpose(r["o_mN"][:, :8], (1, 0, 2, 3, 4)))
        mM.append(np.transpose(r["o_mM"][:, :8], (1, 0, 2, 3)))
        gS.append(np.transpose(r["o_gS"][:, :8], (1, 0, 2, 3, 4, 5)))
        hS.append(np.transpose(r["o_hS"][:, :8], (1, 0, 2, 3, 4, 5)))
    cat = lambda lst: np.ascontiguousarray(np.concatenate(lst, axis=0)).astype(np.float32)
    return (cat(yp), y_sample, cat(mC), cat(mN), cat(mM), cat(gS), cat(hS))
```

```python
import math
from contextlib import ExitStack
import numpy as np
import ml_dtypes
import concourse.bass as bass
import concourse.mybir as mybir
from concourse.bass_utils import run_bass_kernel_spmd

F32 = mybir.dt.float32
BF16 = mybir.dt.bfloat16
AF = mybir.ActivationFunctionType
ALU = mybir.AluOpType
AX = mybir.AxisListType

D = 1024
KC = 8
FF = 2816
FC = 22
T = 4096
TT = 512
NT = T // TT
NSEG = 16
SEGLEN = 256
EPS = 1e-6
INC = 7984
BIG = 3.0e38
C_MQ, C_MK, C_MV, C_MO, C_MIF = 0, 256, 512, 1024, 1536
C_GQ, C_GK, C_GV, C_GR, C_GLR = 1552, 1808, 2064, 2576, 3088
C_HQ, C_HF, C_HV, C_HG, C_MG = 3120, 3376, 3888, 4400, 4912
MIXL = {"m": 64, "g": 64, "h": 32}
MIXI = {"m": 0, "g": 1, "h": 2}
DBG_SPARTS = {"mpass", "m", "g", "h"}
DBG_SKIP_A = False
DBG_NSUB = None
DBG_LEVEL = 9


class Prog:
    ENGS = ("pe", "act", "dve", "pool", "sp")

    def __init__(self, nc, stack):
        self.nc = nc
        self.stack = stack
        self.sem = {e: stack.enter_context(nc.semaphore("prog_" + e)) for e in self.ENGS}
        self.cnt = {e: 0 for e in self.ENGS}
        self.prog = {e: [] for e in self.ENGS}
        self.seen = {e: {} for e in self.ENGS}
        self.last_w = {}
        self.readers = {}
        self.dma_sems = {}
        self.dma_cnt = {}
        self.n_inst = 0
        self.cap = None

    def capture(self, fn):
        assert self.cap is None
        self.cap = []
        try:
            fn()
        finally:
            lst, self.cap = self.cap, None
        return lst

    def commit(self, *lists):
        lists = [l_ for l_ in lists if l_]
        if not lists:
            return
        main = max(lists, key=len)
        others = [l_ for l_ in lists if l_ is not main]
        pos = [0] * len(others)
        for i, item in enumerate(main):
            self._replay(item)
            for oi, ol in enumerate(others):
                want = (len(ol) * (i + 1)) // len(main)
                while pos[oi] < want:
                    self._replay(ol[pos[oi]])
                    pos[oi] += 1

    def _replay(self, item):
        kind = item[0]
        if kind == "op":
            self.op(*item[1:])
        elif kind == "pe":
            self.pe_group(*item[1:])
        else:
            self.dma(*item[1:])

    def _need(self, eng, reads, writes):
        toks = []
        for k in reads:
            t = self.last_w.get(k)
            if t is not None:
                toks.append(t)
        for k in writes:
            t = self.last_w.get(k)
            if t is not None:
                toks.append(t)
            toks.extend(self.readers.get(k, ()))
        best = {}
        for (s, v) in toks:
            if v > best.get(s.name, (None, 0))[1]:
                best[s.name] = (s, v)
        out = []
        seen = self.seen[eng]
        own = self.sem[eng].name if eng == "pe" else None
        for name, (s, v) in best.items():
            if name == own:
                continue
            if seen.get(name, 0) >= v:
                continue
            seen[name] = v
            out.append((s, v))
        return out

    def _commit(self, tok, reads, writes):
        for k in writes:
            self.last_w[k] = tok
            self.readers[k] = []
        for k in reads:
            if k in writes:
                continue
            self.readers.setdefault(k, []).append(tok)

    def op(self, eng, fn, reads=(), writes=()):
        if self.cap is not None:
            self.cap.append(("op", eng, fn, tuple(reads), tuple(writes)))
            return None
        waits = self._need(eng, reads, writes)
        self.cnt[eng] += 1
        sem = self.sem[eng]
        tok = (sem, self.cnt[eng])
        self._commit(tok, reads, writes)

        def emit(e, fn=fn, waits=waits, sem=sem):
            for (s, v) in waits:
                e.wait_ge(s, v)
            fn(e).then_inc(sem, 1)
        self.prog[eng].append(emit)
        self.n_inst += 1
        return tok

    def pe_group(self, fns, reads=(), writes=()):
        if self.cap is not None:
            self.cap.append(("pe", fns, tuple(reads), tuple(writes)))
            return None
        waits = self._need("pe", reads, writes)
        self.cnt["pe"] += 1
        sem = self.sem["pe"]
        tok = (sem, self.cnt["pe"])
        self._commit(tok, reads, writes)

        def emit(e, fns=fns, waits=waits, sem=sem):
            for (s, v) in waits:
                e.wait_ge(s, v)
            for f in fns[:-1]:
                f(e)
            fns[-1](e).then_inc(sem, 1)
        self.prog["pe"].append(emit)
        self.n_inst += len(fns)
        return tok

    def dma(self, eng, slot, fn, reads=(), writes=()):
        if self.cap is not None:
            self.cap.append(("dma", eng, slot, fn, tuple(reads), tuple(writes)))
            return None
        slot = f"{slot}_{eng}"
        if slot not in self.dma_sems:
            self.dma_sems[slot] = self.stack.enter_context(self.nc.semaphore("dma_" + str(slot)))
            self.dma_cnt[slot] = 0
        waits = self._need(eng, reads, writes)
        self.dma_cnt[slot] += 16
        sem = self.dma_sems[slot]
        tok = (sem, self.dma_cnt[slot])
        self._commit(tok, reads, writes)

        def emit(e, fn=fn, waits=waits, sem=sem):
            for (s, v) in waits:
                e.wait_ge(s, v)
            fn(e).then_inc(sem, 16)
        self.prog[eng].append(emit)
        self.n_inst += 1
        return tok

    def barrier(self):
        toks = [(self.sem[e], self.cnt[e]) for e in self.ENGS if self.cnt[e] > 0]
        toks += [(self.dma_sems[s], self.dma_cnt[s]) for s in self.dma_sems]
        for eng in self.ENGS:
            seen = self.seen[eng]
            waits = []
            for (s, v) in toks:
                if seen.get(s.name, 0) >= v:
                    continue
                seen[s.name] = v
                waits.append((s, v))

            def emit(e, waits=waits):
                for (s, v) in waits:
                    e.wait_ge(s, v)
            self.prog[eng].append(emit)
        self.last_w = {}
        self.readers = {}

    def finish(self, block):
        self.barrier()
        P = self.prog

        @block.tensor
        def _(e):
            for f in P["pe"]:
                f(e)

        @block.scalar
        def _(e):
            for f in P["act"]:
                f(e)

        @block.vector
        def _(e):
            for f in P["dve"]:
                f(e)

        @block.gpsimd
        def _(e):
            for f in P["pool"]:
                f(e)

        @block.sync
        def _(e):
            for f in P["sp"]:
                f(e)


def MM(out, lhsT, rhs, start=True, stop=True):
    return lambda e: e.matmul(out, lhsT, rhs, start=start, stop=stop)


def ACT(out, in_, func, **kw):
    return lambda e: e.activation(out, in_, func, **kw)


def TT_(out, a, b, op):
    return lambda e: e.tensor_tensor(out, a, b, op)


def TS(out, a, s1, s2, op0, op1=None):
    if op1 is None:
        return lambda e: e.tensor_scalar(out, a, s1, None, op0)
    return lambda e: e.tensor_scalar(out, a, s1, s2, op0, op1)


def STT(out, in0, scalar, in1, op0, op1):
    return lambda e: e.scalar_tensor_tensor(out, in0, scalar, in1, op0, op1)


def CP(out, in_):
    return lambda e: e.tensor_copy(out, in_)


def DMA(out, in_):
    return lambda e: e.dma_start(out=out, in_=in_)


def scan_consts(L, z, scale):
    s = np.arange(L)[:, None]
    t = np.arange(L)[None, :]
    if z == 0:
        mid = L // 2 - 1
        Wm = ((s > mid) & (s <= t)).astype(np.float32) - ((s > t) & (s <= mid)).astype(np.float32)
        wa = (np.arange(L) <= mid).astype(np.float32)
        wb = (np.arange(L) > mid).astype(np.float32)
        mask = (s <= t).astype(np.float32)
    else:
        mid = L // 2
        Wm = ((s >= t) & (s < mid)).astype(np.float32) - ((s >= mid) & (s < t)).astype(np.float32)
        wa = (np.arange(L) >= mid).astype(np.float32)
        wb = (np.arange(L) < mid).astype(np.float32)
        mask = (s >= t).astype(np.float32)
    Wcat = np.concatenate([Wm, wa[:, None], wb[:, None]], axis=1) * scale
    pos_in_chunk = np.arange(128) % L
    if z == 0:
        kz = (pos_in_chunk < L // 2).astype(np.float32)
    else:
        kz = (pos_in_chunk >= L // 2).astype(np.float32)
    kz = np.ascontiguousarray(np.broadcast_to(kz[None, :], (64, 128))).astype(np.float32)
    return (Wcat.astype(np.float32), (Wm * scale).astype(np.float32),
            (-BIG * mask).astype(np.float32), mask.astype(np.float32), kz)


def grid_position_T(n_tokens, grid_w=64):
    rows = n_tokens // grid_w
    quarter = D // 4
    freqs = np.exp(-math.log(10000.0) * np.arange(quarter, dtype=np.float32) / quarter).astype(np.float32)
    r = np.arange(rows, dtype=np.float32)[:, None] * freqs
    cl = np.arange(grid_w, dtype=np.float32)[:, None] * freqs
    r_emb = np.concatenate([np.sin(r), np.cos(r)], axis=-1)
    c_emb = np.concatenate([np.sin(cl), np.cos(cl)], axis=-1)
    emb = np.concatenate([np.broadcast_to(r_emb[:, None], (rows, grid_w, D // 2)),
                          np.broadcast_to(c_emb[None], (rows, grid_w, D // 2))], axis=-1)
    return np.ascontiguousarray(emb.reshape(rows * grid_w, D).T.astype(np.float32))


def build(dbg=False, stop_after=None, layers=(0, 1)):
    nc = bass.Bass("TRN2", target_bir_lowering=False)

    def din(name, shape, dt=F32):
        return nc.dram_tensor(name, list(shape), dt, kind="ExternalInput").ap()

    def dout(name, shape, dt=F32):
        return nc.dram_tensor(name, list(shape), dt, kind="ExternalOutput").ap()

    def dscr(name, shape, dt=F32):
        return nc.dram_tensor(name, list(shape), dt, kind=("ExternalOutput" if dbg else "Internal")).ap()

    xT_in = din("xT", [D, T])
    posT = din("posT", [D, T])
    cT_in = din("cT", [128, KC])
    w_ada = din("w_ada", [2, D, 9 * D])
    bada_in = din("bada", [128, 2, 72])
    npre_in = din("npre", [128, 2, 3, KC])
    npost_in = din("npost", [128, 2, 3, KC])
    hnorm_in = din("hnorm", [128, 2, 3, 4])
    w_ffn_in = din("w_ffn_in", [2, 2, D, 2 * FF])
    w_ffn_out = din("w_ffn_out", [2, 2, FF, D])
    w_in = din("w_in", [2, D, INC])
    w_branch = din("w_branch", [2, 3, 512, D])
    w_out = din("w_out", [2, D, D])
    mgb_in = din("mgb", [1, 32])
    gwup_in = din("gla_w_up", [2, 2, 16, 256])
    gb_in = din("gla_b", [1, 2 * 2 * 256])
    hgam_in = din("hgam", [1, 512])
    hgamT_in = din("hgamT", [64, 2, 4])
    mC0 = din("mC0", [2, 2, 4, 64, 128])
    mN0 = din("mN0", [2, 2, 4, 64, 128])
    mM0 = din("mM0", [1, 16])
    mM0T_in = din("mM0T", [4, 4])
    mgbT_in = din("mgbT", [4, 8])
    gS0 = din("gS0", [2, 2, 4, 64, 128])
    hS0 = din("hS0", [2, 2, 4, 64, 128])
    flags_in = din("flags", [1, 2 * NSEG * 2])
    ones_in = din("ones", [128, 128])
    consts_in = {}
    for mix, L in MIXL.items():
        for z in (0, 1):
            consts_in[(mix, z)] = (din(f"wcat_{mix}{z}", [L, L + 2]), din(f"wm_{mix}{z}", [L, L]),
                                   din(f"mlo_{mix}{z}", [L, L]), din(f"mhi_{mix}{z}", [L, L]),
                                   din(f"kz_{mix}{z}", [64, 128]))
    yT_out = dout("yT", [D, T])
    o_mC = dout("o_mC", [2, NSEG, 2, 4, 64, 128])
    o_mN = dout("o_mN", [2, NSEG, 2, 4, 64])
    o_mM = dout("o_mM", [2, NSEG, 2, 4])
    o_gS = dout("o_gS", [2, NSEG, 2, 4, 64, 128])
    o_hS = dout("o_hS", [2, NSEG, 2, 4, 64, 128])
    X1 = dscr("X1", [D, T])
    X3 = dscr("X3", [D, T])
    QKT = dscr("QKT", [1280, T])
    HFT = dscr("HFT", [512, T])
    GLRT = dscr("GLRT", [32, T])
    MGT = dscr("MGT", [16, T])
    KTOK = dscr("KTOK", [T, 512])
    HFTOK = dscr("HFTOK", [T, 512])
    MGTOK = dscr("MGTOK", [T, 16])
    VT = dscr("V", [T, 1536], BF16)
    GATES = dscr("GATES", [4608, T], BF16)
    OTs = dscr("OT", [2, 1536, T])
    WB = {"fin": dscr("WB_fin", [2, 2, D, 2 * FF], BF16), "fout": dscr("WB_fout", [2, 2, FF, D], BF16),
          "in": dscr("WB_in", [2, D, INC], BF16), "br": dscr("WB_br", [2, 3, 512, D], BF16),
          "out": dscr("WB_out", [2, D, D], BF16)}
    WF = {"fin": w_ffn_in, "fout": w_ffn_out, "in": w_in, "br": w_branch, "out": w_out}
    NOCONV = {("fin", 0, 0), ("fout", 0, 0), ("in", 0, None)}

    def wsrc(name, l, idx=None):
        conv = (name, l, idx) not in NOCONV
        t = WB[name] if conv else WF[name]
        return (t[l] if idx is None else t[l, idx]), conv

    dests = {"qkt": QKT, "hft": HFT, "glrt": GLRT, "mgt": MGT, "ktok": KTOK, "hftok": HFTOK,
             "mgtok": MGTOK, "v": VT, "gates": GATES}

    with ExitStack() as st:
        P = Prog(nc, st)

        def sb(name, shape, dt=F32):
            return st.enter_context(nc.sbuf_tensor("s_" + name, list(shape), dt))

        ones_f = sb("ones_f", [128, 128])
        ones_b = sb("ones_b", [128, 128], BF16)
        cTt = sb("cTt", [128, KC])
        scT = sb("scT", [128, KC])
        bada = sb("bada", [128, 2, 72])
        npre = sb("npre", [128, 2, 3, KC])
        npost = sb("npost", [128, 2, 3, KC])
        hnorm = sb("hnorm", [128, 2, 3, 4])
        modT = sb("modT", [128, 2, 72])
        A1 = sb("A1", [128, 2, 3, KC])
        G1 = sb("G1", [128, 2, 3, KC])
        fl = sb("fl", [128, 2, NSEG, 2])
        arena = sb("arena", [128, 17408])
        arena2 = sb("arena2", [128, 17408])

        def carve(off, parts, free, dt, ar=None):
            ar = arena if ar is None else ar
            n = 1
            for d_ in free:
                n *= d_
            if dt == BF16:
                v = ar[:].bitcast(BF16)[0:parts, off // 2:off // 2 + n]
            else:
                v = ar[0:parts, off // 4:off // 4 + n]
            if len(free) == 2:
                v = v.rearrange("p (a b) -> p a b", a=free[0])
            elif len(free) == 3:
                v = v.rearrange("p (a b c) -> p a b c", a=free[0], b=free[1])
            return v

        actT = carve(0, 128, [FC, TT], BF16)
        yF = carve(22528, 128, [KC, TT], F32)
        ring = [carve(12288 * i, 128, [3072], F32, arena2) for i in range(3)]
        xT = carve(36864, 128, [KC, TT], F32, arena2)
        hT = carve(53248, 128, [KC, TT], BF16, arena2)
        arena3 = sb("arena3", [128, 7680])
        posb = [carve(2048 * i, 128, [TT], F32, arena3) for i in range(2)]
        sq = [carve(4096 + 1024 * i, 128, [TT], BF16, arena3) for i in range(2)]
        tmp = [carve(6144 + 2048 * i, 128, [TT], F32, arena3) for i in range(2)]
        rstd = carve(10240, 128, [TT], F32, arena3)
        sgt = [carve(12288 + 2048 * i, 128, [TT], F32, arena3) for i in range(2)]
        stf = [carve(61440 + 2048 * i, 128, [TT], F32, arena2) for i in range(4)]
        stb = [carve(16384 + 1024 * i, 128, [TT], BF16, arena3) for i in range(4)]
        yTbs = [carve(38912, 128, [12, TT], BF16), carve(51200, 128, [12, TT], BF16)]
        sq2 = [carve(63488 + 1024 * i, 128, [TT], BF16) for i in range(2)]
        tmp2 = [carve(65536 + 2048 * i, 128, [TT], F32) for i in range(2)]
        gtb2 = [sb(f"gtb2_{i}", [128, TT], BF16) for i in range(2)]
        ofb = [carve(20480 + 2048 * i, 128, [TT], F32, arena3) for i in range(4)]
        gtb = [carve(28672 + 1024 * i, 128, [TT], BF16, arena3) for i in range(2)]
        pb = [st.enter_context(nc.psum_tensor(f"pb{i}", [128, 512], F32)) for i in range(8)]

        block = st.enter_context(nc.Block())

        ring_i = [0]

        def ring_next():
            i = ring_i[0] % 3
            ring_i[0] += 1
            return i

        def load_w(slot, pieces, kc, width, cast=True, bf=False):
            if cast:
                view = ring[slot].bitcast(BF16)[:, 0:kc * width].rearrange("p (k c) -> p k c", k=kc)
            else:
                view = ring[slot][:, 0:kc * width].rearrange("p (k c) -> p k c", k=kc)
            q = "sp" if (bf or not cast) else "pool"
            for (ap, c0, w, off) in pieces:
                src = ap[:, c0:c0 + w].rearrange("(k p) c -> p k c", p=128)
                P.dma(q, f"ring{slot}", DMA(view[:, :, off:off + w], src), writes=[f"ring{slot}"])
            return view

        stq = ["sp"]

        P.dma("sp", "cst", DMA(ones_f[:], ones_in), writes=["ones_f"])
        P.dma("sp", "cst", DMA(cTt[:], cT_in), writes=["cTt"])
        P.dma("sp", "cst", DMA(bada[:], bada_in), writes=["bada"])
        P.dma("sp", "cst", DMA(npre[:], npre_in), writes=["npre"])
        P.dma("sp", "cst", DMA(npost[:], npost_in), writes=["npost"])
        P.dma("sp", "cst", DMA(hnorm[:], hnorm_in), writes=["hnorm"])
        P.dma("sp", "cst", DMA(fl[:].rearrange("p a b c -> p (a b c)"), flags_in.partition_broadcast(128)),
              writes=["fl"])
        def stage0():
            P.op("dve", CP(ones_b[:], ones_f[:]), reads=["ones_f"], writes=["ones_b"])
            P.op("act", ACT(scT[:], cTt[:], AF.Silu), reads=["cTt"], writes=["scT"])
            pm = pb[7]
            for l in layers:
                for blk in range(36):
                    slot = ring_next()
                    wv = load_w(slot, [(w_ada[l], blk * 256, 256, 0)], KC, 256, cast=False)
                    fns = []
                    for c in range(2):
                        j = blk * 2 + c
                        for k in range(KC):
                            fns.append(MM(pm[:, j:j + 1], wv[:, k, c * 128:(c + 1) * 128], scT[:, k:k + 1],
                                          start=(k == 0), stop=(k == KC - 1)))
                    P.pe_group(fns, reads=[f"ring{slot}", "scT"], writes=["pm"])
                P.op("dve", TT_(modT[:, l, :], pm[:, 0:72], bada[:, l, :], ALU.add), reads=["pm", "bada"],
                     writes=["modT"])
                for j in range(3):
                    P.op("dve", STT(A1[:, l, j, :], modT[:, l, (3 * j + 1) * 8:(3 * j + 2) * 8], 1.0, npre[:, l, j, :],
                                    ALU.add, ALU.mult), reads=["modT", "npre"], writes=["A1"])
                    P.op("dve", STT(G1[:, l, j, :], modT[:, l, (3 * j + 2) * 8:(3 * j + 3) * 8],
                                    (1.0 if j == 1 else 0.5), npost[:, l, j, :], ALU.mult, ALU.mult),
                         reads=["modT", "npost"], writes=["G1"])

        def B1(l, j, k):
            return modT[:, l, 3 * j * 8 + k:3 * j * 8 + k + 1]

        def stats_chunk(src_tile, srckey, k, n_chunks):
            b = k % 2
            P.op("act", ACT(sq[b][:], src_tile[:, k, :], AF.Square), reads=[f"{srckey}{k}"], writes=[f"sq{b}"])
            P.pe_group([MM(pb[4][:, :], ones_b[:], sq[b][:], start=(k == 0), stop=(k == n_chunks - 1))],
                       reads=[f"sq{b}", "ones_b"], writes=["pss"])

        def rstd_finish(inv_n):
            pss = pb[4]
            P.op("dve", TS(rstd[:], pss[:, :], inv_n, EPS, ALU.mult, ALU.add), reads=["pss"], writes=["rstd"])
            P.op("act", ACT(rstd[:], rstd[:], AF.Sqrt), reads=["rstd"], writes=["rstd"])
            P.op("dve", lambda e: e.reciprocal(rstd[:], rstd[:]), reads=["rstd"], writes=["rstd"])

        def norm_stats(src_tile, srckey, n_chunks, inv_n):
            for k in range(n_chunks):
                stats_chunk(src_tile, srckey, k, n_chunks)
            rstd_finish(inv_n)

        def norm_to_h(l, j, tok0, pos, have_stats=False):
            if have_stats:
                rstd_finish(1.0 / D)
            else:
                norm_stats(xT, "xT", KC, 1.0 / D)
            for k in range(KC):
                b = k % 2
                P.op("dve", TT_(tmp[b][:], xT[:, k, :], rstd[:], ALU.mult), reads=[f"xT{k}", "rstd"],
                     writes=[f"tmp{b}"])
                if pos:
                    P.dma("sp", f"posb{b}", DMA(posb[b][:], posT[k * 128:(k + 1) * 128, tok0:tok0 + TT]),
                          writes=[f"posb{b}"])
                    P.op("dve", STT(tmp[b][:], tmp[b][:], A1[:, l, j, k:k + 1], posb[b][:], ALU.mult, ALU.add),
                         reads=[f"tmp{b}", "A1", f"posb{b}"], writes=[f"tmp{b}"])
                    P.op("act", ACT(hT[:, k, :], tmp[b][:], AF.Identity, bias=B1(l, j, k)),
                         reads=[f"tmp{b}", "modT"], writes=[f"hT{k}"])
                else:
                    P.op("act", ACT(hT[:, k, :], tmp[b][:], AF.Identity, bias=B1(l, j, k),
                                    scale=A1[:, l, j, k:k + 1]),
                         reads=[f"tmp{b}", "modT", "A1"], writes=[f"hT{k}"])

        hkeys = [f"hT{k}" for k in range(KC)]

        def resid_update(l, j, stats_done=False, next_stats=False):
            if stats_done:
                rstd_finish(1.0 / D)
            else:
                norm_stats(yF, "yF", KC, 1.0 / D)
            for k in range(KC):
                b = k % 2
                P.op("dve", TT_(tmp[b][:], yF[:, k, :], rstd[:], ALU.mult), reads=[f"yF{k}", "rstd"],
                     writes=[f"tmp{b}"])
                P.op("dve", STT(xT[:, k, :], tmp[b][:], G1[:, l, j, k:k + 1], xT[:, k, :], ALU.mult, ALU.add),
                     reads=[f"tmp{b}", "G1", f"xT{k}"], writes=[f"xT{k}"])
                if next_stats:
                    stats_chunk(xT, "xT", k, KC)

        ffn_cnt = [0]

        def ffn(l, which, j, tok0, have_stats=False, next_stats=False):
            norm_to_h(l, j, tok0, pos=False, have_stats=have_stats)
            Wi, bfi = wsrc("fin", l, which)
            Wo, bfo = wsrc("fout", l, which)
            for blk in range(11):
                slot = ring_next()
                wv = load_w(slot, [(Wi, blk * 256, 256, 0), (Wi, FF + blk * 256, 256, 256)], KC, 512, bf=bfi)
                for c in range(2):
                    n = ffn_cnt[0] % 2
                    ffn_cnt[0] += 1
                    pg, pu = pb[0 + n], pb[2 + n]
                    fns = []
                    for k in range(KC):
                        fns.append(MM(pg[:, :], wv[:, k, c * 128:(c + 1) * 128], hT[:, k, :], start=(k == 0),
                                      stop=(k == KC - 1)))
                        fns.append(MM(pu[:, :], wv[:, k, 256 + c * 128:256 + (c + 1) * 128], hT[:, k, :],
                                      start=(k == 0), stop=(k == KC - 1)))
                    P.pe_group(fns, reads=[f"ring{slot}"] + hkeys, writes=[f"pb{n}", f"pb{2 + n}"])
                    f = blk * 2 + c
                    P.op("act", ACT(sgt[n][:], pg[:, :], AF.Silu), reads=[f"pb{n}"], writes=[f"sgt{n}"])
                    P.op("dve", TT_(actT[:, f, :], pu[:, :], sgt[n][:], ALU.mult), reads=[f"pb{2 + n}", f"sgt{n}"],
                         writes=[f"actT{f}"])
            akeys = [f"actT{f}" for f in range(FC)]
            for ob in range(4):
                slot = ring_next()
                wv = load_w(slot, [(Wo, ob * 256, 256, 0)], FC, 256, bf=bfo)
                for c in range(2):
                    dc = ob * 2 + c
                    n = dc % 2
                    py = pb[5 + n]
                    fns = [MM(py[:, :], wv[:, f, c * 128:(c + 1) * 128], actT[:, f, :], start=(f == 0),
                              stop=(f == FC - 1)) for f in range(FC)]
                    P.pe_group(fns, reads=[f"ring{slot}"] + akeys, writes=[f"pb{5 + n}"])
                    if dc >= 1:
                        stats_chunk(yF, "yF", dc - 1, KC)
                    P.op("act", ACT(yF[:, dc, :], py[:, :], AF.Copy), reads=[f"pb{5 + n}"], writes=[f"yF{dc}"])
            stats_chunk(yF, "yF", KC - 1, KC)
            resid_update(l, j, stats_done=True, next_stats=next_stats)

        st_cnt = {"f": 0, "b": 0, "pj": 0}

        def proj_stage(l, tok0):
            Wl, bfw = wsrc("in", l)
            blocks = [
                (0, 512, [("fm", 0, 256, "qkt", 0, None, 0.125), ("fm", 256, 256, "qkt", 256, None, 1.0),
                          ("tm", 256, 256, "ktok", 0, None)]),
                (C_MV, 512, [("tm", 0, 512, "v", 0, None)]),
                (C_MO, 528, [("fm", 0, 512, "gates", 0, AF.Sigmoid, 1.0), ("fm", 512, 16, "mgt", 0, None, 1.0),
                             ("tm", 512, 16, "mgtok", 0, None)]),
                (C_GQ, 512, [("fm", 0, 256, "qkt", 512, None, 0.125), ("fm", 256, 256, "qkt", 768, None, 1.0),
                             ("tm", 256, 256, "ktok", 256, None)]),
                (C_GV, 512, [("tm", 0, 512, "v", 512, None)]),
                (C_GR, 544, [("fm", 0, 512, "gates", 512, AF.Silu, 1.0), ("fm", 512, 32, "glrt", 0, None, 1.0)]),
                (C_HQ, 256, [("fm", 0, 256, "qkt", 1024, None, 1.0)]),
                (C_HF, 512, [("fm", 0, 512, "hft", 0, None, 1.0), ("tm", 0, 512, "hftok", 0, None)]),
                (C_HV, 512, [("tm", 0, 512, "v", 1024, AF.Silu)]),
                (C_HG, 512, [("fm", 0, 512, "gates", 1024, AF.Silu, 1.0)]),
            ] + [(C_MG + i * 512, 512, [("fm", 0, 512, "gates", 1536 + i * 512, AF.Sigmoid, 1.0)]) for i in range(6)]
            for (c0, width, jobs) in blocks:
                slot = ring_next()
                wv = load_w(slot, [(Wl, c0, width, 0)], KC, width, bf=bfw)
                for job in jobs:
                    if job[0] == "fm":
                        _, off, ncols, dname, row0, func, scale = job
                        dst = dests[dname]
                        isb = (dst.dtype == BF16)
                        for cc in range(0, ncols, 128):
                            m = min(128, ncols - cc)
                            n = st_cnt["pj"] % 2
                            st_cnt["pj"] += 1
                            pp = pb[5 + n]
                            fns = [MM(pp[0:m, :], wv[:, k, off + cc:off + cc + m], hT[:, k, :], start=(k == 0),
                                      stop=(k == KC - 1)) for k in range(KC)]
                            P.pe_group(fns, reads=[f"ring{slot}"] + hkeys, writes=[f"pb{5 + n}"])
                            if isb:
                                si = st_cnt["b"] % 4
                                st_cnt["b"] += 1
                                stg, skey = stb[si], f"stb{si}"
                            else:
                                si = st_cnt["f"] % 4
                                st_cnt["f"] += 1
                                stg, skey = stf[si], f"stf{si}"
                            if func is not None:
                                P.op("act", ACT(stg[0:m, :], pp[0:m, :], func), reads=[f"pb{5 + n}"], writes=[skey])
                            elif scale != 1.0:
                                P.op("act", ACT(stg[0:m, :], pp[0:m, :], AF.Copy, scale=scale), reads=[f"pb{5 + n}"],
                                     writes=[skey])
                            else:
                                P.op("dve", CP(stg[0:m, :], pp[0:m, :]), reads=[f"pb{5 + n}"], writes=[skey])
                            P.dma(stq[0], skey, DMA(dst[row0 + cc:row0 + cc + m, tok0:tok0 + TT], stg[0:m, :]),
                                  reads=[skey], writes=[f"d_{dname}"])
                    else:
                        _, off, ncols, dname, col0, func = job
                        dst = dests[dname]
                        isb = (dst.dtype == BF16)
                        for jj in range(TT // 128):
                            n = st_cnt["pj"] % 2
                            st_cnt["pj"] += 1
                            pp = pb[5 + n]
                            fns = [MM(pp[:, 0:ncols], hT[:, k, jj * 128:(jj + 1) * 128], wv[:, k, off:off + ncols],
                                      start=(k == 0), stop=(k == KC - 1)) for k in range(KC)]
                            P.pe_group(fns, reads=[f"ring{slot}"] + hkeys, writes=[f"pb{5 + n}"])
                            if isb:
                                si = st_cnt["b"] % 4
                                st_cnt["b"] += 1
                                stg, skey = stb[si], f"stb{si}"
                            else:
                                si = st_cnt["f"] % 4
                                st_cnt["f"] += 1
                                stg, skey = stf[si], f"stf{si}"
                            if func is not None:
                                P.op("act", ACT(stg[:, 0:ncols], pp[:, 0:ncols], func), reads=[f"pb{5 + n}"],
                                     writes=[skey])
                            else:
                                P.op("dve", CP(stg[:, 0:ncols], pp[:, 0:ncols]), reads=[f"pb{5 + n}"], writes=[skey])
                            t0 = tok0 + jj * 128
                            P.dma(stq[0], skey, DMA(dst[t0:t0 + 128, col0:col0 + ncols], stg[:, 0:ncols]),
                                  reads=[skey], writes=[f"d_{dname}"])

        def load_x(src, tok0):
            P.dma("sp", "xT", DMA(xT[:], src[:, tok0:tok0 + TT].rearrange("(k p) t -> p k t", p=128)),
                  reads=[], writes=[f"xT{k}" for k in range(KC)])

        def store_x(dst, tok0, key):
            P.dma(stq[0], "xTst", DMA(dst[:, tok0:tok0 + TT].rearrange("(k p) t -> p k t", p=128), xT[:]),
                  reads=[f"xT{k}" for k in range(KC)], writes=[key])

        def stage_A(l):
            src = xT_in if l == 0 else X3
            for tt in range(NT):
                tok0 = tt * TT
                load_x(src, tok0)
                ffn(l, 0, 0, tok0, next_stats=True)
                store_x(X1, tok0, "d_x1")
                norm_to_h(l, 1, tok0, pos=True, have_stats=True)
                proj_stage(l, tok0)

        def make_set(si):
            regions = [[arena if si == 0 else arena2, 0, 17408 * 4], [arena3, si * 15360, (si + 1) * 15360]]

            def cv(nbytes, parts, free, dt):
                for r in regions:
                    if r[1] + nbytes <= r[2]:
                        v = carve(r[1], parts, free, dt, r[0])
                        r[1] += nbytes
                        return v
                raise AssertionError("scan set does not fit")
            d_ = {}
            d_["qTt"] = [cv(2048, 64, [4, 128], F32) for _ in range(2)]
            d_["kin"] = [cv(2048, 64, [4, 128], F32) for _ in range(2)]
            d_["ktok"] = [cv(4096, 64, [4, 256], F32) for _ in range(2)]
            d_["vtk"] = [cv(4096, 64, [4, 512], BF16) for _ in range(2)]
            d_["V2"] = [cv(4096, 64, [8, 256], BF16) for _ in range(2)]
            d_["lat"] = cv(4096, 64, [4, 256], F32)
            d_["Ktk"] = [cv(2048, 64, [4, 256], BF16) for _ in range(2)]
            d_["ATm"] = [cv(2048, 64, [16, 64], BF16) for _ in range(2)]
            d_["Sint"] = cv(4096, 64, [4, 256], F32)
            d_["Stm"] = cv(4096, 64, [4, 256], F32)
            d_["Sini"] = cv(4096, 64, [4, 256], F32)
            d_["Sst"] = cv(4096, 64, [4, 256], F32)
            d_["Sbf"] = cv(2048, 64, [4, 256], BF16)
            d_["oev"] = [cv(2048, 128, [4, 128], F32) for _ in range(2)]
            d_["dn"] = cv(2048, 128, [4, 128], F32)
            d_["kTc"] = cv(2048, 64, [4, 128], F32)
            d_["GcQ"] = cv(2048, 64, [4, 128], F32)
            d_["GcK"] = cv(2048, 64, [4, 128], F32)
            d_["QTs"] = [cv(1024, 64, [4, 128], BF16) for _ in range(2)]
            d_["KTs"] = cv(1024, 64, [4, 128], BF16)
            d_["KTz"] = cv(1024, 64, [4, 128], BF16)
            d_["sgk"] = cv(4096, 64, [4, 256], F32)
            d_["mgk"] = [sb(f"mgk{si}_{i}", [64, 2, 16]) for i in range(2)]
            d_["spm"] = sb(f"spm{si}", [64, 2, 4])
            d_["eli"] = sb(f"eli{si}", [64, 2, 4])
            d_["glrTt"] = [sb(f"glrTt{si}_{i}", [16, 128]) for i in range(2)]
            d_["ABt"] = [sb(f"ABt{si}_{i}", [64, 4, 4, 2]) for i in range(2)]
            d_["emi4"] = sb(f"emi4{si}", [64, 4])
            d_["Eo4"] = sb(f"Eo4{si}", [64, NSEG, 4])
            d_["liT"] = sb(f"liT{si}", [4, SEGLEN])
            d_["xfT"] = sb(f"xfT{si}", [4, SEGLEN])
            d_["bT_"] = sb(f"bT_{si}", [4, SEGLEN])
            d_["mrow"] = sb(f"mrow{si}", [4, 8])
            d_["mcur"] = sb(f"mcur{si}", [4, 1])
            return d_

        SETS = [make_set(0), make_set(1)]
        wup = sb("wup", [16, 2, 256])
        gbb = sb("gbb", [64, 2, 2, 256])
        lbb = sb("lbb", [64, 2, 256])
        hgb = sb("hgb", [64, 2, 256])
        lbT = sb("lbT", [64, 4, 2])
        hgT = sb("hgT", [64, 2, 4])
        mgbb = sb("mgbb", [64, 32])
        onesr = sb("onesr", [4, SEGLEN])
        mgbT = sb("mgbT", [4, 8])
        mm0T = sb("mm0T", [4, 4])
        cst = {}
        for mix, L in MIXL.items():
            for z in (0, 1):
                cst[(mix, z)] = (sb(f"wcat{mix}{z}", [L, L + 2]), sb(f"wm{mix}{z}", [L, L]),
                                 sb(f"mlo{mix}{z}", [L, L]), sb(f"mhi{mix}{z}", [L, L]), sb(f"kz{mix}{z}", [64, 128]))
        ps_open = [False] * 8
        ps_ptr = [0]

        def acquire(n):
            while True:
                free = [(ps_ptr[0] + i) % 8 for i in range(8) if not ps_open[(ps_ptr[0] + i) % 8]]
                if len(free) >= n:
                    sel = free[:n]
                    for i in sel:
                        ps_open[i] = True
                    ps_ptr[0] = (sel[-1] + 1) % 8
                    return [(pb[i], f"pb{i}", i) for i in sel]
                yield

        def release(*idx):
            for i in idx:
                ps_open[i] = False

        def stage_S_consts():
            for key, tiles in cst.items():
                for i in range(5):
                    P.dma("sp", "cst", DMA(tiles[i][:], consts_in[key][i]), writes=[f"cst{key}{i}"])
            P.dma("sp", "cst", DMA(mgbb[:], mgb_in.partition_broadcast(64)), writes=["mgbb"])
            P.dma("sp", "cst", DMA(gbb[:].rearrange("p a b c -> p (a b c)"), gb_in.partition_broadcast(64)),
                  writes=["gbb"])
            P.dma("sp", "cst", DMA(hgb[:].rearrange("p a c -> p (a c)"), hgam_in.partition_broadcast(64)),
                  writes=["hgb"])
            P.dma("sp", "cst", DMA(hgT[:], hgamT_in), writes=["hgT"])
            P.op("pool", lambda e: e.memset(onesr[:], 1.0), writes=["onesr"])
            P.dma("sp", "cst", DMA(mgbT[:], mgbT_in), writes=["mgbT"])
            P.dma("sp", "cst", DMA(mm0T[:], mM0T_in), writes=["mm0T"])
            for z in (0, 1):
                P.dma("sp", "cst", DMA(wup[:, z, :], gwup_in[0, z]), writes=["wup"])

        def hgrn_lb(l):
            if l == 0:
                P.op("dve", lambda e: e.memset(lbb[:, 0, :], 0.0), writes=["lbb"])
                P.op("dve", lambda e: e.memset(lbb[:, 1, :], 1.0), writes=["lbb"])
                P.op("dve", lambda e: e.memset(lbT[:, :, 0:1], -1.0), writes=["lbT"])
                P.op("dve", lambda e: e.memset(lbT[:, :, 1:2], 1.0), writes=["lbT"])
            else:
                P.op("dve", TT_(lbb[:, 0, :], hgb[:, 1, :], hgb[:, 0, :], ALU.subtract), reads=["hgb"], writes=["lbb"])
                P.op("act", ACT(lbb[:, 0, :], lbb[:, 0, :], AF.Sigmoid), reads=["lbb"], writes=["lbb"])
                P.op("dve", TS(lbb[:, 1, :], lbb[:, 0, :], -1.0, 1.0, ALU.mult, ALU.add), reads=["lbb"], writes=["lbb"])
                P.op("dve", TT_(lbT[:, :, 1], hgT[:, 0, :], hgT[:, 1, :], ALU.subtract), reads=["hgT"], writes=["lbT"])
                P.op("act", ACT(lbT[:, :, 1], lbT[:, :, 1], AF.Sigmoid), reads=["lbT"], writes=["lbT"])
                P.op("dve", TS(lbT[:, :, 0], lbT[:, :, 1], -1.0, None, ALU.mult), reads=["lbT"], writes=["lbT"])
                for z in (0, 1):
                    P.dma("sp", "wup", DMA(wup[:, z, :], gwup_in[l, z]), writes=["wup"])

        def mlstm_m_pass(l, z, si):
            B = SETS[si]
            liT, xfT, bT_, mrow, mcur, Eo4 = B["liT"], B["xfT"], B["bT_"], B["mrow"], B["mcur"], B["Eo4"]
            K = lambda n: f"{n}_{si}"
            order = range(NSEG) if z == 0 else range(NSEG - 1, -1, -1)
            bi = mgbT[:, (l * 2 + z) * 2 + 0:(l * 2 + z) * 2 + 1]
            bf = mgbT[:, (l * 2 + z) * 2 + 1:(l * 2 + z) * 2 + 2]
            P.op("dve", lambda e: e.memset(mcur[:], 0.0), writes=[K("mcur")])
            for seg in order:
                t0 = seg * SEGLEN
                P.dma("sp", K("liT"), DMA(liT[:], MGT[z * 8:z * 8 + 4, t0:t0 + SEGLEN]), writes=[K("liT")])
                P.dma("sp", K("xfT"), DMA(xfT[:], MGT[z * 8 + 4:z * 8 + 8, t0:t0 + SEGLEN]), writes=[K("xfT")])
                yield
                P.op("act", ACT(xfT[:], xfT[:], AF.Identity, bias=bf), reads=[K("xfT"), "mgbT"], writes=[K("xfT")])
                yield
                P.op("act", ACT(xfT[:], xfT[:], AF.Exp, scale=-1.0), reads=[K("xfT")], writes=[K("xfT")])
                yield
                P.op("act", ACT(xfT[:], xfT[:], AF.Ln, bias=1.0), reads=[K("xfT")], writes=[K("xfT")])
                yield
                for c in range(4):
                    cs = slice(c * 64, (c + 1) * 64)
                    P.op("dve", lambda e, cs=cs: e.tensor_tensor_scan(bT_[:, cs], onesr[:, cs], xfT[:, cs], 0.0,
                                                                      ALU.mult, ALU.add),
                         reads=[K("xfT"), "onesr"], writes=[K("bT_")])
                yield
                if z == 0:
                    P.op("dve", STT(liT[:], liT[:], bi, bT_[:], ALU.add, ALU.add), reads=[K("liT"), K("bT_"), "mgbT"],
                         writes=[K("liT")])
                else:
                    for c in range(4):
                        cs = slice(c * 64, (c + 1) * 64)
                        P.op("dve", STT(liT[:, cs], liT[:, cs], bT_[:, c * 64 + 63:c * 64 + 64], bT_[:, cs],
                                        ALU.add, ALU.subtract), reads=[K("liT"), K("bT_")], writes=[K("liT")])
                    yield
                    P.op("dve", STT(liT[:], liT[:], bi, xfT[:], ALU.add, ALU.add), reads=[K("liT"), K("xfT"), "mgbT"],
                         writes=[K("liT")])
                yield
                P.op("dve", lambda e: e.tensor_reduce(mrow[:, 0:4], liT[:].rearrange("p (c t) -> p c t", c=4), AX.X,
                                                      ALU.max), reads=[K("liT")], writes=[K("mrow")])
                P.op("dve", TS(mrow[:, 4:8], bT_[:].rearrange("p (c t) -> p c t", c=4)[:, :, 63], -1.0, None, ALU.mult),
                     reads=[K("bT_")], writes=[K("mrow")])
                yield
                P.op("dve", TS(mcur[:], mcur[:], fl[0:4, z, seg, 0:1], None, ALU.mult), reads=[K("mcur"), "fl"],
                     writes=[K("mcur")])
                yield
                P.op("dve", STT(mcur[:], mm0T[:, l * 2 + z:l * 2 + z + 1], fl[0:4, z, seg, 1:2], mcur[:], ALU.mult,
                                ALU.add), reads=[K("mcur"), "fl", "mm0T"], writes=[K("mcur")])
                yield
                corder = range(4) if z == 0 else range(3, -1, -1)
                for c in corder:
                    P.op("dve", STT(mcur[:], mcur[:], mrow[:, c:c + 1], mrow[:, 4 + c:5 + c], ALU.max, ALU.add),
                         reads=[K("mcur"), K("mrow")], writes=[K("mcur")])
                    yield
                P.dma("sp", K("mst"), DMA(o_mM[l, seg, z, :].rearrange("(h o) -> h o", o=1), mcur[:]),
                      reads=[K("mcur")], writes=[f"d_mm{z}_{seg}"])
                yield
                P.dma("sp", K("Eo4"), DMA(Eo4[:, seg, :], o_mM[l, seg:seg + 1, z, :].partition_broadcast(64)),
                      reads=[f"d_mm{z}_{seg}"], writes=[K("Eo4")])
                yield
            P.op("act", ACT(Eo4[:], Eo4[:], AF.Exp, scale=-1.0), reads=[K("Eo4")], writes=[K("Eo4")])
            yield

        def state_dram(ap5, l, z):
            return ap5[l, z].rearrange("h d e -> d h e")

        def scan_chain(l, mix, z, si):
            B = SETS[si]
            K = lambda n: f"{n}_{si}"
            lat, sgk = B["lat"], B["sgk"]
            Gtk = lat
            Sint, Stm, Sini, Sst, Sbf, oev, dn = (B[n] for n in ("Sint", "Stm", "Sini", "Sst", "Sbf", "oev", "dn"))
            GcQ, GcK, KTs, KTz, kTc = (B[n] for n in ("GcQ", "GcK", "KTs", "KTz", "kTc"))
            spm, eli, emi4, Eo4 = (B[n] for n in ("spm", "eli", "emi4", "Eo4"))
            pre_done = [0]
            seq_done = [0]
            L = MIXL[mix]
            nch = 128 // L
            E = 256 if mix == "m" else 128
            wcat, wm, mlo, mhi, kzm = cst[(mix, z)]
            ck = [f"cst{(mix, z)}{i}" for i in range(5)]
            qrow = {"m": 0, "g": 512, "h": 1024}[mix]
            krow = {"m": 256, "g": 768}.get(mix)
            kcol = {"m": 0, "g": 256}.get(mix)
            vcol = {"m": 0, "g": 512, "h": 1024}[mix]
            SE = lambda t_: t_[:, :, 0:E]
            order = list(range(T // 128)) if z == 0 else list(range(T // 128 - 1, -1, -1))
            if DBG_NSUB is not None:
                order = order[:DBG_NSUB]
            def pre():
                for ji, j in enumerate(order):
                    while seq_done[0] < ji - 1:
                        yield
                    tok0 = j * 128
                    seg = j // 2
                    jb = ji % 2
                    qTt, kTt, hfTt, ktok, vtk = B["qTt"][jb], B["kin"][jb], B["kin"][jb], B["ktok"][jb], B["vtk"][jb]
                    mgk, glrTt = B["mgk"][jb], B["glrTt"][jb]
                    V2, Ktk, ATm, QTs, ABt = B["V2"][jb], B["Ktk"][jb], B["ATm"][jb], B["QTs"][jb], B["ABt"][jb]
                    KV2, KKtk, KATm, KQTs, KABt = K(f"V2{jb}"), K(f"Ktk{jb}"), K(f"ATm{jb}"), K(f"QTs{jb}"), K(f"ABt{jb}")
                    KQ, KK, KT, KV = K(f"qTt{jb}"), K(f"kin{jb}"), K(f"ktok{jb}"), K(f"vtk{jb}")
                    P.dma("sp", KQ, DMA(qTt[:], QKT[qrow:qrow + 256, tok0:tok0 + 128].rearrange("(h d) t -> d h t", d=64)),
                          writes=[KQ])
                    if mix != "h":
                        P.dma("sp", KK, DMA(kTt[:], QKT[krow:krow + 256, tok0:tok0 + 128].rearrange(
                            "(h d) t -> d h t", d=64)), writes=[KK])
                        P.dma("sp", KT, DMA(ktok[0:L, 0:nch, :],
                                            KTOK[tok0:tok0 + 128, kcol:kcol + 256].rearrange("(c l) f -> l c f", l=L)),
                              writes=[KT])
                    else:
                        P.dma("sp", KK, DMA(hfTt[:], HFT[z * 256:(z + 1) * 256, tok0:tok0 + 128].rearrange(
                            "(h d) t -> d h t", d=64)), writes=[KK])
                        P.dma("sp", KT, DMA(ktok[0:L, 0:nch, :],
                                            HFTOK[tok0:tok0 + 128, z * 256:(z + 1) * 256].rearrange(
                                                "(c l) f -> l c f", l=L)), writes=[KT])
                    P.dma("sp", KV, DMA(vtk[0:L, 0:nch, :],
                                        VT[tok0:tok0 + 128, vcol:vcol + 512].rearrange("(c l) f -> l c f", l=L)),
                          writes=[KV])
                    yield
                    latv = lat[0:L, 0:nch, :]
                    if mix == "g":
                        P.dma("sp", K(f"glrTt{jb}"), DMA(glrTt[:], GLRT[z * 16:(z + 1) * 16, tok0:tok0 + 128]),
                              writes=[K(f"glrTt{jb}")])
                        ((bank, bk, bi_),) = yield from acquire(1)
                        pla = bank[0:L, :].rearrange("p (c f) -> p c f", c=nch)
                        P.pe_group([MM(pla[:, c, :], glrTt[0:16, c * L:(c + 1) * L], wup[:, z, :]) for c in range(nch)],
                                   reads=[K(f"glrTt{jb}"), "wup"], writes=[bk])
                        yield
                        P.op("dve", TT_(latv, pla, gbb[0:L, l, z, :].unsqueeze(1).to_broadcast([L, nch, 256]), ALU.add),
                             reads=[bk, "gbb"], writes=[K("lat")])
                        release(bi_)
                        yield
                        P.op("act", ACT(latv, latv, AF.Exp, scale=-1.0), reads=[K("lat")], writes=[K("lat")])
                        yield
                        P.op("act", ACT(latv, latv, AF.Ln, bias=1.0), reads=[K("lat")], writes=[K("lat")])
                        yield
                        vsrc = vtk[0:L, 0:nch, :].rearrange("p c (h e) -> p (c h) e", h=4)
                        ksrc_tok = ktok[0:L, 0:nch, :]
                        ksrc_T = kTt
                        vkeys = [KV]
                        ktk_reads = [KT]
                        kT_reads = [KK]
                    elif mix == "h":
                        sgv = sgk[0:L, 0:nch, :]
                        hfin = ktok[0:L, 0:nch, :]
                        lb_b = lbb[0:L, 0, :].unsqueeze(1).to_broadcast([L, nch, 256])
                        oml_b = lbb[0:L, 1, :].unsqueeze(1).to_broadcast([L, nch, 256])
                        P.op("act", ACT(sgv, hfin, AF.Sigmoid), reads=[KT], writes=[K("sgk")])
                        P.op("act", ACT(kTc[:], hfTt[:], AF.Sigmoid), reads=[KK], writes=[K("kTc")])
                        yield
                        P.op("dve", TT_(latv, sgv, oml_b, ALU.mult), reads=[K("sgk"), "lbb"], writes=[K("lat")])
                        P.op("pool", TT_(kTc[:], kTc[:], lbT[:, :, 0:1].to_broadcast([64, 4, 128]), ALU.mult),
                             reads=[K("kTc"), "lbT"], writes=[K("kTc")])
                        yield
                        P.op("dve", TT_(latv, latv, lb_b, ALU.add), reads=[K("lat"), "lbb"], writes=[K("lat")])
                        P.op("pool", TT_(kTc[:], kTc[:], lbT[:, :, 1:2].to_broadcast([64, 4, 128]), ALU.add),
                             reads=[K("kTc"), "lbT"], writes=[K("kTc")])
                        yield
                        P.op("act", ACT(latv, latv, AF.Ln), reads=[K("lat")], writes=[K("lat")])
                        P.op("pool", TS(sgv, sgv, -1.0, 1.0, ALU.mult, ALU.add), reads=[K("sgk"), K("lat")],
                             writes=[K("sgk")])
                        yield
                        P.op("pool", TT_(sgv, sgv, oml_b, ALU.mult), reads=[K("sgk"), "lbb"], writes=[K("sgk")])
                        yield
                        vsrc = vtk[0:L, 0:nch, :].rearrange("p c (h e) -> p (c h) e", h=4)
                        ksrc_tok = sgv
                        ksrc_T = kTc
                        vkeys = [KV]
                        ktk_reads = [K("sgk")]
                        kT_reads = [K("kTc")]
                    else:
                        P.dma("sp", K(f"mgk{jb}"), DMA(mgk[0:L, 0:nch, :],
                                                       MGTOK[tok0:tok0 + 128, :].rearrange("(c l) f -> l c f", l=L)),
                              writes=[K(f"mgk{jb}")])
                        yield
                        P.op("dve", TT_(mgk[:], mgk[:], mgbb[0:L, l * 16:(l + 1) * 16].unsqueeze(1).to_broadcast(
                            [L, nch, 16]), ALU.add), reads=[K(f"mgk{jb}"), "mgbb"], writes=[K(f"mgk{jb}")])
                        yield
                        P.op("act", ACT(spm[:], mgk[:, :, z * 8 + 4:z * 8 + 8], AF.Exp, scale=-1.0), reads=[K(f"mgk{jb}")],
                             writes=[K("spm")])
                        P.op("act", ACT(eli[:], mgk[:, :, z * 8:z * 8 + 4], AF.Exp), reads=[K(f"mgk{jb}")], writes=[K("eli")])
                        yield
                        P.op("act", ACT(spm[:], spm[:], AF.Ln, bias=1.0), reads=[K("spm")], writes=[K("spm")])
                        elb = eli[:].rearrange("p c h -> p (c h)").unsqueeze(2).to_broadcast([L, nch * 4, 128])
                        P.op("pool", TT_(V2[0:L, 0:nch * 4, 0:128],
                                         vtk[0:L, 0:nch, :].rearrange("p c (h e) -> p (c h) e", h=4), elb, ALU.mult),
                             reads=[KV, K("eli")], writes=[KV2])
                        yield
                        P.op("dve", CP(lat[0:L, 0:nch, :].rearrange("p c (h d) -> p (c h) d", h=4),
                                       spm[:].rearrange("p c h -> p (c h)").unsqueeze(2).to_broadcast([L, nch * 4, 64])),
                             reads=[K("spm")], writes=[K("lat")])
                        P.op("pool", CP(V2[0:L, 0:nch * 4, 128:256], elb), reads=[K("eli")], writes=[KV2])
                        yield
                        vsrc = V2[0:L, 0:nch * 4, :]
                        ksrc_tok = ktok[0:L, 0:nch, :]
                        ksrc_T = kTt
                        vkeys = [KV2]
                        ktk_reads = [KT]
                        kT_reads = [KK]
                    nhalf = max(1, nch // 2)
                    got = yield from acquire(2 + nhalf)
                    (bank, bk, gi_), (bank2, bk2, ai_) = got[0], got[1]
                    GTp = bank[0:64, 0:4 * nch * L].rearrange("p (h c w) -> p h c w", h=4, c=nch)
                    fns = []
                    for h in range(4):
                        for c in range(nch):
                            fns.append(MM(GTp[:, h, c, :], lat[0:L, c, h * 64:(h + 1) * 64], wcat[:, 0:L]))
                    P.pe_group(fns, reads=[K("lat"), ck[0]], writes=[bk])
                    ABp = bank2[0:64, 0:4 * nch * 2].rearrange("p (h c w) -> p h c w", h=4, c=nch)
                    fns = []
                    for h in range(4):
                        for c in range(nch):
                            fns.append(MM(ABp[:, h, c, :], lat[0:L, c, h * 64:(h + 1) * 64], wcat[:, L:L + 2]))
                    P.pe_group(fns, reads=[K("lat"), ck[0]], writes=[bk2])
                    gtoks = []
                    for half in range(nhalf):
                        ncc = min(2, nch)
                        bank3, bk3, ti_ = got[2 + half]
                        pgk = bank3[0:L, 0:ncc * 256]
                        P.pe_group([MM(pgk, wm[:, :], lat[0:L, half * 2:half * 2 + ncc, :].rearrange("p c f -> p (c f)"))],
                                   reads=[K("lat"), ck[1]], writes=[bk3])
                        gtoks.append((pgk, half * 2, ncc, bk3, ti_))
                    yield
                    GQ = GcQ[:].rearrange("p h (c t) -> p h c t", c=nch)
                    GK = GcK[:].rearrange("p h (c t) -> p h c t", c=nch)
                    P.op("act", ACT(GQ, GTp, AF.Exp), reads=[bk], writes=[K("GcQ")])
                    yield
                    P.op("act", ACT(GK, GTp, AF.Exp, scale=-1.0), reads=[bk], writes=[K("GcK")])
                    release(gi_)
                    P.op("dve", TT_(QTs[:], qTt[:], GcQ[:], ALU.mult), reads=[KQ, K("GcQ")], writes=[KQTs])
                    yield
                    ABv = ABt[:, :, 0:nch, :]
                    P.op("act", ACT(ABv, ABp, AF.Exp), reads=[bk2], writes=[KABt])
                    release(ai_)
                    P.op("dve", TT_(KTs[:], ksrc_T[:], GcK[:], ALU.mult), reads=kT_reads + [K("GcK")], writes=[K("KTs")])
                    yield
                    for (pgk, c0, ncc, bk3, ti_) in gtoks:
                        P.op("act", ACT(Gtk[0:L, c0:c0 + ncc, :].rearrange("p c f -> p (c f)"), pgk, AF.Exp, scale=-1.0),
                             reads=[bk3], writes=[K("lat")])
                        release(ti_)
                    P.op("pool", TT_(KTz[:], KTs[:], kzm[:, :].unsqueeze(1).to_broadcast([64, 4, 128]), ALU.mult),
                         reads=[K("KTs"), ck[4]], writes=[K("KTz")])
                    yield
                    P.op("pool", TT_(Ktk[0:L, 0:nch, :], ksrc_tok, Gtk[0:L, 0:nch, :], ALU.mult),
                         reads=ktk_reads + [K("lat")], writes=[KKtk])
                    ((bank4, bk4, pi_),) = yield from acquire(1)
                    ATp = bank4[0:L, 0:nch * 4 * L].rearrange("p (a t) -> p a t", t=L)
                    fns = []
                    H2 = L // 2
                    for c in range(nch):
                        for h in range(4):
                            full_t = slice(c * L + H2, (c + 1) * L) if z == 0 else slice(c * L, c * L + H2)
                            zero_t = slice(c * L, c * L + H2) if z == 0 else slice(c * L + H2, (c + 1) * L)
                            fo = (H2, L) if z == 0 else (0, H2)
                            zo = (0, H2) if z == 0 else (H2, L)
                            fns.append(MM(ATp[:, c * 4 + h, fo[0]:fo[1]], KTs[:, h, c * L:(c + 1) * L], QTs[:, h, full_t]))
                            fns.append(MM(ATp[:, c * 4 + h, zo[0]:zo[1]], KTz[:, h, c * L:(c + 1) * L], QTs[:, h, zero_t]))
                    P.pe_group(fns, reads=[K("KTs"), K("KTz"), KQTs], writes=[bk4])
                    yield
                    ATmv = ATm[0:L, 0:nch * 4, 0:L]
                    P.op("dve", TT_(ATmv, ATp, mhi[:, :].unsqueeze(1).to_broadcast([L, nch * 4, L]), ALU.mult),
                         reads=[bk4, ck[3]], writes=[KATm])
                    release(pi_)
                    yield

                    pre_done[0] = ji + 1
                    yield

            def seq():
                P.op("dve", lambda e: e.memset(Sst[:], 0.0), writes=[K("S")])
                if mix == "m":
                    P.dma("sp", K("sini"), DMA(Sini[:, :, 0:128], state_dram(mC0, l, z)), writes=[K("Sini")])
                    P.dma("sp", K("sini"), DMA(Sini[:, :, 128:256], state_dram(mN0, l, z)), writes=[K("Sini")])
                    P.dma("sp", K("emi4"), DMA(emi4[:], mM0[0:1, (l * 2 + z) * 4:(l * 2 + z) * 4 + 4].partition_broadcast(
                        64)), writes=[K("emi4")])
                    P.op("act", ACT(emi4[:], emi4[:], AF.Exp), reads=[K("emi4")], writes=[K("emi4")])
                    P.op("dve", TT_(Sini[:], Sini[:], emi4[:].unsqueeze(2).to_broadcast([64, 4, 256]), ALU.mult),
                         reads=[K("Sini"), K("emi4")], writes=[K("Sini")])
                else:
                    P.dma("sp", K("sini"), DMA(Sini[:, :, 0:128], state_dram(gS0 if mix == "g" else hS0, l, z)),
                          writes=[K("Sini")])
                yield

                for ji, j in enumerate(order):
                    while pre_done[0] <= ji:
                        yield
                    tok0 = j * 128
                    seg = j // 2
                    jb = ji % 2
                    V2, Ktk, ATm, QTs, ABt = B["V2"][jb], B["Ktk"][jb], B["ATm"][jb], B["QTs"][jb], B["ABt"][jb]
                    KV2, KKtk, KATm, KQTs, KABt = K(f"V2{jb}"), K(f"Ktk{jb}"), K(f"ATm{jb}"), K(f"QTs{jb}"), K(f"ABt{jb}")
                    vtk = B["vtk"][jb]
                    if mix == "m":
                        vsrc = V2[0:L, 0:nch * 4, :]
                        vkeys = [KV2]
                    else:
                        vsrc = vtk[0:L, 0:nch, :].rearrange("p c (h e) -> p (c h) e", h=4)
                        vkeys = [K(f"vtk{jb}")]
                    ATmv = ATm[0:L, 0:nch * 4, 0:L]
                    ob = oev[j % 2]
                    okey = K(f"oev{j % 2}")
                    corder = list(range(nch)) if z == 0 else list(range(nch - 1, -1, -1))
                    for ci, c in enumerate(corder):
                        first_of_seg = (ci == 0) and ((j % 2 == 0) if z == 0 else (j % 2 == 1))
                        last_of_seg = (ci == nch - 1) and ((j % 2 == 1) if z == 0 else (j % 2 == 0))
                        if first_of_seg:
                            P.op("dve", TS(SE(Sst), SE(Sst), fl[0:64, z, seg, 0:1], None, ALU.mult),
                                 reads=[K("S"), "fl"], writes=[K("S")])
                            yield
                            P.op("dve", STT(SE(Sst), SE(Sini), fl[0:64, z, seg, 1:2], SE(Sst), ALU.mult,
                                            ALU.add), reads=[K("S"), "fl", K("Sini")], writes=[K("S")])
                            yield
                        P.op("dve", TT_(SE(Sint), SE(Sst), ABt[:, :, c, 0:1].to_broadcast([64, 4, E]), ALU.mult),
                             reads=[K("S"), KABt], writes=[K("Sint")])
                        pbanks = []
                        fns = []
                        if E == 256:
                            (b1, k1, i1_), (b2, k2, i2_), (bo, bok, io_) = yield from acquire(3)
                            Pp = [b1[0:64, :].rearrange("p (a e) -> p a e", a=2), b2[0:64, :].rearrange("p (a e) -> p a e", a=2)]
                            ppk = [k1, k2]
                            pidx = [i1_, i2_]
                        else:
                            (b1, k1, i1_), (bo, bok, io_) = yield from acquire(2)
                            Pp = [b1[0:64, 0:256].rearrange("p (a e) -> p a e", a=2),
                                  b1[0:64, 256:512].rearrange("p (a e) -> p a e", a=2)]
                            ppk = [k1, k1]
                            pidx = [i1_]
                        for h in range(4):
                            fns.append(MM(Pp[h // 2][:, h % 2, :], Ktk[0:L, c, h * 64:(h + 1) * 64], vsrc[:, c * 4 + h, 0:E]))
                        P.pe_group(fns, reads=vkeys + [KKtk], writes=list(set(ppk)))
                        yield
                        P.op("act", ACT(SE(Sbf), SE(Sint), AF.Copy), reads=[K("Sint")], writes=[K("Sbf")])
                        if E == 128:
                            P.op("dve", TT_(Stm[:, :, 0:E], b1[0:64, :].rearrange("p (a e) -> p a e", a=4),
                                            Sint[:, :, 0:E], ALU.add), reads=[k1, K("Sint")], writes=[K("Stm")])
                        else:
                            for a in range(2):
                                P.op("dve", TT_(Stm[:, 2 * a:2 * a + 2, 0:E], Pp[a], Sint[:, 2 * a:2 * a + 2, 0:E],
                                                ALU.add), reads=[ppk[a], K("Sint")], writes=[K("Stm")])
                        release(*pidx)
                        yield
                        P.op("dve", TT_(SE(Sst), SE(Stm), ABt[:, :, c, 1:2].to_broadcast([64, 4, E]), ALU.mult),
                             reads=[K("Stm"), KABt], writes=[K("S")])
                        if mix == "m":
                            OC = bo[:, :].rearrange("p (h w t) -> p h w t", h=4, w=2)
                        else:
                            OC = bo[:, 0:4 * L].rearrange("p (h w t) -> p h w t", h=4, w=1)
                        fns = []
                        cols = slice(c * L, (c + 1) * L)
                        for h in range(4):
                            fns.append(MM(OC[:, h, 0, :], vsrc[:, c * 4 + h, 0:128], ATmv[:, c * 4 + h, :], start=True,
                                          stop=False))
                            fns.append(MM(OC[:, h, 0, :], Sbf[:, h, 0:128], QTs[:, h, cols], start=False, stop=True))
                            if mix == "m":
                                fns.append(MM(OC[:, h, 1, :], vsrc[:, c * 4 + h, 128:256], ATmv[:, c * 4 + h, :],
                                              start=True, stop=False))
                                fns.append(MM(OC[:, h, 1, :], Sbf[:, h, 128:256], QTs[:, h, cols], start=False,
                                              stop=True))
                        P.pe_group(fns, reads=vkeys + [KATm, K("Sbf"), KQTs], writes=[bok])
                        yield
                        if mix == "m":
                            dnv = dn[:, :, 0:L]
                            P.op("dve", TS(dnv, OC[:, :, 1, :], -1.0, 1.0, ALU.mult, ALU.max), reads=[bok], writes=[K("dn")])
                            yield
                            P.op("dve", TT_(dnv, OC[:, :, 1, :], dnv, ALU.max), reads=[bok, K("dn")], writes=[K("dn")])
                            yield
                            P.op("dve", lambda e, dnv=dnv: e.reciprocal(dnv, dnv), reads=[K("dn")], writes=[K("dn")])
                            yield
                            P.op("dve", TT_(ob[:, :, cols], OC[:, :, 0, :], dnv, ALU.mult), reads=[bok, K("dn")],
                                 writes=[okey])
                        else:
                            P.op("act", ACT(ob[:, :, cols], OC[:, :, 0, :], AF.Copy), reads=[bok], writes=[okey])
                        release(io_)
                        if last_of_seg:
                            if mix == "m":
                                P.op("dve", TT_(Stm[:], Sst[:], Eo4[:, seg, :].unsqueeze(2).to_broadcast([64, 4, 256]),
                                                ALU.mult), reads=[K("S"), K("Eo4")], writes=[K("Stm")])
                                P.dma("pool", K("stoC"), DMA(state_dram(o_mC[:, seg], l, z), Stm[:, :, 0:128]),
                                      reads=[K("Stm")], writes=["d_so"])
                                for h in range(4):
                                    dstn = o_mN[l, seg, z, h, :].rearrange("(d o) -> d o", o=1)
                                    P.dma("pool", K(f"stoN{h}"), DMA(dstn, Stm[:, h, 128:129]), reads=[K("Stm")],
                                          writes=["d_so"])
                            else:
                                od = o_gS if mix == "g" else o_hS
                                P.dma("pool", K("stoS"), DMA(state_dram(od[:, seg], l, z), Sst[:, :, 0:128]), reads=[K("S")],
                                      writes=["d_so"])
                        yield
                    r0 = MIXI[mix] * 512
                    P.dma("pool", okey, DMA(OTs[z, r0:r0 + 512, tok0:tok0 + 128].rearrange("(h e) t -> e h t", e=128), ob[:]),
                          reads=[okey], writes=["d_ot"])
                    seq_done[0] = ji + 1
                    yield


            return pre(), seq()

        def conv_gen(l_):
            items = []
            if l_ == 0:
                items += [("fin", 0, 1), ("fout", 0, 1), ("br", 0, 0), ("br", 0, 1), ("br", 0, 2), ("out", 0, None)]
                items += [("fin", 1, 0), ("fout", 1, 0), ("in", 1, None), ("fin", 1, 1), ("fout", 1, 1),
                          ("br", 1, 0), ("br", 1, 1), ("br", 1, 2), ("out", 1, None)]
            n = 0
            for (name, l2, idx) in items:
                src = WF[name][l2] if idx is None else WF[name][l2, idx]
                dst = WB[name][l2] if idx is None else WB[name][l2, idx]
                rows = src.shape[0]
                for r0 in range(0, rows, 128):
                    r1 = min(rows, r0 + 128)
                    P.dma("pool", f"cv{n % 4}", DMA(dst[r0:r1, :], src[r0:r1, :]), writes=[f"wb_{name}{l2}{idx}"])
                    n += 1
                    for _ in range(14 if l_ == 0 else 1):
                        yield

        def lockstep(gens, background=None):
            gens = list(gens)
            bg = background if background is not None else []
            while gens:
                for g in list(gens):
                    try:
                        next(g)
                    except StopIteration:
                        gens.remove(g)
                for g in list(bg):
                    try:
                        next(g)
                    except StopIteration:
                        bg.remove(g)

        def stage_S(l):
            hgrn_lb(l)
            bg = []
            if not DBG_SKIP_A and l == 0:
                bg.append(conv_gen(l))
            mp = []
            if "mpass" in DBG_SPARTS:
                mp = [mlstm_m_pass(l, 0, 0), mlstm_m_pass(l, 1, 1)]
            bg_all = bg + mp
            for mix in ("g", "h", "m"):
                if mix == "m" and mp:
                    rest = [g for g in mp if g in bg_all]
                    lockstep(rest)
                    for g in rest:
                        if g in bg_all:
                            bg_all.remove(g)
                if mix in DBG_SPARTS:
                    p0, s0 = scan_chain(l, mix, 0, 0)
                    p1, s1 = scan_chain(l, mix, 1, 1)
                    lockstep([p0, s0, p1, s1], background=bg_all)
            lockstep(list(bg_all))

        bcnt = [0]

        def B_phase1(l, tt, yb):
            yTb = yTbs[yb]
            tok0 = tt * TT
            for i in range(3):
                for h in range(4):
                    n = bcnt[0] % 2
                    bcnt[0] += 1
                    r0 = i * 512 + h * 128
                    o0, o1 = ofb[2 * n], ofb[2 * n + 1]
                    P.dma("sp", f"ofb{2 * n}", DMA(o0[:], OTs[0, r0:r0 + 128, tok0:tok0 + TT]),
                          writes=[f"ofb{2 * n}"])
                    P.dma("sp", f"ofb{2 * n + 1}", DMA(o1[:], OTs[1, r0:r0 + 128, tok0:tok0 + TT]),
                          writes=[f"ofb{2 * n + 1}"])
                    P.dma("sp", f"gtb2{n}", DMA(gtb2[n][:], GATES[r0:r0 + 128, tok0:tok0 + TT]),
                          writes=[f"gtb2{n}"])
                    P.op("pool", TT_(o0[:], o0[:], o1[:], ALU.add), reads=[f"ofb{2 * n}", f"ofb{2 * n + 1}"],
                         writes=[f"ofb{2 * n}"])
                    P.op("act", ACT(sq2[n][:], o0[:], AF.Square), reads=[f"ofb{2 * n}"], writes=[f"sq2{n}"])
                    ph = pb[7]
                    P.pe_group([MM(ph[:, :], ones_b[:], sq2[n][:])], reads=[f"sq2{n}", "ones_b"], writes=["pb7"])
                    P.op("dve", TS(tmp2[n][:], ph[:, :], 1.0 / 128, EPS, ALU.mult, ALU.add), reads=["pb7"],
                         writes=[f"tmp2{n}"])
                    P.op("act", ACT(tmp2[n][:], tmp2[n][:], AF.Sqrt), reads=[f"tmp2{n}"], writes=[f"tmp2{n}"])
                    P.op("dve", lambda e, n=n: e.reciprocal(tmp2[n][:], tmp2[n][:]), reads=[f"tmp2{n}"],
                         writes=[f"tmp2{n}"])
                    P.op("pool", TT_(o0[:], o0[:], tmp2[n][:], ALU.mult), reads=[f"ofb{2 * n}", f"tmp2{n}"],
                         writes=[f"ofb{2 * n}"])
                    P.op("dve", STT(yTb[:, i * 4 + h, :], o0[:], hnorm[:, l, i, h:h + 1], gtb2[n][:], ALU.mult,
                                    ALU.mult), reads=[f"ofb{2 * n}", "hnorm", f"gtb2{n}"],
                         writes=[f"yTb{yb}_{i * 4 + h}"])

        def stage_B(l):
            cap = P.capture(lambda: B_phase1(l, 0, 0))
            P.commit(cap)
            for tt in range(NT):
                capR = P.capture(lambda: B_rest(l, tt, tt % 2))
                capP = P.capture(lambda: B_phase1(l, tt + 1, (tt + 1) % 2)) if tt + 1 < NT else []
                P.commit(capR, capP)

        def B_rest(l, tt, yb):
            dst = X3 if l == 0 else yT_out
            yTb = yTbs[yb]
            if True:
                tok0 = tt * TT
                load_x(X1, tok0)
                for i in range(3):
                    slot = ring_next()
                    wv = load_w(slot, [(wsrc("br", l, i)[0], 0, D, 0)], 4, D, bf=True)
                    for dc in range(KC):
                        n = bcnt[0] % 2
                        bcnt[0] += 1
                        pp = pb[5 + n]
                        fns = [MM(pp[:, :], wv[:, h, dc * 128:(dc + 1) * 128], yTb[:, i * 4 + h, :], start=(h == 0),
                                  stop=(h == 3)) for h in range(4)]
                        P.pe_group(fns, reads=[f"ring{slot}"] + [f"yTb{yb}_{i * 4 + h}" for h in range(4)],
                                   writes=[f"pb{5 + n}"])
                        r0 = 1536 + i * D + dc * 128
                        P.dma("sp", f"gtb{n}", DMA(gtb[n][:], GATES[r0:r0 + 128, tok0:tok0 + TT]), reads=["d_gates"],
                              writes=[f"gtb{n}"])
                        if i == 0:
                            P.op("dve", TT_(yF[:, dc, :], pp[:, :], gtb[n][:], ALU.mult),
                                 reads=[f"pb{5 + n}", f"gtb{n}"], writes=[f"yF{dc}"])
                        else:
                            P.op("dve", TT_(tmp[n][:], pp[:, :], gtb[n][:], ALU.mult), reads=[f"pb{5 + n}", f"gtb{n}"],
                                 writes=[f"tmp{n}"])
                            P.op("pool", TT_(yF[:, dc, :], yF[:, dc, :], tmp[n][:], ALU.add),
                                 reads=[f"yF{dc}", f"tmp{n}"], writes=[f"yF{dc}"])
                for k in range(KC):
                    P.op("act", ACT(hT[:, k, :], yF[:, k, :], AF.Copy), reads=[f"yF{k}"], writes=[f"hT{k}"])
                for ob_ in range(2):
                    slot = ring_next()
                    wv = load_w(slot, [(wsrc("out", l)[0], ob_ * 512, 512, 0)], KC, 512, bf=True)
                    for c in range(4):
                        dc = ob_ * 4 + c
                        n = bcnt[0] % 2
                        bcnt[0] += 1
                        pp = pb[5 + n]
                        fns = [MM(pp[:, :], wv[:, k, c * 128:(c + 1) * 128], hT[:, k, :], start=(k == 0),
                                  stop=(k == KC - 1)) for k in range(KC)]
                        P.pe_group(fns, reads=[f"ring{slot}"] + hkeys, writes=[f"pb{5 + n}"])
                        if dc >= 1:
                            stats_chunk(yF, "yF", dc - 1, KC)
                        P.op("act", ACT(yF[:, dc, :], pp[:, :], AF.Copy), reads=[f"pb{5 + n}"], writes=[f"yF{dc}"])
                stats_chunk(yF, "yF", KC - 1, KC)
                resid_update(l, 1, stats_done=True, next_stats=True)
                ffn(l, 1, 2, tok0, have_stats=True)
                store_x(dst, tok0, "d_x")

        stage_S_consts()
        P.barrier()
        stage0()
        done = False
        for l in layers:
            stq[0] = "sp" if l == 0 else "pool"
            if not DBG_SKIP_A:
                stage_A(l)
            P.barrier()
            if stop_after == ("A", l):
                break
            stage_S(l)
            P.barrier()
            if stop_after == ("S", l):
                break
            stq[0] = "pool"
            stage_B(l)
            P.barrier()
        P.finish(block)
        print("instructions:", P.n_inst, "dma sems:", len(P.dma_sems))
    return nc


_CACHE = {}


def make_in_maps(inputs):
    f = lambda a: np.ascontiguousarray(np.asarray(a, dtype=np.float32))
    x_prompt, x_sample = f(inputs["x_prompt"]), f(inputs["x_sample"])
    c, c_ctx = f(inputs["c"]), f(inputs["c_ctx"])
    common = {
        "w_ada": f(inputs["w_ada"]),
        "bada": f(np.transpose(f(inputs["b_ada"]).reshape(2, 72, 128), (2, 0, 1))),
        "npre": f(np.transpose(f(inputs["norm_pre"]).reshape(2, 3, KC, 128), (3, 0, 1, 2))),
        "npost": f(np.transpose(f(inputs["norm_post"]).reshape(2, 3, KC, 128), (3, 0, 1, 2))),
        "hnorm": f(np.transpose(f(inputs["head_norm"]).reshape(2, 3, 4, 128), (3, 0, 1, 2))),
        "w_ffn_in": f(inputs["w_ffn_in"]), "w_ffn_out": f(inputs["w_ffn_out"]), "w_in": f(inputs["w_in"]),
        "w_branch": f(inputs["w_branch"]), "w_out": f(inputs["w_out"]),
        "mgb": f(inputs["mlstm_gate_bias"]).reshape(1, 32),
        "mgbT": f(f(inputs["mlstm_gate_bias"]).reshape(8, 4).T),
        "gla_w_up": f(inputs["gla_w_up"]),
        "gla_b": f(inputs["gla_b"]).reshape(1, 1024),
        "hgam": f(inputs["hgrn_gamma"]).reshape(1, 512),
        "hgamT": f(np.transpose(f(inputs["hgrn_gamma"]).reshape(2, 4, 64), (2, 0, 1))),
        "ones": np.ones((128, 128), np.float32),
    }
    for mix, L in MIXL.items():
        scale = {"m": -1.0, "g": -1.0 / 16.0, "h": 1.0}[mix]
        for z in (0, 1):
            wcat, wm, mlo, mhi, kz = scan_consts(L, z, scale)
            common[f"kz_{mix}{z}"] = kz
            common[f"wcat_{mix}{z}"] = wcat
            common[f"wm_{mix}{z}"] = wm
            common[f"mlo_{mix}{z}"] = mlo
            common[f"mhi_{mix}{z}"] = mhi
    pos_s = grid_position_T(T)
    zeros_pos = np.zeros((D, T), np.float32)
    maps = []
    for core in range(8):
        m = dict(common)
        fl = np.zeros((2, NSEG, 2), np.float32)
        if core < 4:
            b = core
            m["xT"] = f(x_sample[b].T)
            m["posT"] = pos_s
            m["cT"] = f(c[b].reshape(KC, 128).T)
            m["mC0"] = f(np.transpose(f(inputs["state_mlstm_C"])[b], (0, 1, 2, 3, 4)))
            n0 = f(inputs["state_mlstm_n"])[b]
            m["mN0"] = f(np.repeat(n0[..., None], 128, axis=-1))
            m["mM0"] = f(inputs["state_mlstm_m"])[b].reshape(1, 16)
            m["mM0T"] = f(f(inputs["state_mlstm_m"])[b].reshape(4, 4).T)
            m["gS0"] = f(inputs["state_gla_S"])[b]
            m["hS0"] = f(inputs["state_hgrn_S"])[b]
            fl[0, :, 0] = 1.0
            fl[0, 0, 0] = 0.0
            fl[0, 0, 1] = 1.0
            fl[1, :, 0] = 1.0
            fl[1, NSEG - 1, 0] = 0.0
            fl[1, NSEG - 1, 1] = 1.0
        else:
            j = core - 4
            xp = np.zeros((T, D), np.float32)
            xp[:2048] = x_prompt[8 * j:8 * j + 8].reshape(2048, D)
            m["xT"] = f(xp.T)
            m["posT"] = zeros_pos
            m["cT"] = f(c_ctx.reshape(KC, 128).T)
            m["mC0"] = np.zeros((2, 2, 4, 64, 128), np.float32)
            m["mN0"] = np.zeros((2, 2, 4, 64, 128), np.float32)
            m["mM0"] = np.zeros((1, 16), np.float32)
            m["mM0T"] = np.zeros((4, 4), np.float32)
            m["gS0"] = np.zeros((2, 2, 4, 64, 128), np.float32)
            m["hS0"] = np.zeros((2, 2, 4, 64, 128), np.float32)
        m["flags"] = fl.reshape(1, -1)
        maps.append(m)
    return maps


def kernel(**inputs):
    if "nc" not in _CACHE:
        _CACHE["nc"] = build()
    nc = _CACHE["nc"]
    maps = make_in_maps(inputs)
    res = run_bass_kernel_spmd(nc, maps, core_ids=list(range(8)))
    R = res.results
    y_sample = np.stack([np.ascontiguousarray(R[b]["yT"].T) for b in range(4)], axis=0).astype(np.float32)
    yp = []
    mC, mN, mM, gS, hS = [], [], [], [], []
    for j in range(4):
        r = R[4 + j]
        yp.append(np.ascontiguousarray(r["yT"].T)[:2048].reshape(8, 256, D))
        mC.append(np.transpose(r["o_mC"][:, :8], (1, 0, 2, 3, 4, 5)))
        mN.append(np.transpose(r["o_mN"][:, :8], (1, 0, 2, 3, 4)))
        mM.append(np.transpose(r["o_mM"][:, :8], (1, 0, 2, 3)))
        gS.append(np.transpose(r["o_gS"][:, :8], (1, 0, 2, 3, 4, 5)))
        hS.append(np.transpose(r["o_hS"][:, :8], (1, 0, 2, 3, 4, 5)))
    cat = lambda lst: np.ascontiguousarray(np.concatenate(lst, axis=0)).astype(np.float32)
    return (cat(yp), y_sample, cat(mC), cat(mN), cat(mM), cat(gS), cat(hS))
```

```python
import math
from contextlib import ExitStack
import numpy as np
import ml_dtypes
import concourse.bass as bass
import concourse.mybir as mybir
from concourse.bass_utils import run_bass_kernel_spmd

F32 = mybir.dt.float32
BF16 = mybir.dt.bfloat16
AF = mybir.ActivationFunctionType
ALU = mybir.AluOpType
AX = mybir.AxisListType

D = 1024
KC = 8
FF = 2816
FC = 22
T = 4096
TT = 512
NT = T // TT
NSEG = 16
SEGLEN = 256
EPS = 1e-6
INC = 7984
BIG = 3.0e38
C_MQ, C_MK, C_MV, C_MO, C_MIF = 0, 256, 512, 1024, 1536
C_GQ, C_GK, C_GV, C_GR, C_GLR = 1552, 1808, 2064, 2576, 3088
C_HQ, C_HF, C_HV, C_HG, C_MG = 3120, 3376, 3888, 4400, 4912
MIXL = {"m": 64, "g": 64, "h": 32}
MIXI = {"m": 0, "g": 1, "h": 2}
DBG_SPARTS = {"mpass", "m", "g", "h"}
DBG_SKIP_A = False
DBG_NSUB = None
DBG_LEVEL = 9


class Prog:
    ENGS = ("pe", "act", "dve", "pool", "sp")

    def __init__(self, nc, stack):
        self.nc = nc
        self.stack = stack
        self.sem = {e: stack.enter_context(nc.semaphore("prog_" + e)) for e in self.ENGS}
        self.cnt = {e: 0 for e in self.ENGS}
        self.prog = {e: [] for e in self.ENGS}
        self.seen = {e: {} for e in self.ENGS}
        self.last_w = {}
        self.readers = {}
        self.dma_sems = {}
        self.dma_cnt = {}
        self.n_inst = 0
        self.cap = None

    def capture(self, fn):
        assert self.cap is None
        self.cap = []
        try:
            fn()
        finally:
            lst, self.cap = self.cap, None
        return lst

    def commit(self, *lists):
        lists = [l_ for l_ in lists if l_]
        if not lists:
            return
        main = max(lists, key=len)
        others = [l_ for l_ in lists if l_ is not main]
        pos = [0] * len(others)
        for i, item in enumerate(main):
            self._replay(item)
            for oi, ol in enumerate(others):
                want = (len(ol) * (i + 1)) // len(main)
                while pos[oi] < want:
                    self._replay(ol[pos[oi]])
                    pos[oi] += 1

    def _replay(self, item):
        kind = item[0]
        if kind == "op":
            self.op(*item[1:])
        elif kind == "pe":
            self.pe_group(*item[1:])
        else:
            self.dma(*item[1:])

    def _need(self, eng, reads, writes):
        toks = []
        for k in reads:
            t = self.last_w.get(k)
            if t is not None:
                toks.append(t)
        for k in writes:
            t = self.last_w.get(k)
            if t is not None:
                toks.append(t)
            toks.extend(self.readers.get(k, ()))
        best = {}
        for (s, v) in toks:
            if v > best.get(s.name, (None, 0))[1]:
                best[s.name] = (s, v)
        out = []
        seen = self.seen[eng]
        own = self.sem[eng].name if eng == "pe" else None
        for name, (s, v) in best.items():
            if name == own:
                continue
            if seen.get(name, 0) >= v:
                continue
            seen[name] = v
            out.append((s, v))
        return out

    def _commit(self, tok, reads, writes):
        for k in writes:
            self.last_w[k] = tok
            self.readers[k] = []
        for k in reads:
            if k in writes:
                continue
            self.readers.setdefault(k, []).append(tok)

    def op(self, eng, fn, reads=(), writes=()):
        if self.cap is not None:
            self.cap.append(("op", eng, fn, tuple(reads), tuple(writes)))
            return None
        waits = self._need(eng, reads, writes)
        self.cnt[eng] += 1
        sem = self.sem[eng]
        tok = (sem, self.cnt[eng])
        self._commit(tok, reads, writes)

        def emit(e, fn=fn, waits=waits, sem=sem):
            for (s, v) in waits:
                e.wait_ge(s, v)
            fn(e).then_inc(sem, 1)
        self.prog[eng].append(emit)
        self.n_inst += 1
        return tok

    def pe_group(self, fns, reads=(), writes=()):
        if self.cap is not None:
            self.cap.append(("pe", fns, tuple(reads), tuple(writes)))
            return None
        waits = self._need("pe", reads, writes)
        self.cnt["pe"] += 1
        sem = self.sem["pe"]
        tok = (sem, self.cnt["pe"])
        self._commit(tok, reads, writes)

        def emit(e, fns=fns, waits=waits, sem=sem):
            for (s, v) in waits:
                e.wait_ge(s, v)
            for f in fns[:-1]:
                f(e)
            fns[-1](e).then_inc(sem, 1)
        self.prog["pe"].append(emit)
        self.n_inst += len(fns)
        return tok

    def dma(self, eng, slot, fn, reads=(), writes=()):
        if self.cap is not None:
            self.cap.append(("dma", eng, slot, fn, tuple(reads), tuple(writes)))
            return None
        slot = f"{slot}_{eng}"
        if slot not in self.dma_sems:
            self.dma_sems[slot] = self.stack.enter_context(self.nc.semaphore("dma_" + str(slot)))
            self.dma_cnt[slot] = 0
        waits = self._need(eng, reads, writes)
        self.dma_cnt[slot] += 16
        sem = self.dma_sems[slot]
        tok = (sem, self.dma_cnt[slot])
        self._commit(tok, reads, writes)

        def emit(e, fn=fn, waits=waits, sem=sem):
            for (s, v) in waits:
                e.wait_ge(s, v)
            fn(e).then_inc(sem, 16)
        self.prog[eng].append(emit)
        self.n_inst += 1
        return tok

    def barrier(self):
        toks = [(self.sem[e], self.cnt[e]) for e in self.ENGS if self.cnt[e] > 0]
        toks += [(self.dma_sems[s], self.dma_cnt[s]) for s in self.dma_sems]
        for eng in self.ENGS:
            seen = self.seen[eng]
            waits = []
            for (s, v) in toks:
                if seen.get(s.name, 0) >= v:
                    continue
                seen[s.name] = v
                waits.append((s, v))

            def emit(e, waits=waits):
                for (s, v) in waits:
                    e.wait_ge(s, v)
            self.prog[eng].append(emit)
        self.last_w = {}
        self.readers = {}

    def finish(self, block):
        self.barrier()
        P = self.prog

        @block.tensor
        def _(e):
            for f in P["pe"]:
                f(e)

        @block.scalar
        def _(e):
            for f in P["act"]:
                f(e)

        @block.vector
        def _(e):
            for f in P["dve"]:
                f(e)

        @block.gpsimd
        def _(e):
            for f in P["pool"]:
                f(e)

        @block.sync
        def _(e):
            for f in P["sp"]:
                f(e)


def MM(out, lhsT, rhs, start=True, stop=True):
    return lambda e: e.matmul(out, lhsT, rhs, start=start, stop=stop)


def ACT(out, in_, func, **kw):
    return lambda e: e.activation(out, in_, func, **kw)


def TT_(out, a, b, op):
    return lambda e: e.tensor_tensor(out, a, b, op)


def TS(out, a, s1, s2, op0, op1=None):
    if op1 is None:
        return lambda e: e.tensor_scalar(out, a, s1, None, op0)
    return lambda e: e.tensor_scalar(out, a, s1, s2, op0, op1)


def STT(out, in0, scalar, in1, op0, op1):
    return lambda e: e.scalar_tensor_tensor(out, in0, scalar, in1, op0, op1)


def CP(out, in_):
    return lambda e: e.tensor_copy(out, in_)


def DMA(out, in_):
    return lambda e: e.dma_start(out=out, in_=in_)


def scan_consts(L, z, scale):
    s = np.arange(L)[:, None]
    t = np.arange(L)[None, :]
    if z == 0:
        mid = L // 2 - 1
        Wm = ((s > mid) & (s <= t)).astype(np.float32) - ((s > t) & (s <= mid)).astype(np.float32)
        wa = (np.arange(L) <= mid).astype(np.float32)
        wb = (np.arange(L) > mid).astype(np.float32)
        mask = (s <= t).astype(np.float32)
    else:
        mid = L // 2
        Wm = ((s >= t) & (s < mid)).astype(np.float32) - ((s >= mid) & (s < t)).astype(np.float32)
        wa = (np.arange(L) >= mid).astype(np.float32)
        wb = (np.arange(L) < mid).astype(np.float32)
        mask = (s >= t).astype(np.float32)
    Wcat = np.concatenate([Wm, wa[:, None], wb[:, None]], axis=1) * scale
    pos_in_chunk = np.arange(128) % L
    if z == 0:
        kz = (pos_in_chunk < L // 2).astype(np.float32)
    else:
        kz = (pos_in_chunk >= L // 2).astype(np.float32)
    kz = np.ascontiguousarray(np.broadcast_to(kz[None, :], (64, 128))).astype(np.float32)
    return (Wcat.astype(np.float32), (Wm * scale).astype(np.float32),
            (-BIG * mask).astype(np.float32), mask.astype(np.float32), kz)


def grid_position_T(n_tokens, grid_w=64):
    rows = n_tokens // grid_w
    quarter = D // 4
    freqs = np.exp(-math.log(10000.0) * np.arange(quarter, dtype=np.float32) / quarter).astype(np.float32)
    r = np.arange(rows, dtype=np.float32)[:, None] * freqs
    cl = np.arange(grid_w, dtype=np.float32)[:, None] * freqs
    r_emb = np.concatenate([np.sin(r), np.cos(r)], axis=-1)
    c_emb = np.concatenate([np.sin(cl), np.cos(cl)], axis=-1)
    emb = np.concatenate([np.broadcast_to(r_emb[:, None], (rows, grid_w, D // 2)),
                          np.broadcast_to(c_emb[None], (rows, grid_w, D // 2))], axis=-1)
    return np.ascontiguousarray(emb.reshape(rows * grid_w, D).T.astype(np.float32))


def build(dbg=False, stop_after=None, layers=(0, 1)):
    nc = bass.Bass("TRN2", target_bir_lowering=False)

    def din(name, shape, dt=F32):
        return nc.dram_tensor(name, list(shape), dt, kind="ExternalInput").ap()

    def dout(name, shape, dt=F32):
        return nc.dram_tensor(name, list(shape), dt, kind="ExternalOutput").ap()

    def dscr(name, shape, dt=F32):
        return nc.dram_tensor(name, list(shape), dt, kind=("ExternalOutput" if dbg else "Internal")).ap()

    xT_in = din("xT", [D, T])
    posT = din("posT", [D, T])
    cT_in = din("cT", [128, KC])
    w_ada = din("w_ada", [2, D, 9 * D])
    bada_in = din("bada", [128, 2, 72])
    npre_in = din("npre", [128, 2, 3, KC])
    npost_in = din("npost", [128, 2, 3, KC])
    hnorm_in = din("hnorm", [128, 2, 3, 4])
    w_ffn_in = din("w_ffn_in", [2, 2, D, 2 * FF])
    w_ffn_out = din("w_ffn_out", [2, 2, FF, D])
    w_in = din("w_in", [2, D, INC])
    w_branch = din("w_branch", [2, 3, 512, D])
    w_out = din("w_out", [2, D, D])
    mgb_in = din("mgb", [1, 32])
    gwup_in = din("gla_w_up", [2, 2, 16, 256])
    gb_in = din("gla_b", [1, 2 * 2 * 256])
    hgam_in = din("hgam", [1, 512])
    hgamT_in = din("hgamT", [64, 2, 4])
    mC0 = din("mC0", [2, 2, 4, 64, 128])
    mN0 = din("mN0", [2, 2, 4, 64, 128])
    mM0 = din("mM0", [1, 16])
    mM0T_in = din("mM0T", [4, 4])
    mgbT_in = din("mgbT", [4, 8])
    gS0 = din("gS0", [2, 2, 4, 64, 128])
    hS0 = din("hS0", [2, 2, 4, 64, 128])
    flags_in = din("flags", [1, 2 * NSEG * 2])
    ones_in = din("ones", [128, 128])
    consts_in = {}
    for mix, L in MIXL.items():
        for z in (0, 1):
            consts_in[(mix, z)] = (din(f"wcat_{mix}{z}", [L, L + 2]), din(f"wm_{mix}{z}", [L, L]),
                                   din(f"mlo_{mix}{z}", [L, L]), din(f"mhi_{mix}{z}", [L, L]),
                                   din(f"kz_{mix}{z}", [64, 128]))
    yT_out = dout("yT", [D, T])
    o_mC = dout("o_mC", [2, NSEG, 2, 4, 64, 128])
    o_mN = dout("o_mN", [2, NSEG, 2, 4, 64])
    o_mM = dout("o_mM", [2, NSEG, 2, 4])
    o_gS = dout("o_gS", [2, NSEG, 2, 4, 64, 128])
    o_hS = dout("o_hS", [2, NSEG, 2, 4, 64, 128])
    X1 = dscr("X1", [D, T])
    X3 = dscr("X3", [D, T])
    QKT = dscr("QKT", [1280, T])
    HFT = dscr("HFT", [512, T])
    GLRT = dscr("GLRT", [32, T])
    MGT = dscr("MGT", [16, T])
    KTOK = dscr("KTOK", [T, 512])
    HFTOK = dscr("HFTOK", [T, 512])
    MGTOK = dscr("MGTOK", [T, 16])
    VT = dscr("V", [T, 1536], BF16)
    GATES = dscr("GATES", [4608, T], BF16)
    OTs = dscr("OT", [2, 1536, T])
    WB = {"fin": dscr("WB_fin", [2, 2, D, 2 * FF], BF16), "fout": dscr("WB_fout", [2, 2, FF, D], BF16),
          "in": dscr("WB_in", [2, D, INC], BF16), "br": dscr("WB_br", [2, 3, 512, D], BF16),
          "out": dscr("WB_out", [2, D, D], BF16)}
    WF = {"fin": w_ffn_in, "fout": w_ffn_out, "in": w_in, "br": w_branch, "out": w_out}
    NOCONV = {("fin", 0, 0), ("fout", 0, 0), ("in", 0, None)}

    def wsrc(name, l, idx=None):
        conv = (name, l, idx) not in NOCONV
        t = WB[name] if conv else WF[name]
        return (t[l] if idx is None else t[l, idx]), conv

    dests = {"qkt": QKT, "hft": HFT, "glrt": GLRT, "mgt": MGT, "ktok": KTOK, "hftok": HFTOK,
             "mgtok": MGTOK, "v": VT, "gates": GATES}

    with ExitStack() as st:
        P = Prog(nc, st)

        def sb(name, shape, dt=F32):
            return st.enter_context(nc.sbuf_tensor("s_" + name, list(shape), dt))

        ones_f = sb("ones_f", [128, 128])
        ones_b = sb("ones_b", [128, 128], BF16)
        cTt = sb("cTt", [128, KC])
        scT = sb("scT", [128, KC])
        bada = sb("bada", [128, 2, 72])
        npre = sb("npre", [128, 2, 3, KC])
        npost = sb("npost", [128, 2, 3, KC])
        hnorm = sb("hnorm", [128, 2, 3, 4])
        modT = sb("modT", [128, 2, 72])
        A1 = sb("A1", [128, 2, 3, KC])
        G1 = sb("G1", [128, 2, 3, KC])
        fl = sb("fl", [128, 2, NSEG, 2])
        arena = sb("arena", [128, 17408])
        arena2 = sb("arena2", [128, 17408])

        def carve(off, parts, free, dt, ar=None):
            ar = arena if ar is None else ar
            n = 1
            for d_ in free:
                n *= d_
            if dt == BF16:
                v = ar[:].bitcast(BF16)[0:parts, off // 2:off // 2 + n]
            else:
                v = ar[0:parts, off // 4:off // 4 + n]
            if len(free) == 2:
                v = v.rearrange("p (a b) -> p a b", a=free[0])
            elif len(free) == 3:
                v = v.rearrange("p (a b c) -> p a b c", a=free[0], b=free[1])
            return v

        actT = carve(0, 128, [FC, TT], BF16)
        yF = carve(22528, 128, [KC, TT], F32)
        ring = [carve(12288 * i, 128, [3072], F32, arena2) for i in range(3)]
        xT = carve(36864, 128, [KC, TT], F32, arena2)
        hT = carve(53248, 128, [KC, TT], BF16, arena2)
        arena3 = sb("arena3", [128, 7680])
        posb = [carve(2048 * i, 128, [TT], F32, arena3) for i in range(2)]
        sq = [carve(4096 + 1024 * i, 128, [TT], BF16, arena3) for i in range(2)]
        tmp = [carve(6144 + 2048 * i, 128, [TT], F32, arena3) for i in range(2)]
        rstd = carve(10240, 128, [TT], F32, arena3)
        sgt = [carve(12288 + 2048 * i, 128, [TT], F32, arena3) for i in range(2)]
        stf = [carve(61440 + 2048 * i, 128, [TT], F32, arena2) for i in range(4)]
        stb = [carve(16384 + 1024 * i, 128, [TT], BF16, arena3) for i in range(4)]
        yTbs = [carve(38912, 128, [12, TT], BF16), carve(51200, 128, [12, TT], BF16)]
        sq2 = [carve(63488 + 1024 * i, 128, [TT], BF16) for i in range(2)]
        tmp2 = [carve(65536 + 2048 * i, 128, [TT], F32) for i in range(2)]
        gtb2 = [sb(f"gtb2_{i}", [128, TT], BF16) for i in range(2)]
        ofb = [carve(20480 + 2048 * i, 128, [TT], F32, arena3) for i in range(4)]
        gtb = [carve(28672 + 1024 * i, 128, [TT], BF16, arena3) for i in range(2)]
        pb = [st.enter_context(nc.psum_tensor(f"pb{i}", [128, 512], F32)) for i in range(8)]

        block = st.enter_context(nc.Block())

        ring_i = [0]

        def ring_next():
            i = ring_i[0] % 3
            ring_i[0] += 1
            return i

        def load_w(slot, pieces, kc, width, cast=True, bf=False):
            if cast:
                view = ring[slot].bitcast(BF16)[:, 0:kc * width].rearrange("p (k c) -> p k c", k=kc)
            else:
                view = ring[slot][:, 0:kc * width].rearrange("p (k c) -> p k c", k=kc)
            q = "sp" if (bf or not cast) else "pool"
            for (ap, c0, w, off) in pieces:
                src = ap[:, c0:c0 + w].rearrange("(k p) c -> p k c", p=128)
                P.dma(q, f"ring{slot}", DMA(view[:, :, off:off + w], src), writes=[f"ring{slot}"])
            return view

        stq = ["sp"]

        P.dma("sp", "cst", DMA(ones_f[:], ones_in), writes=["ones_f"])
        P.dma("sp", "cst", DMA(cTt[:], cT_in), writes=["cTt"])
        P.dma("sp", "cst", DMA(bada[:], bada_in), writes=["bada"])
        P.dma("sp", "cst", DMA(npre[:], npre_in), writes=["npre"])
        P.dma("sp", "cst", DMA(npost[:], npost_in), writes=["npost"])
        P.dma("sp", "cst", DMA(hnorm[:], hnorm_in), writes=["hnorm"])
        P.dma("sp", "cst", DMA(fl[:].rearrange("p a b c -> p (a b c)"), flags_in.partition_broadcast(128)),
              writes=["fl"])
        def stage0():
            P.op("dve", CP(ones_b[:], ones_f[:]), reads=["ones_f"], writes=["ones_b"])
            P.op("act", ACT(scT[:], cTt[:], AF.Silu), reads=["cTt"], writes=["scT"])
            pm = pb[7]
            for l in layers:
                for blk in range(36):
                    slot = ring_next()
                    wv = load_w(slot, [(w_ada[l], blk * 256, 256, 0)], KC, 256, cast=False)
                    fns = []
                    for c in range(2):
                        j = blk * 2 + c
                        for k in range(KC):
                            fns.append(MM(pm[:, j:j + 1], wv[:, k, c * 128:(c + 1) * 128], scT[:, k:k + 1],
                                          start=(k == 0), stop=(k == KC - 1)))
                    P.pe_group(fns, reads=[f"ring{slot}", "scT"], writes=["pm"])
                P.op("dve", TT_(modT[:, l, :], pm[:, 0:72], bada[:, l, :], ALU.add), reads=["pm", "bada"],
                     writes=["modT"])
                for j in range(3):
                    P.op("dve", STT(A1[:, l, j, :], modT[:, l, (3 * j + 1) * 8:(3 * j + 2) * 8], 1.0, npre[:, l, j, :],
                                    ALU.add, ALU.mult), reads=["modT", "npre"], writes=["A1"])
                    P.op("dve", STT(G1[:, l, j, :], modT[:, l, (3 * j + 2) * 8:(3 * j + 3) * 8],
                                    (1.0 if j == 1 else 0.5), npost[:, l, j, :], ALU.mult, ALU.mult),
                         reads=["modT", "npost"], writes=["G1"])

        def B1(l, j, k):
            return modT[:, l, 3 * j * 8 + k:3 * j * 8 + k + 1]

        def stats_chunk(src_tile, srckey, k, n_chunks):
            b = k % 2
            P.op("act", ACT(sq[b][:], src_tile[:, k, :], AF.Square), reads=[f"{srckey}{k}"], writes=[f"sq{b}"])
            P.pe_group([MM(pb[4][:, :], ones_b[:], sq[b][:], start=(k == 0), stop=(k == n_chunks - 1))],
                       reads=[f"sq{b}", "ones_b"], writes=["pss"])

        def rstd_finish(inv_n):
            pss = pb[4]
            P.op("dve", TS(rstd[:], pss[:, :], inv_n, EPS, ALU.mult, ALU.add), reads=["pss"], writes=["rstd"])
            P.op("act", ACT(rstd[:], rstd[:], AF.Sqrt), reads=["rstd"], writes=["rstd"])
            P.op("dve", lambda e: e.reciprocal(rstd[:], rstd[:]), reads=["rstd"], writes=["rstd"])

        def norm_stats(src_tile, srckey, n_chunks, inv_n):
            for k in range(n_chunks):
                stats_chunk(src_tile, srckey, k, n_chunks)
            rstd_finish(inv_n)

        def norm_to_h(l, j, tok0, pos, have_stats=False):
            if have_stats:
                rstd_finish(1.0 / D)
            else:
                norm_stats(xT, "xT", KC, 1.0 / D)
            for k in range(KC):
                b = k % 2
                P.op("dve", TT_(tmp[b][:], xT[:, k, :], rstd[:], ALU.mult), reads=[f"xT{k}", "rstd"],
                     writes=[f"tmp{b}"])
                if pos:
                    P.dma("sp", f"posb{b}", DMA(posb[b][:], posT[k * 128:(k + 1) * 128, tok0:tok0 + TT]),
                          writes=[f"posb{b}"])
                    P.op("dve", STT(tmp[b][:], tmp[b][:], A1[:, l, j, k:k + 1], posb[b][:], ALU.mult, ALU.add),
                         reads=[f"tmp{b}", "A1", f"posb{b}"], writes=[f"tmp{b}"])
                    P.op("act", ACT(hT[:, k, :], tmp[b][:], AF.Identity, bias=B1(l, j, k)),
                         reads=[f"tmp{b}", "modT"], writes=[f"hT{k}"])
                else:
                    P.op("act", ACT(hT[:, k, :], tmp[b][:], AF.Identity, bias=B1(l, j, k),
                                    scale=A1[:, l, j, k:k + 1]),
                         reads=[f"tmp{b}", "modT", "A1"], writes=[f"hT{k}"])

        hkeys = [f"hT{k}" for k in range(KC)]

        def resid_update(l, j, stats_done=False, next_stats=False):
            if stats_done:
                rstd_finish(1.0 / D)
            else:
                norm_stats(yF, "yF", KC, 1.0 / D)
            for k in range(KC):
                b = k % 2
                P.op("dve", TT_(tmp[b][:], yF[:, k, :], rstd[:], ALU.mult), reads=[f"yF{k}", "rstd"],
                     writes=[f"tmp{b}"])
                P.op("dve", STT(xT[:, k, :], tmp[b][:], G1[:, l, j, k:k + 1], xT[:, k, :], ALU.mult, ALU.add),
                     reads=[f"tmp{b}", "G1", f"xT{k}"], writes=[f"xT{k}"])
                if next_stats:
                    stats_chunk(xT, "xT", k, KC)

        ffn_cnt = [0]

        def ffn(l, which, j, tok0, have_stats=False, next_stats=False):
            norm_to_h(l, j, tok0, pos=False, have_stats=have_stats)
            Wi, bfi = wsrc("fin", l, which)
            Wo, bfo = wsrc("fout", l, which)
            for blk in range(11):
                slot = ring_next()
                wv = load_w(slot, [(Wi, blk * 256, 256, 0), (Wi, FF + blk * 256, 256, 256)], KC, 512, bf=bfi)
                for c in range(2):
                    n = ffn_cnt[0] % 2
                    ffn_cnt[0] += 1
                    pg, pu = pb[0 + n], pb[2 + n]
                    fns = []
                    for k in range(KC):
                        fns.append(MM(pg[:, :], wv[:, k, c * 128:(c + 1) * 128], hT[:, k, :], start=(k == 0),
                                      stop=(k == KC - 1)))
                        fns.append(MM(pu[:, :], wv[:, k, 256 + c * 128:256 + (c + 1) * 128], hT[:, k, :],
                                      start=(k == 0), stop=(k == KC - 1)))
                    if blk == 0 and c == 0:
                        for k in range(KC):
                            P.pe_group(fns[2 * k:2 * k + 2], reads=[f"ring{slot}", f"hT{k}"],
                                       writes=[f"pb{n}", f"pb{2 + n}"])
                    else:
                        P.pe_group(fns, reads=[f"ring{slot}"] + hkeys, writes=[f"pb{n}", f"pb{2 + n}"])
                    f = blk * 2 + c
                    P.op("act", ACT(sgt[n][:], pg[:, :], AF.Silu), reads=[f"pb{n}"], writes=[f"sgt{n}"])
                    P.op("dve", TT_(actT[:, f, :], pu[:, :], sgt[n][:], ALU.mult), reads=[f"pb{2 + n}", f"sgt{n}"],
                         writes=[f"actT{f}"])
            akeys = [f"actT{f}" for f in range(FC)]
            for ob in range(4):
                slot = ring_next()
                wv = load_w(slot, [(Wo, ob * 256, 256, 0)], FC, 256, bf=bfo)
                for c in range(2):
                    dc = ob * 2 + c
                    n = dc % 2
                    py = pb[5 + n]
                    fns = [MM(py[:, :], wv[:, f, c * 128:(c + 1) * 128], actT[:, f, :], start=(f == 0),
                              stop=(f == FC - 1)) for f in range(FC)]
                    P.pe_group(fns, reads=[f"ring{slot}"] + akeys, writes=[f"pb{5 + n}"])
                    if dc >= 1:
                        stats_chunk(yF, "yF", dc - 1, KC)
                    P.op("act", ACT(yF[:, dc, :], py[:, :], AF.Copy), reads=[f"pb{5 + n}"], writes=[f"yF{dc}"])
            stats_chunk(yF, "yF", KC - 1, KC)
            resid_update(l, j, stats_done=True, next_stats=next_stats)

        st_cnt = {"f": 0, "b": 0, "pj": 0}

        def proj_stage(l, tok0):
            Wl, bfw = wsrc("in", l)
            blocks = [
                (0, 512, [("fm", 0, 256, "qkt", 0, None, 0.125), ("fm", 256, 256, "qkt", 256, None, 1.0),
                          ("tm", 256, 256, "ktok", 0, None)]),
                (C_MV, 512, [("tm", 0, 512, "v", 0, None)]),
                (C_MO, 528, [("fm", 0, 512, "gates", 0, AF.Sigmoid, 1.0), ("fm", 512, 16, "mgt", 0, None, 1.0),
                             ("tm", 512, 16, "mgtok", 0, None)]),
                (C_GQ, 512, [("fm", 0, 256, "qkt", 512, None, 0.125), ("fm", 256, 256, "qkt", 768, None, 1.0),
                             ("tm", 256, 256, "ktok", 256, None)]),
                (C_GV, 512, [("tm", 0, 512, "v", 512, None)]),
                (C_GR, 544, [("fm", 0, 512, "gates", 512, AF.Silu, 1.0), ("fm", 512, 32, "glrt", 0, None, 1.0)]),
                (C_HQ, 256, [("fm", 0, 256, "qkt", 1024, None, 1.0)]),
                (C_HF, 512, [("fm", 0, 512, "hft", 0, None, 1.0), ("tm", 0, 512, "hftok", 0, None)]),
                (C_HV, 512, [("tm", 0, 512, "v", 1024, AF.Silu)]),
                (C_HG, 512, [("fm", 0, 512, "gates", 1024, AF.Silu, 1.0)]),
            ] + [(C_MG + i * 512, 512, [("fm", 0, 512, "gates", 1536 + i * 512, AF.Sigmoid, 1.0)]) for i in range(6)]
            for (c0, width, jobs) in blocks:
                slot = ring_next()
                wv = load_w(slot, [(Wl, c0, width, 0)], KC, width, bf=bfw)
                for job in jobs:
                    if job[0] == "fm":
                        _, off, ncols, dname, row0, func, scale = job
                        dst = dests[dname]
                        isb = (dst.dtype == BF16)
                        for cc in range(0, ncols, 128):
                            m = min(128, ncols - cc)
                            n = st_cnt["pj"] % 2
                            st_cnt["pj"] += 1
                            pp = pb[5 + n]
                            fns = [MM(pp[0:m, :], wv[:, k, off + cc:off + cc + m], hT[:, k, :], start=(k == 0),
                                      stop=(k == KC - 1)) for k in range(KC)]
                            if c0 == 0 and off == 0 and cc == 0:
                                for k in range(KC):
                                    P.pe_group(fns[k:k + 1], reads=[f"ring{slot}", f"hT{k}"], writes=[f"pb{5 + n}"])
                            else:
                                P.pe_group(fns, reads=[f"ring{slot}"] + hkeys, writes=[f"pb{5 + n}"])
                            if isb:
                                si = st_cnt["b"] % 4
                                st_cnt["b"] += 1
                                stg, skey = stb[si], f"stb{si}"
                            else:
                                si = st_cnt["f"] % 4
                                st_cnt["f"] += 1
                                stg, skey = stf[si], f"stf{si}"
                            if func is not None:
                                P.op("act", ACT(stg[0:m, :], pp[0:m, :], func), reads=[f"pb{5 + n}"], writes=[skey])
                            elif scale != 1.0:
                                P.op("act", ACT(stg[0:m, :], pp[0:m, :], AF.Copy, scale=scale), reads=[f"pb{5 + n}"],
                                     writes=[skey])
                            else:
                                P.op("dve", CP(stg[0:m, :], pp[0:m, :]), reads=[f"pb{5 + n}"], writes=[skey])
                            P.dma(stq[0], skey, DMA(dst[row0 + cc:row0 + cc + m, tok0:tok0 + TT], stg[0:m, :]),
                                  reads=[skey], writes=[f"d_{dname}"])
                    else:
                        _, off, ncols, dname, col0, func = job
                        dst = dests[dname]
                        isb = (dst.dtype == BF16)
                        for jj in range(TT // 128):
                            n = st_cnt["pj"] % 2
                            st_cnt["pj"] += 1
                            pp = pb[5 + n]
                            fns = [MM(pp[:, 0:ncols], hT[:, k, jj * 128:(jj + 1) * 128], wv[:, k, off:off + ncols],
                                      start=(k == 0), stop=(k == KC - 1)) for k in range(KC)]
                            P.pe_group(fns, reads=[f"ring{slot}"] + hkeys, writes=[f"pb{5 + n}"])
                            if isb:
                                si = st_cnt["b"] % 4
                                st_cnt["b"] += 1
                                stg, skey = stb[si], f"stb{si}"
                            else:
                                si = st_cnt["f"] % 4
                                st_cnt["f"] += 1
                                stg, skey = stf[si], f"stf{si}"
                            if func is not None:
                                P.op("act", ACT(stg[:, 0:ncols], pp[:, 0:ncols], func), reads=[f"pb{5 + n}"],
                                     writes=[skey])
                            else:
                                P.op("dve", CP(stg[:, 0:ncols], pp[:, 0:ncols]), reads=[f"pb{5 + n}"], writes=[skey])
                            t0 = tok0 + jj * 128
                            P.dma(stq[0], skey, DMA(dst[t0:t0 + 128, col0:col0 + ncols], stg[:, 0:ncols]),
                                  reads=[skey], writes=[f"d_{dname}"])

        def load_x(src, tok0):
            P.dma("sp", "xT", DMA(xT[:], src[:, tok0:tok0 + TT].rearrange("(k p) t -> p k t", p=128)),
                  reads=[], writes=[f"xT{k}" for k in range(KC)])

        def store_x(dst, tok0, key):
            P.dma(stq[0], "xTst", DMA(dst[:, tok0:tok0 + TT].rearrange("(k p) t -> p k t", p=128), xT[:]),
                  reads=[f"xT{k}" for k in range(KC)], writes=[key])

        def stage_A(l):
            src = xT_in if l == 0 else X3
            for tt in range(NT):
                tok0 = tt * TT
                load_x(src, tok0)
                ffn(l, 0, 0, tok0, next_stats=True)
                store_x(X1, tok0, "d_x1")
                norm_to_h(l, 1, tok0, pos=True, have_stats=True)
                proj_stage(l, tok0)

        def make_set(si):
            regions = [[arena if si == 0 else arena2, 0, 17408 * 4], [arena3, si * 15360, (si + 1) * 15360]]

            def cv(nbytes, parts, free, dt):
                for r in regions:
                    if r[1] + nbytes <= r[2]:
                        v = carve(r[1], parts, free, dt, r[0])
                        r[1] += nbytes
                        return v
                raise AssertionError("scan set does not fit")
            d_ = {}
            d_["qTt"] = [cv(2048, 64, [4, 128], F32) for _ in range(2)]
            d_["kin"] = [cv(2048, 64, [4, 128], F32) for _ in range(2)]
            d_["ktok"] = [cv(4096, 64, [4, 256], F32) for _ in range(2)]
            d_["vtk"] = [cv(4096, 64, [4, 512], BF16) for _ in range(2)]
            d_["V2"] = [cv(4096, 64, [8, 256], BF16) for _ in range(2)]
            d_["lat"] = cv(4096, 64, [4, 256], F32)
            d_["Ktk"] = [cv(2048, 64, [4, 256], BF16) for _ in range(2)]
            d_["ATm"] = [cv(2048, 64, [16, 64], BF16) for _ in range(2)]
            d_["Sint"] = cv(4096, 64, [4, 256], F32)
            d_["Stm"] = cv(4096, 64, [4, 256], F32)
            d_["Sini"] = cv(4096, 64, [4, 256], F32)
            d_["Sst"] = cv(4096, 64, [4, 256], F32)
            d_["Sbf"] = cv(2048, 64, [4, 256], BF16)
            d_["oev"] = [cv(2048, 128, [4, 128], F32) for _ in range(2)]
            d_["dn"] = cv(2048, 128, [4, 128], F32)
            d_["kTc"] = cv(2048, 64, [4, 128], F32)
            d_["GcQ"] = cv(2048, 64, [4, 128], F32)
            d_["GcK"] = cv(2048, 64, [4, 128], F32)
            d_["QTs"] = [cv(1024, 64, [4, 128], BF16) for _ in range(2)]
            d_["KTs"] = cv(1024, 64, [4, 128], BF16)
            d_["KTz"] = cv(1024, 64, [4, 128], BF16)
            d_["sgk"] = cv(4096, 64, [4, 256], F32)
            d_["mgk"] = [sb(f"mgk{si}_{i}", [64, 2, 16]) for i in range(2)]
            d_["spm"] = sb(f"spm{si}", [64, 2, 4])
            d_["eli"] = sb(f"eli{si}", [64, 2, 4])
            d_["glrTt"] = [sb(f"glrTt{si}_{i}", [16, 128]) for i in range(2)]
            d_["ABt"] = [sb(f"ABt{si}_{i}", [64, 4, 4, 2]) for i in range(2)]
            d_["emi4"] = sb(f"emi4{si}", [64, 4])
            d_["Eo4"] = sb(f"Eo4{si}", [64, NSEG, 4])
            d_["liT"] = sb(f"liT{si}", [4, SEGLEN])
            d_["xfT"] = sb(f"xfT{si}", [4, SEGLEN])
            d_["bT_"] = sb(f"bT_{si}", [4, SEGLEN])
            d_["mrow"] = sb(f"mrow{si}", [4, 8])
            d_["mcur"] = sb(f"mcur{si}", [4, 1])
            return d_

        SETS = [make_set(0), make_set(1)]
        wup = sb("wup", [16, 2, 256])
        gbb = sb("gbb", [64, 2, 2, 256])
        lbb = sb("lbb", [64, 2, 256])
        hgb = sb("hgb", [64, 2, 256])
        lbT = sb("lbT", [64, 4, 2])
        hgT = sb("hgT", [64, 2, 4])
        mgbb = sb("mgbb", [64, 32])
        onesr = sb("onesr", [4, SEGLEN])
        mgbT = sb("mgbT", [4, 8])
        mm0T = sb("mm0T", [4, 4])
        cst = {}
        for mix, L in MIXL.items():
            for z in (0, 1):
                cst[(mix, z)] = (sb(f"wcat{mix}{z}", [L, L + 2]), sb(f"wm{mix}{z}", [L, L]),
                                 sb(f"mlo{mix}{z}", [L, L]), sb(f"mhi{mix}{z}", [L, L]), sb(f"kz{mix}{z}", [64, 128]))
        ps_open = [False] * 8
        ps_ptr = [0]

        def acquire(n):
            while True:
                free = [(ps_ptr[0] + i) % 8 for i in range(8) if not ps_open[(ps_ptr[0] + i) % 8]]
                if len(free) >= n:
                    sel = free[:n]
                    for i in sel:
                        ps_open[i] = True
                    ps_ptr[0] = (sel[-1] + 1) % 8
                    return [(pb[i], f"pb{i}", i) for i in sel]
                yield

        def release(*idx):
            for i in idx:
                ps_open[i] = False

        def stage_S_consts():
            for key, tiles in cst.items():
                for i in range(5):
                    P.dma("sp", "cst", DMA(tiles[i][:], consts_in[key][i]), writes=[f"cst{key}{i}"])
            P.dma("sp", "cst", DMA(mgbb[:], mgb_in.partition_broadcast(64)), writes=["mgbb"])
            P.dma("sp", "cst", DMA(gbb[:].rearrange("p a b c -> p (a b c)"), gb_in.partition_broadcast(64)),
                  writes=["gbb"])
            P.dma("sp", "cst", DMA(hgb[:].rearrange("p a c -> p (a c)"), hgam_in.partition_broadcast(64)),
                  writes=["hgb"])
            P.dma("sp", "cst", DMA(hgT[:], hgamT_in), writes=["hgT"])
            P.op("pool", lambda e: e.memset(onesr[:], 1.0), writes=["onesr"])
            P.dma("sp", "cst", DMA(mgbT[:], mgbT_in), writes=["mgbT"])
            P.dma("sp", "cst", DMA(mm0T[:], mM0T_in), writes=["mm0T"])
            for z in (0, 1):
                P.dma("sp", "cst", DMA(wup[:, z, :], gwup_in[0, z]), writes=["wup"])

        def hgrn_lb(l):
            if l == 0:
                P.op("dve", lambda e: e.memset(lbb[:, 0, :], 0.0), writes=["lbb"])
                P.op("dve", lambda e: e.memset(lbb[:, 1, :], 1.0), writes=["lbb"])
                P.op("dve", lambda e: e.memset(lbT[:, :, 0:1], -1.0), writes=["lbT"])
                P.op("dve", lambda e: e.memset(lbT[:, :, 1:2], 1.0), writes=["lbT"])
            else:
                P.op("dve", TT_(lbb[:, 0, :], hgb[:, 1, :], hgb[:, 0, :], ALU.subtract), reads=["hgb"], writes=["lbb"])
                P.op("act", ACT(lbb[:, 0, :], lbb[:, 0, :], AF.Sigmoid), reads=["lbb"], writes=["lbb"])
                P.op("dve", TS(lbb[:, 1, :], lbb[:, 0, :], -1.0, 1.0, ALU.mult, ALU.add), reads=["lbb"], writes=["lbb"])
                P.op("dve", TT_(lbT[:, :, 1], hgT[:, 0, :], hgT[:, 1, :], ALU.subtract), reads=["hgT"], writes=["lbT"])
                P.op("act", ACT(lbT[:, :, 1], lbT[:, :, 1], AF.Sigmoid), reads=["lbT"], writes=["lbT"])
                P.op("dve", TS(lbT[:, :, 0], lbT[:, :, 1], -1.0, None, ALU.mult), reads=["lbT"], writes=["lbT"])
                for z in (0, 1):
                    P.dma("sp", "wup", DMA(wup[:, z, :], gwup_in[l, z]), writes=["wup"])

        def mlstm_m_pass(l, z, si):
            B = SETS[si]
            liT, xfT, bT_, mrow, mcur, Eo4 = B["liT"], B["xfT"], B["bT_"], B["mrow"], B["mcur"], B["Eo4"]
            K = lambda n: f"{n}_{si}"
            order = range(NSEG) if z == 0 else range(NSEG - 1, -1, -1)
            bi = mgbT[:, (l * 2 + z) * 2 + 0:(l * 2 + z) * 2 + 1]
            bf = mgbT[:, (l * 2 + z) * 2 + 1:(l * 2 + z) * 2 + 2]
            P.op("dve", lambda e: e.memset(mcur[:], 0.0), writes=[K("mcur")])
            for seg in order:
                t0 = seg * SEGLEN
                P.dma("sp", K("liT"), DMA(liT[:], MGT[z * 8:z * 8 + 4, t0:t0 + SEGLEN]), writes=[K("liT")])
                P.dma("sp", K("xfT"), DMA(xfT[:], MGT[z * 8 + 4:z * 8 + 8, t0:t0 + SEGLEN]), writes=[K("xfT")])
                yield
                P.op("act", ACT(xfT[:], xfT[:], AF.Identity, bias=bf), reads=[K("xfT"), "mgbT"], writes=[K("xfT")])
                yield
                P.op("act", ACT(xfT[:], xfT[:], AF.Exp, scale=-1.0), reads=[K("xfT")], writes=[K("xfT")])
                yield
                P.op("act", ACT(xfT[:], xfT[:], AF.Ln, bias=1.0), reads=[K("xfT")], writes=[K("xfT")])
                yield
                for c in range(4):
                    cs = slice(c * 64, (c + 1) * 64)
                    P.op("dve", lambda e, cs=cs: e.tensor_tensor_scan(bT_[:, cs], onesr[:, cs], xfT[:, cs], 0.0,
                                                                      ALU.mult, ALU.add),
                         reads=[K("xfT"), "onesr"], writes=[K("bT_")])
                yield
                if z == 0:
                    P.op("dve", STT(liT[:], liT[:], bi, bT_[:], ALU.add, ALU.add), reads=[K("liT"), K("bT_"), "mgbT"],
                         writes=[K("liT")])
                else:
                    for c in range(4):
                        cs = slice(c * 64, (c + 1) * 64)
                        P.op("dve", STT(liT[:, cs], liT[:, cs], bT_[:, c * 64 + 63:c * 64 + 64], bT_[:, cs],
                                        ALU.add, ALU.subtract), reads=[K("liT"), K("bT_")], writes=[K("liT")])
                    yield
                    P.op("dve", STT(liT[:], liT[:], bi, xfT[:], ALU.add, ALU.add), reads=[K("liT"), K("xfT"), "mgbT"],
                         writes=[K("liT")])
                yield
                P.op("dve", lambda e: e.tensor_reduce(mrow[:, 0:4], liT[:].rearrange("p (c t) -> p c t", c=4), AX.X,
                                                      ALU.max), reads=[K("liT")], writes=[K("mrow")])
                P.op("dve", TS(mrow[:, 4:8], bT_[:].rearrange("p (c t) -> p c t", c=4)[:, :, 63], -1.0, None, ALU.mult),
                     reads=[K("bT_")], writes=[K("mrow")])
                yield
                P.op("dve", TS(mcur[:], mcur[:], fl[0:4, z, seg, 0:1], None, ALU.mult), reads=[K("mcur"), "fl"],
                     writes=[K("mcur")])
                yield
                P.op("dve", STT(mcur[:], mm0T[:, l * 2 + z:l * 2 + z + 1], fl[0:4, z, seg, 1:2], mcur[:], ALU.mult,
                                ALU.add), reads=[K("mcur"), "fl", "mm0T"], writes=[K("mcur")])
                yield
                corder = range(4) if z == 0 else range(3, -1, -1)
                for c in corder:
                    P.op("dve", STT(mcur[:], mcur[:], mrow[:, c:c + 1], mrow[:, 4 + c:5 + c], ALU.max, ALU.add),
                         reads=[K("mcur"), K("mrow")], writes=[K("mcur")])
                    yield
                P.dma("sp", K("mst"), DMA(o_mM[l, seg, z, :].rearrange("(h o) -> h o", o=1), mcur[:]),
                      reads=[K("mcur")], writes=[f"d_mm{z}_{seg}"])
                yield
                P.dma("sp", K("Eo4"), DMA(Eo4[:, seg, :], o_mM[l, seg:seg + 1, z, :].partition_broadcast(64)),
                      reads=[f"d_mm{z}_{seg}"], writes=[K("Eo4")])
                yield
            P.op("act", ACT(Eo4[:], Eo4[:], AF.Exp, scale=-1.0), reads=[K("Eo4")], writes=[K("Eo4")])
            yield

        def state_dram(ap5, l, z):
            return ap5[l, z].rearrange("h d e -> d h e")

        def scan_chain(l, mix, z, si):
            B = SETS[si]
            K = lambda n: f"{n}_{si}"
            lat, sgk = B["lat"], B["sgk"]
            Gtk = lat
            Sint, Stm, Sini, Sst, Sbf, oev, dn = (B[n] for n in ("Sint", "Stm", "Sini", "Sst", "Sbf", "oev", "dn"))
            GcQ, GcK, KTs, KTz, kTc = (B[n] for n in ("GcQ", "GcK", "KTs", "KTz", "kTc"))
            spm, eli, emi4, Eo4 = (B[n] for n in ("spm", "eli", "emi4", "Eo4"))
            pre_done = [0]
            seq_done = [0]
            L = MIXL[mix]
            nch = 128 // L
            E = 256 if mix == "m" else 128
            wcat, wm, mlo, mhi, kzm = cst[(mix, z)]
            ck = [f"cst{(mix, z)}{i}" for i in range(5)]
            qrow = {"m": 0, "g": 512, "h": 1024}[mix]
            krow = {"m": 256, "g": 768}.get(mix)
            kcol = {"m": 0, "g": 256}.get(mix)
            vcol = {"m": 0, "g": 512, "h": 1024}[mix]
            SE = lambda t_: t_[:, :, 0:E]
            order = list(range(T // 128)) if z == 0 else list(range(T // 128 - 1, -1, -1))
            if DBG_NSUB is not None:
                order = order[:DBG_NSUB]
            def pre():
                for ji, j in enumerate(order):
                    while seq_done[0] < ji - 1:
                        yield
                    tok0 = j * 128
                    seg = j // 2
                    jb = ji % 2
                    qTt, kTt, hfTt, ktok, vtk = B["qTt"][jb], B["kin"][jb], B["kin"][jb], B["ktok"][jb], B["vtk"][jb]
                    mgk, glrTt = B["mgk"][jb], B["glrTt"][jb]
                    V2, Ktk, ATm, QTs, ABt = B["V2"][jb], B["Ktk"][jb], B["ATm"][jb], B["QTs"][jb], B["ABt"][jb]
                    KV2, KKtk, KATm, KQTs, KABt = K(f"V2{jb}"), K(f"Ktk{jb}"), K(f"ATm{jb}"), K(f"QTs{jb}"), K(f"ABt{jb}")
                    KQ, KK, KT, KV = K(f"qTt{jb}"), K(f"kin{jb}"), K(f"ktok{jb}"), K(f"vtk{jb}")
                    P.dma("sp", KQ, DMA(qTt[:], QKT[qrow:qrow + 256, tok0:tok0 + 128].rearrange("(h d) t -> d h t", d=64)),
                          writes=[KQ])
                    if mix != "h":
                        P.dma("sp", KK, DMA(kTt[:], QKT[krow:krow + 256, tok0:tok0 + 128].rearrange(
                            "(h d) t -> d h t", d=64)), writes=[KK])
                        P.dma("sp", KT, DMA(ktok[0:L, 0:nch, :],
                                            KTOK[tok0:tok0 + 128, kcol:kcol + 256].rearrange("(c l) f -> l c f", l=L)),
                              writes=[KT])
                    else:
                        P.dma("sp", KK, DMA(hfTt[:], HFT[z * 256:(z + 1) * 256, tok0:tok0 + 128].rearrange(
                            "(h d) t -> d h t", d=64)), writes=[KK])
                        P.dma("sp", KT, DMA(ktok[0:L, 0:nch, :],
                                            HFTOK[tok0:tok0 + 128, z * 256:(z + 1) * 256].rearrange(
                                                "(c l) f -> l c f", l=L)), writes=[KT])
                    P.dma("sp", KV, DMA(vtk[0:L, 0:nch, :],
                                        VT[tok0:tok0 + 128, vcol:vcol + 512].rearrange("(c l) f -> l c f", l=L)),
                          writes=[KV])
                    yield
                    latv = lat[0:L, 0:nch, :]
                    if mix == "g":
                        P.dma("sp", K(f"glrTt{jb}"), DMA(glrTt[:], GLRT[z * 16:(z + 1) * 16, tok0:tok0 + 128]),
                              writes=[K(f"glrTt{jb}")])
                        ((bank, bk, bi_),) = yield from acquire(1)
                        pla = bank[0:L, :].rearrange("p (c f) -> p c f", c=nch)
                        P.pe_group([MM(pla[:, c, :], glrTt[0:16, c * L:(c + 1) * L], wup[:, z, :]) for c in range(nch)],
                                   reads=[K(f"glrTt{jb}"), "wup"], writes=[bk])
                        yield
                        P.op("dve", TT_(latv, pla, gbb[0:L, l, z, :].unsqueeze(1).to_broadcast([L, nch, 256]), ALU.add),
                             reads=[bk, "gbb"], writes=[K("lat")])
                        release(bi_)
                        yield
                        P.op("act", ACT(latv, latv, AF.Exp, scale=-1.0), reads=[K("lat")], writes=[K("lat")])
                        yield
                        P.op("act", ACT(latv, latv, AF.Ln, bias=1.0), reads=[K("lat")], writes=[K("lat")])
                        yield
                        vsrc = vtk[0:L, 0:nch, :].rearrange("p c (h e) -> p (c h) e", h=4)
                        ksrc_tok = ktok[0:L, 0:nch, :]
                        ksrc_T = kTt
                        vkeys = [KV]
                        ktk_reads = [KT]
                        kT_reads = [KK]
                    elif mix == "h":
                        sgv = sgk[0:L, 0:nch, :]
                        hfin = ktok[0:L, 0:nch, :]
                        lb_b = lbb[0:L, 0, :].unsqueeze(1).to_broadcast([L, nch, 256])
                        oml_b = lbb[0:L, 1, :].unsqueeze(1).to_broadcast([L, nch, 256])
                        P.op("act", ACT(sgv, hfin, AF.Sigmoid), reads=[KT], writes=[K("sgk")])
                        P.op("act", ACT(kTc[:], hfTt[:], AF.Sigmoid), reads=[KK], writes=[K("kTc")])
                        yield
                        P.op("dve", TT_(latv, sgv, oml_b, ALU.mult), reads=[K("sgk"), "lbb"], writes=[K("lat")])
                        P.op("pool", TT_(kTc[:], kTc[:], lbT[:, :, 0:1].to_broadcast([64, 4, 128]), ALU.mult),
                             reads=[K("kTc"), "lbT"], writes=[K("kTc")])
                        yield
                        P.op("dve", TT_(latv, latv, lb_b, ALU.add), reads=[K("lat"), "lbb"], writes=[K("lat")])
                        P.op("pool", TT_(kTc[:], kTc[:], lbT[:, :, 1:2].to_broadcast([64, 4, 128]), ALU.add),
                             reads=[K("kTc"), "lbT"], writes=[K("kTc")])
                        yield
                        P.op("act", ACT(latv, latv, AF.Ln), reads=[K("lat")], writes=[K("lat")])
                        P.op("pool", TS(sgv, sgv, -1.0, 1.0, ALU.mult, ALU.add), reads=[K("sgk"), K("lat")],
                             writes=[K("sgk")])
                        yield
                        P.op("pool", TT_(sgv, sgv, oml_b, ALU.mult), reads=[K("sgk"), "lbb"], writes=[K("sgk")])
                        yield
                        vsrc = vtk[0:L, 0:nch, :].rearrange("p c (h e) -> p (c h) e", h=4)
                        ksrc_tok = sgv
                        ksrc_T = kTc
                        vkeys = [KV]
                        ktk_reads = [K("sgk")]
                        kT_reads = [K("kTc")]
                    else:
                        P.dma("sp", K(f"mgk{jb}"), DMA(mgk[0:L, 0:nch, :],
                                                       MGTOK[tok0:tok0 + 128, :].rearrange("(c l) f -> l c f", l=L)),
                              writes=[K(f"mgk{jb}")])
                        yield
                        P.op("dve", TT_(mgk[:], mgk[:], mgbb[0:L, l * 16:(l + 1) * 16].unsqueeze(1).to_broadcast(
                            [L, nch, 16]), ALU.add), reads=[K(f"mgk{jb}"), "mgbb"], writes=[K(f"mgk{jb}")])
                        yield
                        P.op("act", ACT(spm[:], mgk[:, :, z * 8 + 4:z * 8 + 8], AF.Exp, scale=-1.0), reads=[K(f"mgk{jb}")],
                             writes=[K("spm")])
                        P.op("act", ACT(eli[:], mgk[:, :, z * 8:z * 8 + 4], AF.Exp), reads=[K(f"mgk{jb}")], writes=[K("eli")])
                        yield
                        P.op("act", ACT(spm[:], spm[:], AF.Ln, bias=1.0), reads=[K("spm")], writes=[K("spm")])
                        elb = eli[:].rearrange("p c h -> p (c h)").unsqueeze(2).to_broadcast([L, nch * 4, 128])
                        P.op("pool", TT_(V2[0:L, 0:nch * 4, 0:128],
                                         vtk[0:L, 0:nch, :].rearrange("p c (h e) -> p (c h) e", h=4), elb, ALU.mult),
                             reads=[KV, K("eli")], writes=[KV2])
                        yield
                        P.op("dve", CP(lat[0:L, 0:nch, :].rearrange("p c (h d) -> p (c h) d", h=4),
                                       spm[:].rearrange("p c h -> p (c h)").unsqueeze(2).to_broadcast([L, nch * 4, 64])),
                             reads=[K("spm")], writes=[K("lat")])
                        P.op("pool", CP(V2[0:L, 0:nch * 4, 128:256], elb), reads=[K("eli")], writes=[KV2])
                        yield
                        vsrc = V2[0:L, 0:nch * 4, :]
                        ksrc_tok = ktok[0:L, 0:nch, :]
                        ksrc_T = kTt
                        vkeys = [KV2]
                        ktk_reads = [KT]
                        kT_reads = [KK]
                    nhalf = max(1, nch // 2)
                    got = yield from acquire(2 + nhalf)
                    (bank, bk, gi_), (bank2, bk2, ai_) = got[0], got[1]
                    GTp = bank[0:64, 0:4 * nch * L].rearrange("p (h c w) -> p h c w", h=4, c=nch)
                    fns = []
                    for h in range(4):
                        for c in range(nch):
                            fns.append(MM(GTp[:, h, c, :], lat[0:L, c, h * 64:(h + 1) * 64], wcat[:, 0:L]))
                    P.pe_group(fns, reads=[K("lat"), ck[0]], writes=[bk])
                    ABp = bank2[0:64, 0:4 * nch * 2].rearrange("p (h c w) -> p h c w", h=4, c=nch)
                    fns = []
                    for h in range(4):
                        for c in range(nch):
                            fns.append(MM(ABp[:, h, c, :], lat[0:L, c, h * 64:(h + 1) * 64], wcat[:, L:L + 2]))
                    P.pe_group(fns, reads=[K("lat"), ck[0]], writes=[bk2])
                    gtoks = []
                    for half in range(nhalf):
                        ncc = min(2, nch)
                        bank3, bk3, ti_ = got[2 + half]
                        pgk = bank3[0:L, 0:ncc * 256]
                        P.pe_group([MM(pgk, wm[:, :], lat[0:L, half * 2:half * 2 + ncc, :].rearrange("p c f -> p (c f)"))],
                                   reads=[K("lat"), ck[1]], writes=[bk3])
                        gtoks.append((pgk, half * 2, ncc, bk3, ti_))
                    yield
                    GQ = GcQ[:].rearrange("p h (c t) -> p h c t", c=nch)
                    GK = GcK[:].rearrange("p h (c t) -> p h c t", c=nch)
                    P.op("act", ACT(GQ, GTp, AF.Exp), reads=[bk], writes=[K("GcQ")])
                    yield
                    P.op("act", ACT(GK, GTp, AF.Exp, scale=-1.0), reads=[bk], writes=[K("GcK")])
                    release(gi_)
                    P.op("dve", TT_(QTs[:], qTt[:], GcQ[:], ALU.mult), reads=[KQ, K("GcQ")], writes=[KQTs])
                    yield
                    ABv = ABt[:, :, 0:nch, :]
                    P.op("act", ACT(ABv, ABp, AF.Exp), reads=[bk2], writes=[KABt])
                    release(ai_)
                    P.op("dve", TT_(KTs[:], ksrc_T[:], GcK[:], ALU.mult), reads=kT_reads + [K("GcK")], writes=[K("KTs")])
                    yield
                    for (pgk, c0, ncc, bk3, ti_) in gtoks:
                        P.op("act", ACT(Gtk[0:L, c0:c0 + ncc, :].rearrange("p c f -> p (c f)"), pgk, AF.Exp, scale=-1.0),
                             reads=[bk3], writes=[K("lat")])
                        release(ti_)
                    P.op("pool", TT_(KTz[:], KTs[:], kzm[:, :].unsqueeze(1).to_broadcast([64, 4, 128]), ALU.mult),
                         reads=[K("KTs"), ck[4]], writes=[K("KTz")])
                    yield
                    P.op("pool", TT_(Ktk[0:L, 0:nch, :], ksrc_tok, Gtk[0:L, 0:nch, :], ALU.mult),
                         reads=ktk_reads + [K("lat")], writes=[KKtk])
                    ((bank4, bk4, pi_),) = yield from acquire(1)
                    ATp = bank4[0:L, 0:nch * 4 * L].rearrange("p (a t) -> p a t", t=L)
                    fns = []
                    H2 = L // 2
                    for c in range(nch):
                        for h in range(4):
                            full_t = slice(c * L + H2, (c + 1) * L) if z == 0 else slice(c * L, c * L + H2)
                            zero_t = slice(c * L, c * L + H2) if z == 0 else slice(c * L + H2, (c + 1) * L)
                            fo = (H2, L) if z == 0 else (0, H2)
                            zo = (0, H2) if z == 0 else (H2, L)
                            fns.append(MM(ATp[:, c * 4 + h, fo[0]:fo[1]], KTs[:, h, c * L:(c + 1) * L], QTs[:, h, full_t]))
                            fns.append(MM(ATp[:, c * 4 + h, zo[0]:zo[1]], KTz[:, h, c * L:(c + 1) * L], QTs[:, h, zero_t]))
                    P.pe_group(fns, reads=[K("KTs"), K("KTz"), KQTs], writes=[bk4])
                    yield
                    ATmv = ATm[0:L, 0:nch * 4, 0:L]
                    P.op("dve", TT_(ATmv, ATp, mhi[:, :].unsqueeze(1).to_broadcast([L, nch * 4, L]), ALU.mult),
                         reads=[bk4, ck[3]], writes=[KATm])
                    release(pi_)
                    yield

                    pre_done[0] = ji + 1
                    yield

            def seq():
                P.op("dve", lambda e: e.memset(Sst[:], 0.0), writes=[K("S")])
                if mix == "m":
                    P.dma("sp", K("sini"), DMA(Sini[:, :, 0:128], state_dram(mC0, l, z)), writes=[K("Sini")])
                    P.dma("sp", K("sini"), DMA(Sini[:, :, 128:256], state_dram(mN0, l, z)), writes=[K("Sini")])
                    P.dma("sp", K("emi4"), DMA(emi4[:], mM0[0:1, (l * 2 + z) * 4:(l * 2 + z) * 4 + 4].partition_broadcast(
                        64)), writes=[K("emi4")])
                    P.op("act", ACT(emi4[:], emi4[:], AF.Exp), reads=[K("emi4")], writes=[K("emi4")])
                    P.op("dve", TT_(Sini[:], Sini[:], emi4[:].unsqueeze(2).to_broadcast([64, 4, 256]), ALU.mult),
                         reads=[K("Sini"), K("emi4")], writes=[K("Sini")])
                else:
                    P.dma("sp", K("sini"), DMA(Sini[:, :, 0:128], state_dram(gS0 if mix == "g" else hS0, l, z)),
                          writes=[K("Sini")])
                yield

                for ji, j in enumerate(order):
                    while pre_done[0] <= ji:
                        yield
                    tok0 = j * 128
                    seg = j // 2
                    jb = ji % 2
                    V2, Ktk, ATm, QTs, ABt = B["V2"][jb], B["Ktk"][jb], B["ATm"][jb], B["QTs"][jb], B["ABt"][jb]
                    KV2, KKtk, KATm, KQTs, KABt = K(f"V2{jb}"), K(f"Ktk{jb}"), K(f"ATm{jb}"), K(f"QTs{jb}"), K(f"ABt{jb}")
                    vtk = B["vtk"][jb]
                    if mix == "m":
                        vsrc = V2[0:L, 0:nch * 4, :]
                        vkeys = [KV2]
                    else:
                        vsrc = vtk[0:L, 0:nch, :].rearrange("p c (h e) -> p (c h) e", h=4)
                        vkeys = [K(f"vtk{jb}")]
                    ATmv = ATm[0:L, 0:nch * 4, 0:L]
                    ob = oev[j % 2]
                    okey = K(f"oev{j % 2}")
                    corder = list(range(nch)) if z == 0 else list(range(nch - 1, -1, -1))
                    for ci, c in enumerate(corder):
                        first_of_seg = (ci == 0) and ((j % 2 == 0) if z == 0 else (j % 2 == 1))
                        last_of_seg = (ci == nch - 1) and ((j % 2 == 1) if z == 0 else (j % 2 == 0))
                        if first_of_seg:
                            P.op("dve", TS(SE(Sst), SE(Sst), fl[0:64, z, seg, 0:1], None, ALU.mult),
                                 reads=[K("S"), "fl"], writes=[K("S")])
                            yield
                            P.op("dve", STT(SE(Sst), SE(Sini), fl[0:64, z, seg, 1:2], SE(Sst), ALU.mult,
                                            ALU.add), reads=[K("S"), "fl", K("Sini")], writes=[K("S")])
                            yield
                        P.op("dve", TT_(SE(Sint), SE(Sst), ABt[:, :, c, 0:1].to_broadcast([64, 4, E]), ALU.mult),
                             reads=[K("S"), KABt], writes=[K("Sint")])
                        pbanks = []
                        fns = []
                        if E == 256:
                            (b1, k1, i1_), (b2, k2, i2_), (bo, bok, io_) = yield from acquire(3)
                            Pp = [b1[0:64, :].rearrange("p (a e) -> p a e", a=2), b2[0:64, :].rearrange("p (a e) -> p a e", a=2)]
                            ppk = [k1, k2]
                            pidx = [i1_, i2_]
                        else:
                            (b1, k1, i1_), (bo, bok, io_) = yield from acquire(2)
                            Pp = [b1[0:64, 0:256].rearrange("p (a e) -> p a e", a=2),
                                  b1[0:64, 256:512].rearrange("p (a e) -> p a e", a=2)]
                            ppk = [k1, k1]
                            pidx = [i1_]
                        for h in range(4):
                            fns.append(MM(Pp[h // 2][:, h % 2, :], Ktk[0:L, c, h * 64:(h + 1) * 64], vsrc[:, c * 4 + h, 0:E]))
                        P.pe_group(fns, reads=vkeys + [KKtk], writes=list(set(ppk)))
                        yield
                        P.op("act", ACT(SE(Sbf), SE(Sint), AF.Copy), reads=[K("Sint")], writes=[K("Sbf")])
                        if E == 128:
                            P.op("dve", TT_(Stm[:, :, 0:E], b1[0:64, :].rearrange("p (a e) -> p a e", a=4),
                                            Sint[:, :, 0:E], ALU.add), reads=[k1, K("Sint")], writes=[K("Stm")])
                        else:
                            for a in range(2):
                                P.op("dve", TT_(Stm[:, 2 * a:2 * a + 2, 0:E], Pp[a], Sint[:, 2 * a:2 * a + 2, 0:E],
                                                ALU.add), reads=[ppk[a], K("Sint")], writes=[K("Stm")])
                        release(*pidx)
                        yield
                        P.op("dve", TT_(SE(Sst), SE(Stm), ABt[:, :, c, 1:2].to_broadcast([64, 4, E]), ALU.mult),
                             reads=[K("Stm"), KABt], writes=[K("S")])
                        if mix == "m":
                            OC = bo[:, :].rearrange("p (h w t) -> p h w t", h=4, w=2)
                        else:
                            OC = bo[:, 0:4 * L].rearrange("p (h w t) -> p h w t", h=4, w=1)
                        fns = []
                        cols = slice(c * L, (c + 1) * L)
                        for h in range(4):
                            fns.append(MM(OC[:, h, 0, :], vsrc[:, c * 4 + h, 0:128], ATmv[:, c * 4 + h, :], start=True,
                                          stop=False))
                            fns.append(MM(OC[:, h, 0, :], Sbf[:, h, 0:128], QTs[:, h, cols], start=False, stop=True))
                            if mix == "m":
                                fns.append(MM(OC[:, h, 1, :], vsrc[:, c * 4 + h, 128:256], ATmv[:, c * 4 + h, :],
                                              start=True, stop=False))
                                fns.append(MM(OC[:, h, 1, :], Sbf[:, h, 128:256], QTs[:, h, cols], start=False,
                                              stop=True))
                        P.pe_group(fns, reads=vkeys + [KATm, K("Sbf"), KQTs], writes=[bok])
                        yield
                        if mix == "m":
                            dnv = dn[:, :, 0:L]
                            P.op("dve", TS(dnv, OC[:, :, 1, :], -1.0, 1.0, ALU.mult, ALU.max), reads=[bok], writes=[K("dn")])
                            yield
                            P.op("dve", TT_(dnv, OC[:, :, 1, :], dnv, ALU.max), reads=[bok, K("dn")], writes=[K("dn")])
                            yield
                            P.op("dve", lambda e, dnv=dnv: e.reciprocal(dnv, dnv), reads=[K("dn")], writes=[K("dn")])
                            yield
                            P.op("dve", TT_(ob[:, :, cols], OC[:, :, 0, :], dnv, ALU.mult), reads=[bok, K("dn")],
                                 writes=[okey])
                        else:
                            P.op("act", ACT(ob[:, :, cols], OC[:, :, 0, :], AF.Copy), reads=[bok], writes=[okey])
                        release(io_)
                        if last_of_seg:
                            if mix == "m":
                                P.op("dve", TT_(Stm[:], Sst[:], Eo4[:, seg, :].unsqueeze(2).to_broadcast([64, 4, 256]),
                                                ALU.mult), reads=[K("S"), K("Eo4")], writes=[K("Stm")])
                                P.dma("pool", K("stoC"), DMA(state_dram(o_mC[:, seg], l, z), Stm[:, :, 0:128]),
                                      reads=[K("Stm")], writes=["d_so"])
                                for h in range(4):
                                    dstn = o_mN[l, seg, z, h, :].rearrange("(d o) -> d o", o=1)
                                    P.dma("pool", K(f"stoN{h}"), DMA(dstn, Stm[:, h, 128:129]), reads=[K("Stm")],
                                          writes=["d_so"])
                            else:
                                od = o_gS if mix == "g" else o_hS
                                P.dma("pool", K("stoS"), DMA(state_dram(od[:, seg], l, z), Sst[:, :, 0:128]), reads=[K("S")],
                                      writes=["d_so"])
                        yield
                    r0 = MIXI[mix] * 512
                    P.dma("pool", okey, DMA(OTs[z, r0:r0 + 512, tok0:tok0 + 128].rearrange("(h e) t -> e h t", e=128), ob[:]),
                          reads=[okey], writes=["d_ot"])
                    seq_done[0] = ji + 1
                    yield


            return pre(), seq()

        def conv_gen(l_):
            items = []
            if l_ == 0:
                items += [("fin", 0, 1), ("fout", 0, 1), ("br", 0, 0), ("br", 0, 1), ("br", 0, 2), ("out", 0, None)]
                items += [("fin", 1, 0), ("fout", 1, 0), ("in", 1, None), ("fin", 1, 1), ("fout", 1, 1),
                          ("br", 1, 0), ("br", 1, 1), ("br", 1, 2), ("out", 1, None)]
            n = 0
            for (name, l2, idx) in items:
                src = WF[name][l2] if idx is None else WF[name][l2, idx]
                dst = WB[name][l2] if idx is None else WB[name][l2, idx]
                rows = src.shape[0]
                for r0 in range(0, rows, 128):
                    r1 = min(rows, r0 + 128)
                    P.dma("pool", f"cv{n % 4}", DMA(dst[r0:r1, :], src[r0:r1, :]), writes=[f"wb_{name}{l2}{idx}"])
                    n += 1
                    for _ in range(14 if l_ == 0 else 1):
                        yield

        def lockstep(gens, background=None):
            gens = list(gens)
            bg = background if background is not None else []
            while gens:
                for g in list(gens):
                    try:
                        next(g)
                    except StopIteration:
                        gens.remove(g)
                for g in list(bg):
                    try:
                        next(g)
                    except StopIteration:
                        bg.remove(g)

        def stage_S(l):
            hgrn_lb(l)
            bg = []
            if not DBG_SKIP_A and l == 0:
                bg.append(conv_gen(l))
            mp = []
            if "mpass" in DBG_SPARTS:
                mp = [mlstm_m_pass(l, 0, 0), mlstm_m_pass(l, 1, 1)]
            bg_all = bg + mp
            for mix in ("g", "h", "m"):
                if mix == "m" and mp:
                    rest = [g for g in mp if g in bg_all]
                    lockstep(rest)
                    for g in rest:
                        if g in bg_all:
                            bg_all.remove(g)
                if mix in DBG_SPARTS:
                    p0, s0 = scan_chain(l, mix, 0, 0)
                    p1, s1 = scan_chain(l, mix, 1, 1)
                    lockstep([p0, s0, p1, s1], background=bg_all)
            lockstep(list(bg_all))

        bcnt = [0]

        def B_phase1(l, tt, yb):
            yTb = yTbs[yb]
            tok0 = tt * TT
            for i in range(3):
                for h in range(4):
                    n = bcnt[0] % 2
                    bcnt[0] += 1
                    r0 = i * 512 + h * 128
                    o0, o1 = ofb[2 * n], ofb[2 * n + 1]
                    P.dma("sp", f"ofb{2 * n}", DMA(o0[:], OTs[0, r0:r0 + 128, tok0:tok0 + TT]),
                          writes=[f"ofb{2 * n}"])
                    P.dma("sp", f"ofb{2 * n + 1}", DMA(o1[:], OTs[1, r0:r0 + 128, tok0:tok0 + TT]),
                          writes=[f"ofb{2 * n + 1}"])
                    P.dma("sp", f"gtb2{n}", DMA(gtb2[n][:], GATES[r0:r0 + 128, tok0:tok0 + TT]),
                          writes=[f"gtb2{n}"])
                    P.op("pool", TT_(o0[:], o0[:], o1[:], ALU.add), reads=[f"ofb{2 * n}", f"ofb{2 * n + 1}"],
                         writes=[f"ofb{2 * n}"])
                    P.op("act", ACT(sq2[n][:], o0[:], AF.Square), reads=[f"ofb{2 * n}"], writes=[f"sq2{n}"])
                    ph = pb[7]
                    P.pe_group([MM(ph[:, :], ones_b[:], sq2[n][:])], reads=[f"sq2{n}", "ones_b"], writes=["pb7"])
                    P.op("dve", TS(tmp2[n][:], ph[:, :], 1.0 / 128, EPS, ALU.mult, ALU.add), reads=["pb7"],
                         writes=[f"tmp2{n}"])
                    P.op("act", ACT(tmp2[n][:], tmp2[n][:], AF.Sqrt), reads=[f"tmp2{n}"], writes=[f"tmp2{n}"])
                    P.op("dve", lambda e, n=n: e.reciprocal(tmp2[n][:], tmp2[n][:]), reads=[f"tmp2{n}"],
                         writes=[f"tmp2{n}"])
                    P.op("pool", TT_(o0[:], o0[:], tmp2[n][:], ALU.mult), reads=[f"ofb{2 * n}", f"tmp2{n}"],
                         writes=[f"ofb{2 * n}"])
                    P.op("dve", STT(yTb[:, i * 4 + h, :], o0[:], hnorm[:, l, i, h:h + 1], gtb2[n][:], ALU.mult,
                                    ALU.mult), reads=[f"ofb{2 * n}", "hnorm", f"gtb2{n}"],
                         writes=[f"yTb{yb}_{i * 4 + h}"])

        def stage_B(l):
            cap = P.capture(lambda: B_phase1(l, 0, 0))
            P.commit(cap)
            for tt in range(NT):
                capR = P.capture(lambda: B_rest(l, tt, tt % 2))
                capP = P.capture(lambda: B_phase1(l, tt + 1, (tt + 1) % 2)) if tt + 1 < NT else []
                P.commit(capR, capP)

        def B_rest(l, tt, yb):
            dst = X3 if l == 0 else yT_out
            yTb = yTbs[yb]
            if True:
                tok0 = tt * TT
                load_x(X1, tok0)
                for i in range(3):
                    slot = ring_next()
                    wv = load_w(slot, [(wsrc("br", l, i)[0], 0, D, 0)], 4, D, bf=True)
                    for dc in range(KC):
                        n = bcnt[0] % 2
                        bcnt[0] += 1
                        pp = pb[5 + n]
                        fns = [MM(pp[:, :], wv[:, h, dc * 128:(dc + 1) * 128], yTb[:, i * 4 + h, :], start=(h == 0),
                                  stop=(h == 3)) for h in range(4)]
                        P.pe_group(fns, reads=[f"ring{slot}"] + [f"yTb{yb}_{i * 4 + h}" for h in range(4)],
                                   writes=[f"pb{5 + n}"])
                        r0 = 1536 + i * D + dc * 128
                        P.dma("sp", f"gtb{n}", DMA(gtb[n][:], GATES[r0:r0 + 128, tok0:tok0 + TT]), reads=["d_gates"],
                              writes=[f"gtb{n}"])
                        if i == 0:
                            P.op("dve", TT_(yF[:, dc, :], pp[:, :], gtb[n][:], ALU.mult),
                                 reads=[f"pb{5 + n}", f"gtb{n}"], writes=[f"yF{dc}"])
                        else:
                            P.op("dve", TT_(tmp[n][:], pp[:, :], gtb[n][:], ALU.mult), reads=[f"pb{5 + n}", f"gtb{n}"],
                                 writes=[f"tmp{n}"])
                            P.op("pool", TT_(yF[:, dc, :], yF[:, dc, :], tmp[n][:], ALU.add),
                                 reads=[f"yF{dc}", f"tmp{n}"], writes=[f"yF{dc}"])
                for k in range(KC):
                    P.op("act", ACT(hT[:, k, :], yF[:, k, :], AF.Copy), reads=[f"yF{k}"], writes=[f"hT{k}"])
                for ob_ in range(2):
                    slot = ring_next()
                    wv = load_w(slot, [(wsrc("out", l)[0], ob_ * 512, 512, 0)], KC, 512, bf=True)
                    for c in range(4):
                        dc = ob_ * 4 + c
                        n = bcnt[0] % 2
                        bcnt[0] += 1
                        pp = pb[5 + n]
                        fns = [MM(pp[:, :], wv[:, k, c * 128:(c + 1) * 128], hT[:, k, :], start=(k == 0),
                                  stop=(k == KC - 1)) for k in range(KC)]
                        P.pe_group(fns, reads=[f"ring{slot}"] + hkeys, writes=[f"pb{5 + n}"])
                        if dc >= 1:
                            stats_chunk(yF, "yF", dc - 1, KC)
                        P.op("act", ACT(yF[:, dc, :], pp[:, :], AF.Copy), reads=[f"pb{5 + n}"], writes=[f"yF{dc}"])
                stats_chunk(yF, "yF", KC - 1, KC)
                resid_update(l, 1, stats_done=True, next_stats=True)
                ffn(l, 1, 2, tok0, have_stats=True)
                store_x(dst, tok0, "d_x")

        stage_S_consts()
        P.barrier()
        stage0()
        done = False
        for l in layers:
            stq[0] = "sp" if l == 0 else "pool"
            if not DBG_SKIP_A:
                stage_A(l)
            P.barrier()
            if stop_after == ("A", l):
                break
            stage_S(l)
            P.barrier()
            if stop_after == ("S", l):
                break
            stq[0] = "pool"
            stage_B(l)
            P.barrier()
        P.finish(block)
        print("instructions:", P.n_inst, "dma sems:", len(P.dma_sems))
    return nc


_CACHE = {}


def make_in_maps(inputs):
    f = lambda a: np.ascontiguousarray(np.asarray(a, dtype=np.float32))
    x_prompt, x_sample = f(inputs["x_prompt"]), f(inputs["x_sample"])
    c, c_ctx = f(inputs["c"]), f(inputs["c_ctx"])
    common = {
        "w_ada": f(inputs["w_ada"]),
        "bada": f(np.transpose(f(inputs["b_ada"]).reshape(2, 72, 128), (2, 0, 1))),
        "npre": f(np.transpose(f(inputs["norm_pre"]).reshape(2, 3, KC, 128), (3, 0, 1, 2))),
        "npost": f(np.transpose(f(inputs["norm_post"]).reshape(2, 3, KC, 128), (3, 0, 1, 2))),
        "hnorm": f(np.transpose(f(inputs["head_norm"]).reshape(2, 3, 4, 128), (3, 0, 1, 2))),
        "w_ffn_in": f(inputs["w_ffn_in"]), "w_ffn_out": f(inputs["w_ffn_out"]), "w_in": f(inputs["w_in"]),
        "w_branch": f(inputs["w_branch"]), "w_out": f(inputs["w_out"]),
        "mgb": f(inputs["mlstm_gate_bias"]).reshape(1, 32),
        "mgbT": f(f(inputs["mlstm_gate_bias"]).reshape(8, 4).T),
        "gla_w_up": f(inputs["gla_w_up"]),
        "gla_b": f(inputs["gla_b"]).reshape(1, 1024),
        "hgam": f(inputs["hgrn_gamma"]).reshape(1, 512),
        "hgamT": f(np.transpose(f(inputs["hgrn_gamma"]).reshape(2, 4, 64), (2, 0, 1))),
        "ones": np.ones((128, 128), np.float32),
    }
    for mix, L in MIXL.items():
        scale = {"m": -1.0, "g": -1.0 / 16.0, "h": 1.0}[mix]
        for z in (0, 1):
            wcat, wm, mlo, mhi, kz = scan_consts(L, z, scale)
            common[f"kz_{mix}{z}"] = kz
            common[f"wcat_{mix}{z}"] = wcat
            common[f"wm_{mix}{z}"] = wm
            common[f"mlo_{mix}{z}"] = mlo
            common[f"mhi_{mix}{z}"] = mhi
    pos_s = grid_position_T(T)
    zeros_pos = np.zeros((D, T), np.float32)
    maps = []
    for core in range(8):
        m = dict(common)
        fl = np.zeros((2, NSEG, 2), np.float32)
        if core < 4:
            b = core
            m["xT"] = f(x_sample[b].T)
            m["posT"] = pos_s
            m["cT"] = f(c[b].reshape(KC, 128).T)
            m["mC0"] = f(np.transpose(f(inputs["state_mlstm_C"])[b], (0, 1, 2, 3, 4)))
            n0 = f(inputs["state_mlstm_n"])[b]
            m["mN0"] = f(np.repeat(n0[..., None], 128, axis=-1))
            m["mM0"] = f(inputs["state_mlstm_m"])[b].reshape(1, 16)
            m["mM0T"] = f(f(inputs["state_mlstm_m"])[b].reshape(4, 4).T)
            m["gS0"] = f(inputs["state_gla_S"])[b]
            m["hS0"] = f(inputs["state_hgrn_S"])[b]
            fl[0, :, 0] = 1.0
            fl[0, 0, 0] = 0.0
            fl[0, 0, 1] = 1.0
            fl[1, :, 0] = 1.0
            fl[1, NSEG - 1, 0] = 0.0
            fl[1, NSEG - 1, 1] = 1.0
        else:
            j = core - 4
            xp = np.zeros((T, D), np.float32)
            xp[:2048] = x_prompt[8 * j:8 * j + 8].reshape(2048, D)
            m["xT"] = f(xp.T)
            m["posT"] = zeros_pos
            m["cT"] = f(c_ctx.reshape(KC, 128).T)
            m["mC0"] = np.zeros((2, 2, 4, 64, 128), np.float32)
            m["mN0"] = np.zeros((2, 2, 4, 64, 128), np.float32)
            m["mM0"] = np.zeros((1, 16), np.float32)
            m["mM0T"] = np.zeros((4, 4), np.float32)
            m["gS0"] = np.zeros((2, 2, 4, 64, 128), np.float32)
            m["hS0"] = np.zeros((2, 2, 4, 64, 128), np.float32)
        m["flags"] = fl.reshape(1, -1)
        maps.append(m)
    return maps


def kernel(**inputs):
    if "nc" not in _CACHE:
        _CACHE["nc"] = build()
    nc = _CACHE["nc"]
    maps = make_in_maps(inputs)
    res = run_bass_kernel_spmd(nc, maps, core_ids=list(range(8)))
    R = res.results
    y_sample = np.stack([np.ascontiguousarray(R[b]["yT"].T) for b in range(4)], axis=0).astype(np.float32)
    yp = []
    mC, mN, mM, gS, hS = [], [], [], [], []
    for j in range(4):
        r = R[4 + j]
        yp.append(np.ascontiguousarray(r["yT"].T)[:2048].reshape(8, 256, D))
        mC.append(np.transpose(r["o_mC"][:, :8], (1, 0, 2, 3, 4, 5)))
        mN.append(np.transpose(r["o_mN"][:, :8], (1, 0, 2, 3, 4)))
        mM.append(np.transpose(r["o_mM"][:, :8], (1, 0, 2, 3)))
        gS.append(np.transpose(r["o_gS"][:, :8], (1, 0, 2, 3, 4, 5)))
        hS.append(np.transpose(r["o_hS"][:, :8], (1, 0, 2, 3, 4, 5)))
    cat = lambda lst: np.ascontiguousarray(np.concatenate(lst, axis=0)).astype(np.float32)
    return (cat(yp), y_sample, cat(mC), cat(mN), cat(mM), cat(gS), cat(hS))
```
